# Optimizing a Trainium2 kernel written in Bass

```python
import math
import jax, jax.numpy as jnp
from jax import lax
import numpy as np

D_MODEL = 2048
BATCH = 8
SEQ = 4096
DEPTH = 2

MLA_HEADS = 8
MLA_Q_RANK = 512
MLA_KV_RANK = 512
MLA_NOPE = 128
MLA_ROPE = 64
MLA_V = 128
ROPE_THETA = 10000.0
ML_HEADS = 4
ML_QK = 128
ML_V = 256
ML_CHUNK = 64
ML_F_BIAS_LO = 3.0
ML_F_BIAS_HI = 6.0
DA_HEADS = 8
DA_HEAD = 128
D_FF = 5632
Q_BLOCK = 128
EPS = 1e-6
NEG_INIT = -1e30

EVEN_SPLITS = (MLA_Q_RANK, MLA_KV_RANK, MLA_ROPE, ML_HEADS * ML_QK, ML_HEADS * ML_QK, ML_HEADS * ML_V, ML_HEADS * ML_V, 4 * ML_HEADS)
EVEN_IN = sum(EVEN_SPLITS)
EVEN_OUT = MLA_HEADS * MLA_V + ML_HEADS * ML_V
ODD_IN = 3 * DA_HEADS * 2 * DA_HEAD
ODD_OUT = DA_HEADS * 2 * DA_HEAD

kernel_name = "hybrid_mla_mlstm_diffattn_macaron_encoder"


def rmsnorm(x, g):
    xf = x.astype(jnp.float32)
    y = xf * lax.rsqrt(jnp.mean(xf * xf, axis=-1, keepdims=True) + EPS)
    return (y * g.astype(jnp.float32)).astype(x.dtype)


def swiglu(x, w_gu, w_down):
    gate, up = jnp.split(x @ w_gu, 2, axis=-1)
    return (jax.nn.silu(gate) * up) @ w_down


def rotate(x, cos, sin):
    x1, x2 = jnp.split(x, 2, axis=-1)
    return jnp.concatenate([x1 * cos - x2 * sin, x2 * cos + x1 * sin], axis=-1)


def split_cols(z, sizes):
    idx = np.cumsum(sizes)[:-1].tolist()
    return jnp.split(z, idx, axis=-1)


def alibi_slopes(n):
    return jnp.asarray([2.0 ** (-8.0 * (h + 1) / n) for h in range(n)], dtype=jnp.float32)


def to_query_blocks(t):
    b, h, s, d = t.shape
    return jnp.moveaxis(t.reshape(b, h, s // Q_BLOCK, Q_BLOCK, d), 2, 0)


def from_query_blocks(o):
    nb, b, h, q, d = o.shape
    return jnp.moveaxis(o, 0, 2).reshape(b, h, nb * q, d)


def mla_attention(q, k, v):
    q = q * (q.shape[-1] ** -0.5)

    def block(qi):
        s = jnp.einsum('bhqd,bhkd->bhqk', qi, k).astype(jnp.float32)
        p = jax.nn.softmax(s, axis=-1).astype(v.dtype)
        return jnp.einsum('bhqk,bhkd->bhqd', p, v)

    return from_query_blocks(lax.map(block, to_query_blocks(q)))


def mlstm_direction(q, k, v, log_i, log_f):
    b, h, s, _ = q.shape
    dk, dv = q.shape[-1], v.shape[-1]
    nc, L = s // ML_CHUNK, ML_CHUNK

    def to_chunks(t):
        return jnp.moveaxis(t.reshape(t.shape[:2] + (nc, L) + t.shape[3:]), 2, 0)

    lower = jnp.tril(jnp.ones((L, L), dtype=bool))

    def step(carry, inp):
        c_st, n_st, m_st = carry
        qc, kc, vc, li, lf = inp
        a = jnp.cumsum(lf, axis=-1)
        g = a[..., -1]
        dmat = jnp.where(lower, a[..., :, None] - a[..., None, :] + li[..., None, :], -jnp.inf)
        inter = a + m_st[..., None]
        m = jnp.maximum(inter, jnp.max(dmat, axis=-1))
        w_inter = jnp.exp(inter - m)
        qk = jnp.einsum('bhld,bhsd->bhls', qc, kc) * jnp.exp(dmat - m[..., None])
        num = w_inter[..., None] * jnp.einsum('bhld,bhde->bhle', qc, c_st) + jnp.einsum('bhls,bhse->bhle', qk, vc)
        den = w_inter * jnp.einsum('bhld,bhd->bhl', qc, n_st) + jnp.sum(qk, axis=-1)
        h_out = num / jnp.maximum(jnp.abs(den), jnp.exp(-m))[..., None]
        r = g[..., None] - a + li
        m_new = jnp.maximum(g + m_st, jnp.max(r, axis=-1))
        w_old = jnp.exp(g + m_st - m_new)
        w_r = jnp.exp(r - m_new[..., None])
        c_new = w_old[..., None, None] * c_st + jnp.einsum('bhs,bhsd,bhse->bhde', w_r, kc, vc)
        n_new = w_old[..., None] * n_st + jnp.einsum('bhs,bhsd->bhd', w_r, kc)
        return (c_new, n_new, m_new), h_out

    init = (jnp.zeros((b, h, dk, dv), jnp.float32), jnp.zeros((b, h, dk), jnp.float32),
            jnp.full((b, h), NEG_INIT, jnp.float32))
    _, hs = lax.scan(step, init, (to_chunks(q), to_chunks(k), to_chunks(v), to_chunks(log_i), to_chunks(log_f)))
    return jnp.moveaxis(hs, 0, 2).reshape(b, h, s, dv)


def mlstm_bidirectional(q, k, v, li_f, lf_f, li_b, lf_b):
    flip = lambda t: jnp.flip(t, axis=2)
    h_f = mlstm_direction(q, k, v, li_f, lf_f)
    h_b = flip(mlstm_direction(flip(q), flip(k), flip(v), flip(li_b), flip(lf_b)))
    return h_f + h_b


def even_mixer(h, cos, sin, w_in, g_cq, w_uq, g_ckv, w_ukv, b_gates, g_mlstm, w_o):
    b, s, _ = h.shape
    c_q, c_kv, k_r, m_q, m_k, m_v, m_o, m_g = split_cols(h @ w_in, EVEN_SPLITS)
    q = (rmsnorm(c_q, g_cq) @ w_uq).reshape(b, s, MLA_HEADS, MLA_NOPE + MLA_ROPE)
    kv = (rmsnorm(c_kv, g_ckv) @ w_ukv).reshape(b, s, MLA_HEADS, MLA_NOPE + MLA_V)
    q = jnp.concatenate([q[..., :MLA_NOPE], rotate(q[..., MLA_NOPE:], cos[:, :, None, :], sin[:, :, None, :])], axis=-1)
    k_rope = jnp.broadcast_to(rotate(k_r, cos, sin)[:, :, None, :], (b, s, MLA_HEADS, MLA_ROPE))
    k = jnp.concatenate([kv[..., :MLA_NOPE], k_rope], axis=-1)
    v = kv[..., MLA_NOPE:]
    a_out = mla_attention(q.transpose(0, 2, 1, 3), k.transpose(0, 2, 1, 3), v.transpose(0, 2, 1, 3))
    a_out = a_out.transpose(0, 2, 1, 3).reshape(b, s, MLA_HEADS * MLA_V)
    heads = lambda t, d: t.reshape(b, s, ML_HEADS, d).transpose(0, 2, 1, 3).astype(jnp.float32)
    mq = heads(m_q, ML_QK)
    mk = heads(m_k, ML_QK) * (ML_QK ** -0.5)
    mv = heads(m_v, ML_V)
    gates = (m_g + b_gates).astype(jnp.float32).reshape(b, s, 4, ML_HEADS).transpose(2, 0, 3, 1)
    hm = mlstm_bidirectional(mq, mk, mv, gates[0], jax.nn.log_sigmoid(gates[1]), gates[2], jax.nn.log_sigmoid(gates[3]))
    hm = hm * lax.rsqrt(jnp.mean(hm * hm, axis=-1, keepdims=True) + EPS) * g_mlstm.astype(jnp.float32).reshape(ML_HEADS, 1, ML_V)
    m_out = hm.transpose(0, 2, 1, 3).reshape(b, s, ML_HEADS * ML_V).astype(h.dtype) * jax.nn.sigmoid(m_o)
    return jnp.concatenate([a_out, m_out], axis=-1) @ w_o


def diff_attention(q1, q2, k1, k2, v, positions, lam):
    slopes = alibi_slopes(DA_HEADS)
    nb = positions.shape[1] // Q_BLOCK
    pb = jnp.moveaxis(positions.reshape(positions.shape[0], nb, Q_BLOCK), 1, 0)

    def block(args):
        q1i, q2i, pi = args
        dist = jnp.abs(pi[:, :, None] - positions[:, None, :]).astype(jnp.float32)
        bias = -slopes[None, :, None, None] * dist[:, None, :, :]
        s1 = jnp.einsum('bhqd,bhkd->bhqk', q1i, k1).astype(jnp.float32) + bias
        s2 = jnp.einsum('bhqd,bhkd->bhqk', q2i, k2).astype(jnp.float32) + bias
        a = jax.nn.softmax(s1, axis=-1) - lam * jax.nn.softmax(s2, axis=-1)
        return jnp.einsum('bhqk,bhkd->bhqd', a.astype(v.dtype), v)

    return from_query_blocks(lax.map(block, (to_query_blocks(q1), to_query_blocks(q2), pb)))


def odd_mixer(h, positions, w_in, lam_q1, lam_k1, lam_q2, lam_k2, g_sub, w_o, lam_init):
    b, s, _ = h.shape
    q, k, v = jnp.split(h @ w_in, 3, axis=-1)
    q = q.reshape(b, s, DA_HEADS, 2, DA_HEAD) * (DA_HEAD ** -0.5)
    k = k.reshape(b, s, DA_HEADS, 2, DA_HEAD)
    v = v.reshape(b, s, DA_HEADS, 2 * DA_HEAD).transpose(0, 2, 1, 3)
    q1, q2 = q[..., 0, :].transpose(0, 2, 1, 3), q[..., 1, :].transpose(0, 2, 1, 3)
    k1, k2 = k[..., 0, :].transpose(0, 2, 1, 3), k[..., 1, :].transpose(0, 2, 1, 3)
    f32 = jnp.float32
    lam = (jnp.exp(jnp.sum(lam_q1.astype(f32) * lam_k1.astype(f32)))
           - jnp.exp(jnp.sum(lam_q2.astype(f32) * lam_k2.astype(f32))) + lam_init)
    o = diff_attention(q1, q2, k1, k2, v, positions, lam)
    o = rmsnorm(o, g_sub) * (1.0 - lam_init)
    return o.transpose(0, 2, 1, 3).reshape(b, s, ODD_OUT) @ w_o


def setup_inputs(seed: int = 0) -> dict:
    key = jax.random.key(seed)
    ks = iter(jax.random.split(key, 64))
    f32 = jnp.float32

    def dense(fan_in, fan_out):
        return jax.random.normal(next(ks), (fan_in, fan_out), f32) * (fan_in ** -0.5)

    def gain(n):
        return 1.0 + 0.02 * jax.random.normal(next(ks), (n,), f32)

    def small(n, scale):
        return scale * jax.random.normal(next(ks), (n,), f32)

    x = jax.random.normal(next(ks), (BATCH, SEQ, D_MODEL), f32)
    offset = jax.random.randint(next(ks), (BATCH, 1), 0, SEQ, dtype=jnp.int32)
    positions = offset + jnp.arange(SEQ, dtype=jnp.int32)[None, :]
    f_bias = jnp.linspace(ML_F_BIAS_LO, ML_F_BIAS_HI, ML_HEADS, dtype=f32)
    b_gates = jnp.concatenate([small(ML_HEADS, 0.1), f_bias + small(ML_HEADS, 0.1),
                               small(ML_HEADS, 0.1), f_bias + small(ML_HEADS, 0.1)])
    return {
        "x": x,
        "positions": positions,
        "l0_ffn1_norm": gain(D_MODEL),
        "l0_ffn1_w_gu": dense(D_MODEL, 2 * D_FF),
        "l0_ffn1_w_down": dense(D_FF, D_MODEL),
        "l0_mix_norm": gain(D_MODEL),
        "l0_w_in": dense(D_MODEL, EVEN_IN),
        "l0_g_cq": gain(MLA_Q_RANK),
        "l0_w_uq": dense(MLA_Q_RANK, MLA_HEADS * (MLA_NOPE + MLA_ROPE)),
        "l0_g_ckv": gain(MLA_KV_RANK),
        "l0_w_ukv": dense(MLA_KV_RANK, MLA_HEADS * (MLA_NOPE + MLA_V)),
        "l0_b_gates": b_gates,
        "l0_g_mlstm": gain(ML_HEADS * ML_V),
        "l0_w_o": dense(EVEN_OUT, D_MODEL),
        "l0_ffn2_norm": gain(D_MODEL),
        "l0_ffn2_w_gu": dense(D_MODEL, 2 * D_FF),
        "l0_ffn2_w_down": dense(D_FF, D_MODEL),
        "l1_ffn1_norm": gain(D_MODEL),
        "l1_ffn1_w_gu": dense(D_MODEL, 2 * D_FF),
        "l1_ffn1_w_down": dense(D_FF, D_MODEL),
        "l1_mix_norm": gain(D_MODEL),
        "l1_w_in": dense(D_MODEL, ODD_IN),
        "l1_lam_q1": small(DA_HEAD, 0.1),
        "l1_lam_k1": small(DA_HEAD, 0.1),
        "l1_lam_q2": small(DA_HEAD, 0.1),
        "l1_lam_k2": small(DA_HEAD, 0.1),
        "l1_g_sub": gain(2 * DA_HEAD),
        "l1_w_o": dense(ODD_OUT, D_MODEL),
        "l1_ffn2_norm": gain(D_MODEL),
        "l1_ffn2_w_gu": dense(D_MODEL, 2 * D_FF),
        "l1_ffn2_w_down": dense(D_FF, D_MODEL),
        "final_norm": gain(D_MODEL),
    }


def reference(x, positions,
              l0_ffn1_norm, l0_ffn1_w_gu, l0_ffn1_w_down,
              l0_mix_norm, l0_w_in, l0_g_cq, l0_w_uq, l0_g_ckv, l0_w_ukv, l0_b_gates, l0_g_mlstm, l0_w_o,
              l0_ffn2_norm, l0_ffn2_w_gu, l0_ffn2_w_down,
              l1_ffn1_norm, l1_ffn1_w_gu, l1_ffn1_w_down,
              l1_mix_norm, l1_w_in, l1_lam_q1, l1_lam_k1, l1_lam_q2, l1_lam_k2, l1_g_sub, l1_w_o,
              l1_ffn2_norm, l1_ffn2_w_gu, l1_ffn2_w_down,
              final_norm):
    inv_freq = ROPE_THETA ** (-jnp.arange(0, MLA_ROPE, 2, dtype=jnp.float32) / MLA_ROPE)
    ang = positions.astype(jnp.float32)[..., None] * inv_freq
    cos, sin = jnp.cos(ang).astype(x.dtype), jnp.sin(ang).astype(x.dtype)

    ffn_params = [((l0_ffn1_norm, l0_ffn1_w_gu, l0_ffn1_w_down), (l0_ffn2_norm, l0_ffn2_w_gu, l0_ffn2_w_down)),
                  ((l1_ffn1_norm, l1_ffn1_w_gu, l1_ffn1_w_down), (l1_ffn2_norm, l1_ffn2_w_gu, l1_ffn2_w_down))]
    mix_norms = [l0_mix_norm, l1_mix_norm]
    even_params = [(l0_w_in, l0_g_cq, l0_w_uq, l0_g_ckv, l0_w_ukv, l0_b_gates, l0_g_mlstm, l0_w_o)]
    odd_params = [(l1_w_in, l1_lam_q1, l1_lam_k1, l1_lam_q2, l1_lam_k2, l1_g_sub, l1_w_o)]

    for layer in range(DEPTH):
        f1, f2 = ffn_params[layer]
        x = x + 0.5 * swiglu(rmsnorm(x, f1[0]), f1[1], f1[2])
        hn = rmsnorm(x, mix_norms[layer])
        if layer % 2 == 0:
            x = x + even_mixer(hn, cos, sin, *even_params[layer // 2])
        else:
            lam_init = 0.8 - 0.6 * math.exp(-0.3 * layer)
            x = x + odd_mixer(hn, positions, *odd_params[layer // 2], lam_init=lam_init)
        x = x + 0.5 * swiglu(rmsnorm(x, f2[0]), f2[1], f2[2])
    return rmsnorm(x, final_norm)
```

```python
import contextlib
import numpy as np
import concourse.bass as bass
import concourse.mybir as mybir
from concourse.bass_utils import run_bass_kernel_spmd

F32 = mybir.dt.float32
BF16 = mybir.dt.bfloat16
I32 = mybir.dt.int32
AF = mybir.ActivationFunctionType
ALU = mybir.AluOpType

D_MODEL = 2048
SEQ = 4096
D_FF = 5632
EPS = 1e-6
NCORES = 8
TT = 512
NT = SEQ // TT
KC = D_MODEL // 128
FC = D_FF // 128


class Buf:
    def __init__(self, name):
        self.name = name
        self.last_w = None
        self.readers = []
        self.sem = None
        self.sem_n = 0


class Op:
    __slots__ = ("eng", "fn", "reads", "writes", "dma", "deps", "needs_inc", "ev", "idx", "tiny")

    def __init__(self, eng, fn, reads, writes, dma, tiny=False):
        self.eng, self.fn, self.reads, self.writes, self.dma = eng, fn, reads, writes, dma
        self.tiny = tiny
        self.deps = []
        self.needs_inc = False
        self.ev = None


class Phase:
    ENGS = ("pe", "act", "dve", "pool", "sp")

    def __init__(self, nc, tag):
        self.nc = nc
        self.tag = tag
        self.ops = []
        self.eng = {"pe": nc.tensor, "act": nc.scalar, "dve": nc.vector, "pool": nc.gpsimd, "sp": nc.sync}
        self.snap = nc.snapshot_sems()
        self.es = contextlib.ExitStack()
        self.bufs = []

    def sbuf(self, name, shape, dt):
        return self.es.enter_context(self.nc.sbuf_tensor(f"{self.tag}_{name}", list(shape), dt))

    def psum(self, name, shape, dt=F32):
        return self.es.enter_context(self.nc.psum_tensor(f"{self.tag}_{name}", list(shape), dt))

    def buf(self, name):
        b = Buf(name)
        self.bufs.append(b)
        return b

    def op(self, eng, fn, reads=(), writes=(), tiny=False):
        o = Op(eng, fn, tuple(reads), tuple(writes), False, tiny)
        self.ops.append(o)
        return o

    def dma(self, queue, out, in_, sb):
        sb = tuple(sb) if isinstance(sb, (tuple, list)) else (sb,)
        o = Op(queue, lambda e: e.dma_start(out=out, in_=in_), (), sb, True)
        self.ops.append(o)
        return o

    def flush(self):
        nc = self.nc
        for o in self.ops:
            deps = []
            for b in o.reads:
                if b.last_w is not None:
                    deps.append(b.last_w)
            for b in o.writes:
                if b.last_w is not None:
                    deps.append(b.last_w)
                deps.extend(b.readers)
            seen = set()
            for p in deps:
                if p is o or id(p) in seen:
                    continue
                seen.add(id(p))
                if (not p.dma) and (not o.dma) and p.eng == o.eng and not (p.tiny or o.tiny):
                    continue
                if (not p.dma) and o.dma and p.eng == o.eng:
                    continue
                o.deps.append(p)
                p.needs_inc = True
            for b in o.reads:
                if not o.dma:
                    b.readers = [r for r in b.readers if r.dma or r.eng != o.eng]
                b.readers.append(o)
            for b in o.writes:
                b.last_w = o
                b.readers = []
        last = {}
        for o in self.ops:
            if not o.dma:
                last[o.eng] = o
        for o in last.values():
            o.needs_inc = True
        esem = {}
        ecnt = {}
        for e in self.ENGS:
            esem[e] = nc.alloc_semaphore(f"{self.tag}_t_{e}")
            ecnt[e] = 0
        waited = {e: {} for e in self.ENGS}
        all_ev = {}

        def do_wait(e, ev):
            h, v, key = ev
            if waited[e].get(key, 0) >= v:
                return
            waited[e][key] = v
            self.eng[e].wait_ge(h, v)

        for o in self.ops:
            e = o.eng
            for p in o.deps:
                do_wait(e, p.ev)
            with nc.allow_non_contiguous_dma(reason="small strided side loads"):
                ins = o.fn(self.eng[e])
            if o.dma:
                b = o.writes[0]
                if b.sem is None:
                    b.sem = nc.alloc_semaphore(f"{self.tag}_d_{b.name}")
                b.sem_n += 16
                assert b.sem_n < 60000, b.name
                ins.then_inc(b.sem, 16)
                o.ev = (b.sem, b.sem_n, "d_" + b.name)
                all_ev[o.ev[2]] = o.ev
            elif o.needs_inc:
                ecnt[e] += 1
                assert ecnt[e] < 60000, (self.tag, e)
                ins.then_inc(esem[e], 1)
                o.ev = (esem[e], ecnt[e], "t_" + e)
                all_ev[o.ev[2]] = o.ev
        for ev in all_ev.values():
            do_wait("sp", ev)
        nc.all_engine_barrier()
        nc.clear_and_free_semaphores(nc.allocated_since(self.snap))
        nc.all_engine_barrier()
        self.es.close()
        self.ops = []


def mm_group(out, pairs):
    def fn(pe):
        n = len(pairs)
        for i, (l, r) in enumerate(pairs):
            ins = pe.matmul(out, l, r, start=(i == 0), stop=(i == n - 1))
        return ins
    return fn


def emit_ffn(nc, R, wgu, wdn, gn, tag, tiles=NT):
    ph = Phase(nc, tag)
    NGU, NDN = 3, 2
    xb = [ph.sbuf(f"x{i}", [128, KC, TT], F32) for i in range(2)]
    xsq = ph.sbuf("xsq", [128, KC, TT], BF16)
    xn = ph.sbuf("xn", [128, KC, TT], BF16)
    hb = ph.sbuf("h", [128, FC, TT], BF16)
    wg = [ph.sbuf(f"wg{i}", [128, 2, KC * 128], BF16) for i in range(NGU)]
    wd = [ph.sbuf(f"wd{i}", [128, FC * 128], BF16) for i in range(NDN)]
    stmp = [ph.sbuf(f"st{i}", [128, TT], F32) for i in range(2)]
    rstd = ph.sbuf("rstd", [128, TT], F32)
    ones = ph.sbuf("ones", [128, 128], BF16)
    gsb = ph.sbuf("g", [128, KC], F32)
    psum = ph.psum("ps", [128, 8, TT], F32)
    Bx = [ph.buf(f"x{i}") for i in range(2)]
    Bxsq, Bxn, Bh, Brstd, Bones, Bg = (ph.buf(n) for n in ("xsq", "xn", "h", "rstd", "ones", "g"))
    Bwg = [ph.buf(f"wg{i}") for i in range(NGU)]
    Bwd = [ph.buf(f"wd{i}") for i in range(NDN)]
    Bst = [ph.buf(f"st{i}") for i in range(2)]
    Bps = [ph.buf(f"ps{i}") for i in range(8)]
    Rv = R.rearrange("(kc p) s -> p kc s", p=128)

    ph.dma("sp", gsb[:], gn, Bg)
    ph.op("dve", lambda e: e.memset(ones[:], 1.0), writes=[Bones])
    ph.dma("sp", xb[0][:], Rv[:, :, 0:TT], Bx[0])
    gi = 0
    di = 0
    for t in range(tiles):
        x, BX = xb[t % 2], Bx[t % 2]
        if t + 1 < tiles:
            ph.dma("sp", xb[(t + 1) % 2][:], Rv[:, :, (t + 1) * TT:(t + 2) * TT], Bx[(t + 1) % 2])
        ph.op("act", lambda e, x=x: e.activation(out=xsq[:], in_=x[:], func=AF.Square), [BX], [Bxsq])
        ph.op("pe", mm_group(psum[:, 0, :], [(ones[:], xsq[:, kc, :]) for kc in range(KC)]), [Bones, Bxsq], [Bps[0]])
        ph.op("act", lambda e: e.activation(out=rstd[:], in_=psum[:, 0, :], func=AF.Sqrt, bias=EPS,
                                            scale=1.0 / D_MODEL), [Bps[0]], [Brstd])
        ph.op("dve", lambda e: e.reciprocal(out=rstd[:], in_=rstd[:]), [Brstd], [Brstd])

        def xn_fn(e, x=x):
            for kc in range(KC):
                ins = e.scalar_tensor_tensor(out=xn[:, kc, :], in0=x[:, kc, :], scalar=gsb[:, kc:kc + 1],
                                             in1=rstd[:], op0=ALU.mult, op1=ALU.mult)
            return ins
        ph.op("dve", xn_fn, [BX, Bg, Brstd], [Bxn])
        for j in range(FC):
            s = gi % NGU
            ph.dma("pool", wg[s][:, 0, :], wgu[j], Bwg[s])
            ph.dma("pool", wg[s][:, 1, :], wgu[FC + j], Bwg[s])
            bk = 2 * (gi % 3)
            for half in range(2):
                ph.op("pe", mm_group(psum[:, bk + half, :],
                                     [(wg[s][:, half, kc * 128:(kc + 1) * 128], xn[:, kc, :]) for kc in range(KC)]),
                      [Bwg[s], Bxn], [Bps[bk + half]])
            st, BST = stmp[gi % 2], Bst[gi % 2]
            ph.op("act", lambda e, st=st, bk=bk: e.activation(out=st[:], in_=psum[:, bk, :], func=AF.Silu),
                  [Bps[bk]], [BST])
            ph.op("dve", lambda e, st=st, bk=bk, j=j: e.tensor_tensor(out=hb[:, j, :], in0=st[:],
                                                                       in1=psum[:, bk + 1, :], op=ALU.mult),
                  [BST, Bps[bk + 1]], [Bh])
            gi += 1
        for m in range(KC):
            s = di % NDN
            ph.dma("pool", wd[s][:], wdn[m], Bwd[s])
            bk = 6 + di % 2
            ph.op("pe", mm_group(psum[:, bk, :], [(wd[s][:, kc * 128:(kc + 1) * 128], hb[:, kc, :]) for kc in range(FC)]),
                  [Bwd[s], Bh], [Bps[bk]])
            ph.op("dve", lambda e, x=x, bk=bk, m=m: e.scalar_tensor_tensor(
                out=x[:, m, :], in0=psum[:, bk, :], scalar=0.5, in1=x[:, m, :], op0=ALU.mult, op1=ALU.add),
                [Bps[bk]], [BX])
            di += 1
        ph.dma("sp", Rv[:, :, t * TT:(t + 1) * TT], x[:], BX)
    ph.flush()


class WS:
    def __init__(self, ph, name, nslots, width, queue="pool"):
        self.t = [ph.sbuf(f"{name}{i}", [128, width], BF16) for i in range(nslots)]
        self.b = [ph.buf(f"{name}{i}") for i in range(nslots)]
        self.i = 0
        self.ph = ph
        self.q = queue

    def load(self, src):
        k = self.i % len(self.t)
        self.i += 1
        self.ph.dma(self.q, self.t[k][:], src, self.b[k])
        return self.t[k], self.b[k]


class Rot:
    def __init__(self, items):
        self.items = items
        self.i = 0

    def next(self):
        it = self.items[self.i % len(self.items)]
        self.i += 1
        return it


def make_banks(ph, psum, ids):
    return Rot([(psum[:, k, :], ph.buf(f"bank{k}")) for k in ids])


def make_stage(ph, name, n, shape, dt):
    return Rot([(ph.sbuf(f"{name}{i}", shape, dt), ph.buf(f"{name}{i}")) for i in range(n)])


def rms_norm(ph, x, Bx, koff, nk, g, Bg, out, Bout, ooff, sq, Bsq, ones, Bones, bank, Bbank, rstd, Brstd, post=1.0):
    D = nk * 128
    ph.op("act", lambda e: e.activation(out=sq[:, 0:nk, :], in_=x[:, koff:koff + nk, :], func=AF.Square), [Bx], [Bsq])
    ph.op("pe", mm_group(bank, [(ones[:], sq[:, kc, :]) for kc in range(nk)]), [Bones, Bsq], [Bbank])
    ph.op("act", lambda e: e.activation(out=rstd[:], in_=bank, func=AF.Sqrt, bias=EPS * post ** -2,
                                        scale=1.0 / (D * post * post)), [Bbank], [Brstd])
    ph.op("dve", lambda e: e.reciprocal(out=rstd[:], in_=rstd[:]), [Brstd], [Brstd])

    def fn(e):
        for kc in range(nk):
            ins = e.scalar_tensor_tensor(out=out[:, ooff + kc, :], in0=x[:, koff + kc, :], scalar=g[:, kc:kc + 1],
                                         in1=rstd[:], op0=ALU.mult, op1=ALU.mult)
        return ins
    ph.op("dve", fn, [Bx, Bg, Brstd], [Bout])


def emit_tables(nc, POS, INVF, COS, SIN):
    ph = Phase(nc, "tb")
    posi = ph.sbuf("posi", [64, SEQ], I32)
    ang = ph.sbuf("ang", [64, SEQ], F32)
    r1 = ph.sbuf("r1", [64, SEQ], F32)
    r2 = ph.sbuf("r2", [64, SEQ], F32)
    invf = ph.sbuf("invf", [64, 1], F32)
    Bp, Ba, B1, B2, Bi = (ph.buf(n) for n in ("posi", "ang", "r1", "r2", "invf"))
    TWO_PI = 2.0 * np.pi
    SH = 1.0 - 1e-6
    C1 = 6.28125
    C2 = float(np.float32(TWO_PI - C1))
    ki = ph.sbuf("ki", [64, SEQ], I32)
    Bk = ph.buf("ki")
    ph.dma("sp", posi[:], POS.partition_broadcast(64), Bp)
    ph.dma("sp", invf[:], INVF, Bi)
    ph.op("dve", lambda e: e.tensor_copy(out=ang[:], in_=posi[:]), [Bp], [Ba])
    ph.op("dve", lambda e: e.tensor_scalar(out=ang[:], in0=ang[:], scalar1=invf[:, 0:1], scalar2=None, op0=ALU.mult),
          [Ba, Bi], [Ba])
    ph.op("dve", lambda e: e.tensor_scalar(out=r1[:], in0=ang[:], scalar1=1.0 / TWO_PI, scalar2=None, op0=ALU.mult),
          [Ba], [B1])
    ph.op("dve", lambda e: e.tensor_copy(out=ki[:], in_=r1[:]), [B1], [Bk])
    ph.op("dve", lambda e: e.tensor_copy(out=r1[:], in_=ki[:]), [Bk], [B1])
    ph.op("dve", lambda e: e.scalar_tensor_tensor(out=ang[:], in0=r1[:], scalar=-C1, in1=ang[:], op0=ALU.mult,
                                                  op1=ALU.add), [B1, Ba], [Ba])
    ph.op("dve", lambda e: e.scalar_tensor_tensor(out=ang[:], in0=r1[:], scalar=-C2, in1=ang[:], op0=ALU.mult,
                                                  op1=ALU.add), [B1, Ba], [Ba])

    def fold(t, Bt):
        ph.op("dve", lambda e: e.tensor_scalar(out=r1[:], in0=t[:], scalar1=np.pi, scalar2=TWO_PI, op0=ALU.is_gt,
                                               op1=ALU.mult), [Bt], [B1])
        ph.op("dve", lambda e: e.tensor_tensor(out=t[:], in0=t[:], in1=r1[:], op=ALU.subtract), [Bt, B1], [Bt])
        ph.op("dve", lambda e: e.tensor_scalar(out=r1[:], in0=t[:], scalar1=-np.pi, scalar2=TWO_PI, op0=ALU.is_lt,
                                               op1=ALU.mult), [Bt], [B1])
        ph.op("dve", lambda e: e.tensor_tensor(out=t[:], in0=t[:], in1=r1[:], op=ALU.add), [Bt, B1], [Bt])
    fold(ang, Ba)
    ph.op("dve", lambda e: e.tensor_scalar(out=r2[:], in0=ang[:], scalar1=np.pi / 2, scalar2=None, op0=ALU.add),
          [Ba], [B2])
    fold(r2, B2)
    sn = ph.sbuf("sn", [64, SEQ], F32)
    Bs = ph.buf("sn")
    ph.op("act", lambda e: e.activation(out=sn[0:32, :], in_=ang[0:32, :], func=AF.Sin, scale=-SH), [Ba], [Bs])
    ph.op("act", lambda e: e.activation(out=sn[32:64, :], in_=ang[32:64, :], func=AF.Sin, scale=SH), [Ba], [Bs])
    ph.dma("sp", SIN, sn[:], Bs)
    ph.op("act", lambda e: e.activation(out=r1[:], in_=r2[:], func=AF.Sin, scale=SH), [B2, B1], [B1])
    ph.dma("sp", COS, r1[:], B1)
    ph.flush()


def run_pipeline(steps, LA):
    n = len(steps)
    for i in range(n + LA):
        if i < n:
            steps[i][0]()
        if i - LA >= 0:
            steps[i - LA][1]()


def rope_combine(ph, bA, BA, bB, BB, cos_t, sin_t, Btab, f32st, bfst, scale, dst):
    s1, B1 = f32st.next()
    s2, B2 = f32st.next()
    o, Bo = bfst.next()
    ph.op("dve", lambda e: e.tensor_tensor(out=s1[0:64, :], in0=bA, in1=cos_t, op=ALU.mult), [BA, Btab], [B1])
    ph.op("dve", lambda e: e.tensor_tensor(out=s2[0:64, :], in0=bB, in1=sin_t, op=ALU.mult), [BB, Btab], [B2])
    ph.op("dve", lambda e: e.tensor_tensor(out=s1[0:64, :], in0=s1[0:64, :], in1=s2[0:64, :], op=ALU.add), [B1, B2], [B1])
    ph.op("act", lambda e: e.activation(out=o[0:64, :], in_=s1[0:64, :], func=AF.Copy, scale=scale), [B1], [Bo])
    ph.dma("sp", dst, o[0:64, :], Bo)


def emit_inproj0(nc, R, GMIX, W0, GCQ, GCKV, WUQ, WUKV, BG, COS, SIN, QN, QR, KN, KR, VT, MQ, MK, MVT, MO, G, tiles=NT):
    ph = Phase(nc, "ip0")
    x = ph.sbuf("x", [128, KC, TT], F32)
    sq = ph.sbuf("sq", [128, KC, TT], BF16)
    hn = ph.sbuf("hn", [128, KC, TT], BF16)
    cl = ph.sbuf("cl", [128, 8, TT], F32)
    cn = ph.sbuf("cn", [128, 8, TT], BF16)
    wuq = ph.sbuf("wuq", [128, 16, 512], BF16)
    wukv = ph.sbuf("wukv", [128, 16, 512], BF16)
    cos = ph.sbuf("cos", [64, SEQ], F32)
    sin = ph.sbuf("sin", [64, SEQ], F32)
    rstd = ph.sbuf("rstd", [128, TT], F32)
    ones = ph.sbuf("ones", [128, 128], BF16)
    gmix = ph.sbuf("gmix", [128, KC], F32)
    gcq = ph.sbuf("gcq", [128, 4], F32)
    gckv = ph.sbuf("gckv", [128, 4], F32)
    bg = ph.sbuf("bg", [16, 1], F32)
    psum = ph.psum("ps", [128, 8, TT], F32)
    Bx, Bsq, Bhn, Bcl, Bcn, Bwuq, Bwukv, Btab, Brstd, Bones, Bc = (ph.buf(n) for n in (
        "x", "sq", "hn", "cl", "cn", "wuq", "wukv", "tab", "rstd", "ones", "consts"))
    ws = WS(ph, "w", 6, KC * 128)
    banks = make_banks(ph, psum, list(range(8)))
    f32st = make_stage(ph, "fs", 4, [128, TT], F32)
    bfst = make_stage(ph, "bs", 6, [128, TT], BF16)
    Rv = R.rearrange("(kc p) s -> p kc s", p=128)

    ph.dma("sp", gmix[:], GMIX, Bc)
    ph.dma("sp", gcq[:], GCQ, Bc)
    ph.dma("sp", gckv[:], GCKV, Bc)
    ph.dma("sp", bg[:], BG, Bc)
    ph.dma("sp", cos[:], COS, Btab)
    ph.dma("sp", sin[:], SIN, Btab)
    for i in range(16):
        ph.dma("pool", wuq[:, i, :], WUQ[i], Bwuq)
        ph.dma("pool", wukv[:, i, :], WUKV[i], Bwukv)
    ph.op("dve", lambda e: e.memset(ones[:], 1.0), writes=[Bones])
    ph.dma("sp", x[:], Rv[:, :, 0:TT], Bx)

    def fm(w, Bw, c0, c1, a, Ba, aoff, nk, wstride=128):
        bank, Bb = banks.next()
        M = c1 - c0
        ph.op("pe", mm_group(bank[0:M, :], [(w[:, kc * wstride + c0:kc * wstride + c1], a[:, aoff + kc, :])
                                            for kc in range(nk)]), [Bw, Ba], [Bb])
        return bank, Bb

    def evac_store(bank, Bb, M, func, scale, dst, bias=None, f32=False):
        st, Bs = (f32st if f32 else bfst).next()
        if bias is None:
            ph.op("act", lambda e: e.activation(out=st[0:M, :], in_=bank[0:M, :], func=func, scale=scale), [Bb], [Bs])
        else:
            ph.op("act", lambda e: e.activation(out=st[0:M, :], in_=bank[0:M, :], func=func, scale=scale, bias=bias),
                  [Bb, Bc], [Bs])
        ph.dma("sp", dst, st[0:M, :], Bs)

    def tm_group(wlist, a, Ba, aoff, nk, tb, wstride, coff):
        bank, Bb = banks.next()

        def fn(pe):
            for c, (w, _) in enumerate(wlist):
                for kc in range(nk):
                    ins = pe.matmul(bank[:, c * 128:(c + 1) * 128], a[:, aoff + kc, tb * 128:(tb + 1) * 128],
                                    w[:, kc * wstride + coff:kc * wstride + coff + 128], start=(kc == 0), stop=(kc == nk - 1))
            return ins
        ph.op("pe", fn, [Ba] + [b_ for (_, b_) in wlist], [Bb])
        return bank, Bb

    for t in range(tiles):
        ts = slice(t * TT, (t + 1) * TT)
        bank, Bb = banks.next()
        rms_norm(ph, x, Bx, 0, KC, gmix, Bc, hn, Bhn, 0, sq, Bsq, ones, Bones, bank, Bb, rstd, Brstd)
        if t + 1 < tiles:
            ph.dma("sp", x[:], Rv[:, :, (t + 1) * TT:(t + 2) * TT], Bx)
        for i in range(8):
            w, Bw = ws.load(W0[i])
            bank, Bb = fm(w, Bw, 0, 128, hn, Bhn, 0, KC)
            ph.op("act", lambda e, i=i, bank=bank: e.activation(out=cl[:, i, :], in_=bank, func=AF.Copy), [Bb], [Bcl])
        w, Bw = ws.load(W0[8])
        bA, BA = fm(w, Bw, 0, 64, hn, Bhn, 0, KC)
        bB, BB = fm(w, Bw, 64, 128, hn, Bhn, 0, KC)
        rope_combine(ph, bA[0:64, :], BA, bB[0:64, :], BB, cos[:, ts], sin[:, ts], Btab, f32st, bfst, 1.0, KR[:, ts])
        for h in range(4):
            w, Bw = ws.load(W0[9 + h])
            bank, Bb = fm(w, Bw, 0, 128, hn, Bhn, 0, KC)
            evac_store(bank, Bb, 128, AF.Copy, 1.0, MQ[h][:, ts])
        for h in range(4):
            w, Bw = ws.load(W0[13 + h])
            bank, Bb = fm(w, Bw, 0, 128, hn, Bhn, 0, KC)
            evac_store(bank, Bb, 128, AF.Copy, 128.0 ** -0.5, MK[h][:, ts])
        for cg in range(2):
            wl = [ws.load(W0[17 + cg * 4 + c]) for c in range(4)]
            for tb in range(4):
                bank, Bb = tm_group(wl, hn, Bhn, 0, KC, tb, 128, 0)
                evac_store(bank, Bb, 128, AF.Copy, 1.0, MVT[t * TT + tb * 128:t * TT + (tb + 1) * 128, cg * 512:(cg + 1) * 512])
        for i in range(8):
            w, Bw = ws.load(W0[25 + i])
            bank, Bb = fm(w, Bw, 0, 128, hn, Bhn, 0, KC)
            evac_store(bank, Bb, 128, AF.Sigmoid, 1.0, MO[i * 128:(i + 1) * 128, ts])
        w, Bw = ws.load(W0[33])
        bank, Bb = fm(w, Bw, 0, 16, hn, Bhn, 0, KC)
        evac_store(bank, Bb, 16, AF.Identity, 1.0, G[:, ts], bias=bg[:, 0:1], f32=True)
        bank, Bb = banks.next()
        rms_norm(ph, cl, Bcl, 0, 4, gcq, Bc, cn, Bcn, 0, sq, Bsq, ones, Bones, bank, Bb, rstd, Brstd)
        bank, Bb = banks.next()
        rms_norm(ph, cl, Bcl, 4, 4, gckv, Bc, cn, Bcn, 4, sq, Bsq, ones, Bones, bank, Bb, rstd, Brstd)
        qs = 192.0 ** -0.5
        for h in range(8):
            wq = wuq[:, 2 * h, :]
            bank, Bb = fm(wq, Bwuq, 0, 128, cn, Bcn, 0, 4)
            evac_store(bank, Bb, 128, AF.Copy, qs, QN[h][:, ts])
            wr = wuq[:, 2 * h + 1, :]
            bA, BA = fm(wr, Bwuq, 0, 64, cn, Bcn, 0, 4)
            bB, BB = fm(wr, Bwuq, 64, 128, cn, Bcn, 0, 4)
            rope_combine(ph, bA[0:64, :], BA, bB[0:64, :], BB, cos[:, ts], sin[:, ts], Btab, f32st, bfst, qs, QR[h][:, ts])
        for h in range(8):
            wk = wukv[:, 2 * h, :]
            bank, Bb = fm(wk, Bwukv, 0, 128, cn, Bcn, 4, 4)
            evac_store(bank, Bb, 128, AF.Copy, 1.0, KN[h][:, ts])
        for hg in range(2):
            wl = [(wukv[:, 2 * (hg * 4 + c) + 1, :], Bwukv) for c in range(4)]
            for tb in range(4):
                bank, Bb = tm_group(wl, cn, Bcn, 4, 4, tb, 128, 0)
                evac_store(bank, Bb, 128, AF.Copy, 1.0, VT[t * TT + tb * 128:t * TT + (tb + 1) * 128, hg * 512:(hg + 1) * 512])
    ph.flush()


def emit_mla(nc, QN, QR, KN, KR, VT, MIX, heads=8, qtiles=NT):
    ph = Phase(nc, "mla")
    NB = 2
    qn = [ph.sbuf(f"qn{i}", [128, SEQ], BF16) for i in range(NB)]
    qr = [ph.sbuf(f"qr{i}", [128, SEQ], BF16) for i in range(NB)]
    kn = [ph.sbuf(f"kn{i}", [128, SEQ], BF16) for i in range(NB)]
    vv = [ph.sbuf(f"v{i}", [128, 32, 128], BF16) for i in range(NB)]
    kr = ph.sbuf("kr", [128, SEQ], BF16)
    ones = ph.sbuf("ones", [128, 128], BF16)
    rden = ph.sbuf("rden", [128, TT], F32)
    psum = ph.psum("ps", [128, 8, TT], F32)
    Bqn, Bqr, Bkn, Bv = ([ph.buf(f"{n}{i}") for i in range(NB)] for n in ("qn", "qr", "kn", "v"))
    Bkr, Bones, Brden = ph.buf("kr"), ph.buf("ones"), ph.buf("rden")
    sbanks = make_banks(ph, psum, [0, 1, 2, 3])
    obanks = Rot([((psum[:, 4 + 2 * i, :], ph.buf(f"ob{i}")), (psum[:, 5 + 2 * i, :], ph.buf(f"db{i}"))) for i in range(2)])
    pst = make_stage(ph, "pt", 4, [128, TT], BF16)
    ost = make_stage(ph, "os", 2, [128, TT], BF16)
    VTv = VT.rearrange("(st p) c -> p st c", p=128)
    ph.op("pool", lambda e: e.memset(kr[64:128, :], 0.0), writes=[Bkr])
    for i in range(NB):
        ph.op("pool", lambda e, i=i: e.memset(qr[i][64:128, :], 0.0), writes=[Bqr[i]])
    ph.dma("sp", kr[0:64, :], KR, Bkr)
    ph.op("dve", lambda e: e.memset(ones[:], 1.0), writes=[Bones])

    def load(h):
        i = h % NB
        ph.dma("sp", qn[i][:], QN[h], Bqn[i])
        ph.dma("sp", qr[i][0:64, :], QR[h], Bqr[i])
        ph.dma("sp", kn[i][:], KN[h], Bkn[i])
        for a in range(4):
            ph.dma("sp", vv[i][:, a * 8:(a + 1) * 8, :], VTv[:, a * 8:(a + 1) * 8, h * 128:(h + 1) * 128], Bv[i])
    load(0)
    if heads > 1:
        load(1)
    steps = []
    for h in range(heads):
        for jq in range(qtiles):
            grp = {}
            for jk in range(32):
                st = {}

                def A(h=h, jq=jq, jk=jk, st=st):
                    i = h % NB
                    qs = slice(jq * TT, (jq + 1) * TT)
                    ks = slice(jk * 128, (jk + 1) * 128)
                    sb, Bsb = sbanks.next()
                    ph.op("pe", mm_group(sb, [(kn[i][:, ks], qn[i][:, qs]), (kr[:, ks], qr[i][:, qs])]),
                          [Bkn[i], Bqn[i], Bkr, Bqr[i]], [Bsb])
                    pt, Bpt = pst.next()
                    ph.op("act", lambda e, pt=pt, sb=sb: e.activation(out=pt[:], in_=sb, func=AF.Exp), [Bsb], [Bpt])
                    st["pt"] = (pt, Bpt)

                def Bf(h=h, jq=jq, jk=jk, st=st, grp=grp):
                    i = h % NB
                    qs = slice(jq * TT, (jq + 1) * TT)
                    if jk == 0:
                        grp["acc"] = obanks.next()
                    (ob, Bob), (db, Bdb) = grp["acc"]
                    pt, Bpt = st["pt"]

                    def fn(pe):
                        pe.matmul(ob, vv[i][:, jk, :], pt[:], start=(jk == 0), stop=(jk == 31))
                        return pe.matmul(db, ones[:], pt[:], start=(jk == 0), stop=(jk == 31))
                    ph.op("pe", fn, [Bv[i], Bpt, Bones], [Bob, Bdb])
                    if jk == 31:
                        ph.op("dve", lambda e: e.reciprocal(out=rden[:], in_=db), [Bdb], [Brden])
                        o, Bo = ost.next()
                        ph.op("dve", lambda e: e.tensor_tensor(out=o[:], in0=ob, in1=rden[:], op=ALU.mult), [Bob, Brden], [Bo])
                        ph.dma("sp", MIX[h * 128:(h + 1) * 128, qs], o[:], Bo)
                        if jq == qtiles - 1 and h + 2 < heads:
                            load(h + 2)
                steps.append((A, Bf))
    run_pipeline(steps, 2)
    ph.flush()


def emit_oproj(nc, R, A, WO, tag, tiles=NT):
    ph = Phase(nc, tag)
    xb = [ph.sbuf(f"x{i}", [128, KC, TT], F32) for i in range(2)]
    ab = [ph.sbuf(f"a{i}", [128, KC, TT], BF16) for i in range(2)]
    Bx = [ph.buf(f"x{i}") for i in range(2)]
    Ba = [ph.buf(f"a{i}") for i in range(2)]
    psum = ph.psum("ps", [128, 8, TT], F32)
    banks = make_banks(ph, psum, list(range(8)))
    ws = WS(ph, "w", 4, KC * 128)
    Rv = R.rearrange("(kc p) s -> p kc s", p=128)
    Av = A.rearrange("(kc p) s -> p kc s", p=128)

    def load(t):
        ph.dma("sp", xb[t % 2][:], Rv[:, :, t * TT:(t + 1) * TT], Bx[t % 2])
        ph.dma("sp", ab[t % 2][:], Av[:, :, t * TT:(t + 1) * TT], Ba[t % 2])
    load(0)
    for t in range(tiles):
        x, a, BX, BA = xb[t % 2], ab[t % 2], Bx[t % 2], Ba[t % 2]
        if t + 1 < tiles:
            load(t + 1)
        for m in range(KC):
            w, Bw = ws.load(WO[m])
            bank, Bb = banks.next()
            ph.op("pe", mm_group(bank, [(w[:, kc * 128:(kc + 1) * 128], a[:, kc, :]) for kc in range(KC)]), [Bw, BA], [Bb])
            ph.op("dve", lambda e, x=x, m=m, bank=bank: e.tensor_tensor(out=x[:, m, :], in0=bank, in1=x[:, m, :], op=ALU.add),
                  [Bb], [BX])
        ph.dma("sp", Rv[:, :, t * TT:(t + 1) * TT], x[:], BX)
    ph.flush()


def emit_final(nc, R, GF, OUT, tiles=NT):
    ph = Phase(nc, "fin")
    xb = [ph.sbuf(f"x{i}", [128, KC, TT], F32) for i in range(2)]
    ob = [ph.sbuf(f"o{i}", [128, KC, TT], F32) for i in range(2)]
    sq = ph.sbuf("sq", [128, KC, TT], BF16)
    rstd = ph.sbuf("rstd", [128, TT], F32)
    ones = ph.sbuf("ones", [128, 128], BF16)
    g = ph.sbuf("g", [128, KC], F32)
    psum = ph.psum("ps", [128, 8, TT], F32)
    Bx = [ph.buf(f"x{i}") for i in range(2)]
    Bo = [ph.buf(f"o{i}") for i in range(2)]
    Bsq, Brstd, Bones, Bg = (ph.buf(n) for n in ("sq", "rstd", "ones", "g"))
    banks = make_banks(ph, psum, [0, 1])
    Rv = R.rearrange("(kc p) s -> p kc s", p=128)
    Ov = OUT.rearrange("(kc p) s -> p kc s", p=128)
    ph.dma("sp", g[:], GF, Bg)
    ph.op("dve", lambda e: e.memset(ones[:], 1.0), writes=[Bones])
    ph.dma("sp", xb[0][:], Rv[:, :, 0:TT], Bx[0])
    for t in range(tiles):
        if t + 1 < tiles:
            ph.dma("sp", xb[(t + 1) % 2][:], Rv[:, :, (t + 1) * TT:(t + 2) * TT], Bx[(t + 1) % 2])
        bank, Bb = banks.next()
        rms_norm(ph, xb[t % 2], Bx[t % 2], 0, KC, g, Bg, ob[t % 2], Bo[t % 2], 0, sq, Bsq, ones, Bones, bank, Bb, rstd, Brstd)
        ph.dma("sp", Ov[:, :, t * TT:(t + 1) * TT], ob[t % 2][:], Bo[t % 2])
    ph.flush()


def emit_inproj1(nc, R, GMIX, W1, Q12, K12, VT, tiles=NT):
    ph = Phase(nc, "ip1")
    x = ph.sbuf("x", [128, KC, TT], F32)
    sq = ph.sbuf("sq", [128, KC, TT], BF16)
    hn = ph.sbuf("hn", [128, KC, TT], BF16)
    rstd = ph.sbuf("rstd", [128, TT], F32)
    ones = ph.sbuf("ones", [128, 128], BF16)
    gmix = ph.sbuf("gmix", [128, KC], F32)
    psum = ph.psum("ps", [128, 8, TT], F32)
    Bx, Bsq, Bhn, Brstd, Bones, Bc = (ph.buf(n) for n in ("x", "sq", "hn", "rstd", "ones", "consts"))
    ws = WS(ph, "w", 6, KC * 128)
    banks = make_banks(ph, psum, list(range(8)))
    bfst = make_stage(ph, "bs", 6, [128, TT], BF16)
    Rv = R.rearrange("(kc p) s -> p kc s", p=128)
    ph.dma("sp", gmix[:], GMIX, Bc)
    ph.op("dve", lambda e: e.memset(ones[:], 1.0), writes=[Bones])
    ph.dma("sp", x[:], Rv[:, :, 0:TT], Bx)
    for t in range(tiles):
        ts = slice(t * TT, (t + 1) * TT)
        bank, Bb = banks.next()
        rms_norm(ph, x, Bx, 0, KC, gmix, Bc, hn, Bhn, 0, sq, Bsq, ones, Bones, bank, Bb, rstd, Brstd)
        if t + 1 < tiles:
            ph.dma("sp", x[:], Rv[:, :, (t + 1) * TT:(t + 2) * TT], Bx)
        for i in range(32):
            w, Bw = ws.load(W1[i])
            bank, Bb = banks.next()
            ph.op("pe", mm_group(bank, [(w[:, kc * 128:(kc + 1) * 128], hn[:, kc, :]) for kc in range(KC)]), [Bw, Bhn], [Bb])
            st, Bs = bfst.next()
            sc = 128.0 ** -0.5 if i < 16 else 1.0
            ph.op("act", lambda e, st=st, bank=bank, sc=sc: e.activation(out=st[:], in_=bank, func=AF.Copy, scale=sc), [Bb], [Bs])
            ph.dma("sp", (Q12 if i < 16 else K12)[i % 16][:, ts], st[:], Bs)
        for cg in range(4):
            wl = [ws.load(W1[32 + cg * 4 + c]) for c in range(4)]
            for tb in range(4):
                bank, Bb = banks.next()

                def fn(pe, wl=wl, tb=tb, bank=bank):
                    for c, (w, _) in enumerate(wl):
                        for kc in range(KC):
                            ins = pe.matmul(bank[:, c * 128:(c + 1) * 128], hn[:, kc, tb * 128:(tb + 1) * 128],
                                            w[:, kc * 128:(kc + 1) * 128], start=(kc == 0), stop=(kc == KC - 1))
                    return ins
                ph.op("pe", fn, [Bhn] + [b_ for (_, b_) in wl], [Bb])
                st, Bs = bfst.next()
                ph.op("act", lambda e, st=st, bank=bank: e.activation(out=st[:], in_=bank, func=AF.Copy), [Bb], [Bs])
                ph.dma("sp", VT[t * TT + tb * 128:t * TT + (tb + 1) * 128, cg * 512:(cg + 1) * 512], st[:], Bs)
    ph.flush()


def emit_diff(nc, Q12, K12, VT, POS, POSC, LAM, GSUB, OD, heads=8, qtiles=NT, DBG=None):
    ph = Phase(nc, "da")
    NB = 2
    pq = ph.sbuf("pq", [128, SEQ], F32)
    pk = ph.sbuf("pk", [128, 32], F32)
    pki = ph.sbuf("pki", [128, 32], I32)
    dall = ph.sbuf("dall", [128, 32, TT], F32)
    qq = [ph.sbuf(f"q{i}", [128, 2, TT], BF16) for i in range(NB)]
    kk = [ph.sbuf(f"k{i}", [128, 2, SEQ], BF16) for i in range(NB)]
    vv = [ph.sbuf(f"v{i}", [128, 32, 256], BF16) for i in range(NB)]
    ones = ph.sbuf("ones", [128, 128], BF16)
    lamv = ph.sbuf("lamv", [128, 512], F32)
    lamt = ph.sbuf("lamt", [128, 256], F32)
    lam = ph.sbuf("lam", [128, 4], F32)
    gsub = ph.sbuf("gsub", [128, 2], F32)
    r1 = ph.sbuf("r1", [128, 2, TT], F32)
    ot = ph.sbuf("ot", [128, 2, TT], F32)
    tmp = ph.sbuf("tmp", [128, TT], F32)
    sq = ph.sbuf("sq", [128, 2, TT], BF16)
    acc = [ph.sbuf(f"acc{i}", [128, 2, TT], F32) for i in range(2)]
    ahi = ph.sbuf("ahi", [128, 2, TT], BF16)
    alo = ph.sbuf("alo", [128, 2, TT], BF16)
    psum = ph.psum("ps", [128, 8, TT], F32)
    Bpq, Bpk, Bdall, Bones, Blam, Bc, Br1, Bot, Btmp, Bsq, Bahi, Balo = (ph.buf(n) for n in (
        "pq", "pk", "dall", "ones", "lam", "consts", "r1", "ot", "tmp", "sq", "ahi", "alo"))
    Bacc = [ph.buf("acc0"), ph.buf("acc1")]
    Bq = [ph.buf(f"q{i}") for i in range(NB)]
    Bk = [ph.buf(f"k{i}") for i in range(NB)]
    Bv = [ph.buf(f"v{i}") for i in range(NB)]
    spairs = Rot([(psum[:, 0:2, :], ph.buf("sp0")), (psum[:, 2:4, :], ph.buf("sp1"))])
    N_ = [[psum[:, 4, :], psum[:, 5, :]], [psum[:, 6, :], psum[:, 7, :]]]
    Bn = [[ph.buf(f"n{m}{c}") for c in range(2)] for m in range(2)]
    tst = make_stage(ph, "ts", 3, [128, 2, TT], BF16)
    est = make_stage(ph, "es", 3, [128, TT], BF16)
    pst = make_stage(ph, "pt", 4, [128, 2, TT], BF16)
    ost = make_stage(ph, "os", 2, [128, TT], BF16)
    VTv = VT.rearrange("(st p) c -> p st c", p=128)

    pqi = dall[:, 0:8, :].rearrange("p a b -> p (a b)").bitcast(I32)
    ph.dma("sp", pqi, POS.partition_broadcast(128), Bdall)
    ph.dma("sp", pki[:], POSC, Bpk)
    ph.dma("sp", lamv[:], LAM.partition_broadcast(128), Blam)
    ph.dma("sp", gsub[:], GSUB, Bc)
    ph.op("dve", lambda e: e.tensor_copy(out=pq[:], in_=pqi), [Bdall], [Bpq])
    ph.op("dve", lambda e: e.tensor_copy(out=pk[:], in_=pki[:]), [Bpk], [Bpk], tiny=True)
    ph.op("dve", lambda e: e.tensor_scalar(out=pk[:], in0=pk[:], scalar1=-1.0, scalar2=None, op0=ALU.mult), [Bpk], [Bpk], tiny=True)
    ph.op("dve", lambda e: e.memset(ones[:], 1.0), writes=[Bones])
    T = dict(tiny=True)
    ph.op("dve", lambda e: e.memset(lam[:], 0.0), writes=[Blam], **T)
    ph.op("dve", lambda e: e.tensor_tensor(out=lamt[:, 0:128], in0=lamv[:, 0:128], in1=lamv[:, 128:256], op=ALU.mult), [Blam], [Blam], **T)
    ph.op("dve", lambda e: e.tensor_tensor(out=lamt[:, 128:256], in0=lamv[:, 256:384], in1=lamv[:, 384:512], op=ALU.mult), [Blam], [Blam], **T)
    ph.op("dve", lambda e: e.reduce_sum(out=lam[:, 0:1], in_=lamt[:, 0:128], axis=mybir.AxisListType.X), [Blam], [Blam], **T)
    ph.op("dve", lambda e: e.reduce_sum(out=lam[:, 1:2], in_=lamt[:, 128:256], axis=mybir.AxisListType.X), [Blam], [Blam], **T)
    ph.op("act", lambda e: e.activation(out=lam[:, 0:2], in_=lam[:, 0:2], func=AF.Exp), [Blam], [Blam], **T)
    ph.op("act", lambda e: e.activation(out=lam[:, 3:4], in_=lam[:, 0:1], func=AF.Identity, scale=-1.0, bias=-LAM_INIT1), [Blam], [Blam], **T)
    ph.op("act", lambda e: e.activation(out=lam[:, 2:3], in_=lam[:, 1:2], func=AF.Identity, scale=1.0, bias=lam[:, 3:4]), [Blam], [Blam], **T)
    if DBG is not None:
        ph.dma("sp", DBG[0], lam[:], Blam)

    def load(jq, h, n):
        i = n % NB
        qs = slice(jq * TT, (jq + 1) * TT)
        for w in range(2):
            ph.dma("sp", qq[i][:, w, :], Q12[h * 2 + w][:, qs], Bq[i])
            ph.dma("sp", kk[i][:, w, :], K12[h * 2 + w], Bk[i])
        for a_ in range(4):
            ph.dma("sp", vv[i][:, a_ * 8:(a_ + 1) * 8, :], VTv[:, a_ * 8:(a_ + 1) * 8, h * 256:(h + 1) * 256], Bv[i])

    order = [(jq, h) for jq in range(qtiles) for h in range(heads)]
    load(order[0][0], order[0][1], 0)
    if len(order) > 1:
        load(order[1][0], order[1][1], 1)
    post = 1.0 - LAM_INIT1
    steps = []
    for n, (jq, h) in enumerate(order):
        for jk in range(32):
            st = {}

            def A(n=n, jq=jq, h=h, jk=jk, st=st):
                i = n % NB
                qs = slice(jq * TT, (jq + 1) * TT)
                ks = slice(jk * 128, (jk + 1) * 128)
                if h == 0 and jk == 0:
                    def dfn(e):
                        for j2 in range(32):
                            e.tensor_scalar(out=dall[:, j2, :], in0=pq[:, qs], scalar1=pk[:, j2:j2 + 1], scalar2=None, op0=ALU.add)
                            ins = e.scalar_tensor_tensor(out=dall[:, j2, :], in0=dall[:, j2, :], scalar=-1.0, in1=dall[:, j2, :],
                                                         op0=ALU.mult, op1=ALU.max)
                        return ins
                    ph.op("dve", dfn, [Bpq, Bpk], [Bdall])
                    if DBG is not None and jq == 0:
                        ph.dma("sp", DBG[1], dall[:, 0:2, :], Bdall)
                nslope = -(2.0 ** (-8.0 * (h + 1) / 8))
                ee, Be = est.next()
                ph.op("act", lambda e: e.activation(out=ee[:], in_=dall[:, jk, :], func=AF.Exp, scale=nslope), [Bdall], [Be])
                sp, Bsp = spairs.next()

                def sfn(pe):
                    pe.matmul(sp[:, 0, :], kk[i][:, 0, ks], qq[i][:, 0, :], start=True, stop=True)
                    return pe.matmul(sp[:, 1, :], kk[i][:, 1, ks], qq[i][:, 1, :], start=True, stop=True)
                ph.op("pe", sfn, [Bk[i], Bq[i]], [Bsp])
                tt, Bt = tst.next()
                ph.op("act", lambda e: e.activation(out=tt[:], in_=sp, func=AF.Exp), [Bsp], [Bt])
                pt, Bp = pst.next()

                def pfn(e):
                    e.tensor_tensor(out=pt[:, 0, :], in0=tt[:, 0, :], in1=ee[:], op=ALU.mult)
                    return e.tensor_tensor(out=pt[:, 1, :], in0=tt[:, 1, :], in1=ee[:], op=ALU.mult)
                ph.op("dve", pfn, [Bt, Be], [Bp])
                ac, Bac = acc[n % 2], Bacc[n % 2]
                if jk == 0:
                    ph.op("pool", lambda e: e.tensor_copy(out=ac[:], in_=pt[:]), [Bp], [Bac])
                else:
                    ph.op("pool", lambda e: e.tensor_tensor(out=ac[:], in0=ac[:], in1=pt[:], op=ALU.add), [Bp, Bac], [Bac])
                st["pt"] = (pt, Bp)

            def Bf(n=n, jq=jq, h=h, jk=jk, st=st):
                i = n % NB
                qs = slice(jq * TT, (jq + 1) * TT)
                pt, Bp = st["pt"]

                def fn(pe):
                    for m in range(2):
                        for c in range(2):
                            ins = pe.matmul(N_[m][c], vv[i][:, jk, c * 128:(c + 1) * 128], pt[:, m, :], start=(jk == 0), stop=(jk == 31))
                    return ins
                ph.op("pe", fn, [Bv[i], Bp], [Bn[0][0], Bn[0][1], Bn[1][0], Bn[1][1]])
                if jk != 31:
                    return
                ac, Bac = acc[n % 2], Bacc[n % 2]
                ph.op("pool", lambda e: e.tensor_copy(out=ahi[:], in_=ac[:]), [Bac], [Bahi])
                ph.op("pool", lambda e: e.tensor_tensor(out=alo[:], in0=ac[:], in1=ahi[:], op=ALU.subtract), [Bac, Bahi], [Balo])
                dp, Bdp = spairs.next()

                def dfn2(pe):
                    for m in range(2):
                        pe.matmul(dp[:, m, :], ones[:], ahi[:, m, :], start=True, stop=False)
                        ins = pe.matmul(dp[:, m, :], ones[:], alo[:, m, :], start=False, stop=True)
                    return ins
                ph.op("pe", dfn2, [Bones, Bahi, Balo], [Bdp])
                ph.op("dve", lambda e: e.reciprocal(out=r1[:], in_=dp), [Bdp], [Br1])
                ph.op("dve", lambda e: e.tensor_scalar(out=r1[:, 1, :], in0=r1[:, 1, :], scalar1=lam[:, 2:3], scalar2=None, op0=ALU.mult),
                      [Br1, Blam], [Br1])
                for c in range(2):
                    ph.op("dve", lambda e, c=c: e.tensor_tensor(out=ot[:, c, :], in0=N_[0][c], in1=r1[:, 0, :], op=ALU.mult),
                          [Bn[0][c], Br1], [Bot])
                    ph.op("dve", lambda e, c=c: e.tensor_tensor(out=tmp[:], in0=N_[1][c], in1=r1[:, 1, :], op=ALU.mult),
                          [Bn[1][c], Br1], [Btmp])
                    ph.op("dve", lambda e, c=c: e.tensor_tensor(out=ot[:, c, :], in0=ot[:, c, :], in1=tmp[:], op=ALU.add),
                          [Bot, Btmp], [Bot])
                ph.op("act", lambda e: e.activation(out=sq[:], in_=ot[:], func=AF.Square), [Bot], [Bsq])
                np_, Bnp = spairs.next()
                ph.op("pe", mm_group(np_[:, 0, :], [(ones[:], sq[:, c, :]) for c in range(2)]), [Bones, Bsq], [Bnp])
                ph.op("act", lambda e: e.activation(out=r1[:, 0, :], in_=np_[:, 0, :], func=AF.Sqrt, bias=EPS * post ** -2,
                                                    scale=1.0 / (256.0 * post * post)), [Bnp, Br1], [Br1])
                ph.op("dve", lambda e: e.reciprocal(out=r1[:, 0, :], in_=r1[:, 0, :]), [Br1], [Br1])
                for c in range(2):
                    o, Bo = ost.next()
                    ph.op("dve", lambda e, o=o, c=c: e.scalar_tensor_tensor(out=o[:], in0=ot[:, c, :], scalar=gsub[:, c:c + 1],
                                                                           in1=r1[:, 0, :], op0=ALU.mult, op1=ALU.mult), [Bot, Bc, Br1], [Bo])
                    ph.dma("sp", OD[h * 256 + c * 128:h * 256 + (c + 1) * 128, qs], o[:], Bo)
                if n + 2 < len(order):
                    load(order[n + 2][0], order[n + 2][1], n + 2)
            steps.append((A, Bf))
    run_pipeline(steps, 1)
    ph.flush()


def emit_mlstm_prep(nc, G, UD, NEGM, EMD):
    ph = Phase(nc, "mlp")
    S = SEQ
    li = ph.sbuf("li", [4, S], F32)
    lf = ph.sbuf("lf", [4, S], F32)
    fc = ph.sbuf("fc", [4, S], F32)
    u = ph.sbuf("u", [4, S], F32)
    m0 = ph.sbuf("m0", [4, S], F32)
    m1 = ph.sbuf("m1", [4, S], F32)
    one = ph.sbuf("one", [4, S], F32)
    tot = ph.sbuf("tot", [4, 1], F32)
    Btot = ph.buf("tot")
    Bli, Blf, Bfc, Bu, Bm0, Bm1, Bone = (ph.buf(n) for n in ("li", "lf", "fc", "u", "m0", "m1", "one"))
    ph.op("dve", lambda e: e.memset(one[:], 1.0), writes=[Bone])
    for d in range(2):
        ph.dma("sp", li[:], G[8 * d:8 * d + 4, :], Bli)
        ph.dma("sp", lf[:], G[8 * d + 4:8 * d + 8, :], Blf)
        ph.op("act", lambda e, lf=lf: e.activation(out=lf[:], in_=lf[:], func=AF.Exp, scale=-1.0), [Blf], [Blf])
        ph.op("act", lambda e, lf=lf: e.activation(out=lf[:], in_=lf[:], func=AF.Ln, bias=1.0, scale=1.0), [Blf], [Blf])
        ph.op("dve", lambda e, lf=lf: e.tensor_scalar(out=lf[:], in0=lf[:], scalar1=-1.0, scalar2=None, op0=ALU.mult), [Blf], [Blf])
        ph.op("dve", lambda e, fc=fc, one=one, lf=lf: e.tensor_tensor_scan(out=fc[:], data0=one[:], data1=lf[:], initial=0.0,
                                                                          op0=ALU.mult, op1=ALU.add), [Bone, Blf], [Bfc])
        if d == 1:
            ph.op("dve", lambda e: e.tensor_copy(out=tot[:], in_=fc[:, S - 1:S]), [Bfc], [Btot], tiny=True)
            ph.op("dve", lambda e: e.scalar_tensor_tensor(out=fc[:], in0=fc[:], scalar=-1.0, in1=lf[:], op0=ALU.mult,
                                                          op1=ALU.add), [Bfc, Blf], [Bfc])
            ph.op("dve", lambda e: e.tensor_scalar(out=fc[:], in0=fc[:], scalar1=tot[:, 0:1], scalar2=None, op0=ALU.add),
                  [Bfc, Btot], [Bfc])
        ph.op("dve", lambda e, u=u, li=li, fc=fc: e.tensor_tensor(out=u[:], in0=li[:], in1=fc[:], op=ALU.subtract), [Bli, Bfc], [Bu])
        if d == 0:
            ph.op("dve", lambda e, m0=m0, u=u: e.tensor_tensor_scan(out=m0[:], data0=u[:], data1=u[:], initial=-1e30,
                                                                    op0=ALU.max, op1=ALU.max), [Bu], [Bm0])
            mf, Bmf = m0, Bm0
        else:
            ph.op("dve", lambda e, m0=m0, u=u: e.tensor_copy(out=m0[:], in_=u[:]), [Bu], [Bm0])
            cur, Bcur, nxt, Bnxt = m0, Bm0, m1, Bm1
            k = 1
            while k < S:
                ph.op("dve", lambda e, cur=cur, nxt=nxt, k=k: e.tensor_tensor(out=nxt[:, 0:S - k], in0=cur[:, 0:S - k],
                                                                             in1=cur[:, k:S], op=ALU.max), [Bcur], [Bnxt])
                ph.op("dve", lambda e, cur=cur, nxt=nxt, k=k: e.tensor_copy(out=nxt[:, S - k:S], in_=cur[:, S - k:S]), [Bcur], [Bnxt])
                cur, Bcur, nxt, Bnxt = nxt, Bnxt, cur, Bcur
                k *= 2
            mf, Bmf = cur, Bcur
        ph.dma("sp", UD[4 * d:4 * d + 4, :], u[:], Bu)
        ph.op("dve", lambda e, fc=fc, mf=mf: e.tensor_tensor(out=fc[:], in0=fc[:], in1=mf[:], op=ALU.add), [Bfc, Bmf], [Bfc])
        ph.op("act", lambda e, fc=fc: e.activation(out=fc[:], in_=fc[:], func=AF.Exp, scale=-1.0), [Bfc], [Bfc])
        ph.dma("sp", EMD[4 * d:4 * d + 4, :], fc[:], Bfc)
        ph.op("dve", lambda e, lf=lf, mf=mf: e.tensor_scalar(out=lf[:], in0=mf[:], scalar1=-1.0, scalar2=None, op0=ALU.mult),
              [Bmf, Blf], [Blf])
        ph.dma("sp", NEGM[4 * d:4 * d + 4, :], lf[:], Blf)
    ph.flush()


def emit_mlstm(nc, MQ, MK, MVT, MO, UD, NEGM, EMD, GML, MIX, heads=4):
    ph = Phase(nc, "ml")
    S = SEQ
    NB = 2
    mq = [ph.sbuf(f"mq{i}", [128, S], BF16) for i in range(NB)]
    mk = [ph.sbuf(f"mk{i}", [128, S], BF16) for i in range(NB)]
    mv = [ph.sbuf(f"mv{i}", [128, 32, 256], BF16) for i in range(NB)]
    negm = [ph.sbuf(f"negm{i}", [128, S], F32) for i in range(NB)]
    em = [ph.sbuf(f"em{i}", [128, S], F32) for i in range(NB)]
    ucol = [ph.sbuf(f"ucol{i}", [128, 32], F32) for i in range(NB)]
    hm = ph.sbuf("hm", [128, 2, S], F32)
    ones = ph.sbuf("ones", [128, 128], BF16)
    gml = ph.sbuf("gml", [128, 8], F32)
    r = ph.sbuf("r", [128, TT], F32)
    tmp = ph.sbuf("tmp", [128, TT], F32)
    sq = ph.sbuf("sq", [128, 2, TT], BF16)
    psum = ph.psum("ps", [128, 8, TT], F32)
    Bmq, Bmk, Bmv, Bnegm, Bem, Bucol = ([ph.buf(f"{n}{i}") for i in range(NB)] for n in ("mq", "mk", "mv", "negm", "em", "ucol"))
    Bhm, Bones, Bc, Br, Btmp, Bsq = (ph.buf(n) for n in ("hm", "ones", "consts", "r", "tmp", "sq"))
    sbanks = make_banks(ph, psum, [0, 1, 2, 3])
    NUM = [psum[:, 4, :], psum[:, 5, :]]
    DEN = psum[:, 6, :]
    STB = psum[:, 7, :]
    Bnum = [ph.buf("num0"), ph.buf("num1")]
    Bden, Bstb = ph.buf("den"), ph.buf("stb")
    wst = make_stage(ph, "wt", 3, [128, TT], BF16)
    sst = make_stage(ph, "sc", 3, [128, TT], BF16)
    ast = make_stage(ph, "at", 4, [128, TT], BF16)
    mst = make_stage(ph, "mo", 2, [128, TT], BF16)
    ost = make_stage(ph, "os", 2, [128, TT], BF16)
    MVv = MVT.rearrange("(st p) c -> p st c", p=128)
    ph.dma("sp", gml[:], GML, Bc)
    ph.op("dve", lambda e: e.memset(ones[:], 1.0), writes=[Bones])

    def load_head(h):
        i = h % NB
        ph.dma("sp", mq[i][:], MQ[h], Bmq[i])
        ph.dma("sp", mk[i][:], MK[h], Bmk[i])
        for a_ in range(4):
            ph.dma("sp", mv[i][:, a_ * 8:(a_ + 1) * 8, :], MVv[:, a_ * 8:(a_ + 1) * 8, h * 256:(h + 1) * 256], Bmv[i])

    def load_dir(g):
        h, d = g // 2, g % 2
        i = g % NB
        row = 4 * d + h
        ph.dma("sp", negm[i][:], NEGM[row:row + 1, :].partition_broadcast(128), Bnegm[i])
        ph.dma("sp", em[i][:], EMD[row:row + 1, :].partition_broadcast(128), Bem[i])
        for a_ in range(4):
            ph.dma("sp", ucol[i][:, a_ * 8:(a_ + 1) * 8], UD[row, a_ * 1024:(a_ + 1) * 1024].rearrange("(st p) -> p st", p=128),
                   Bucol[i])

    def head_norm(h):
        for jt in range(NT):
            ts = slice(jt * TT, (jt + 1) * TT)
            ph.op("act", lambda e, ts=ts: e.activation(out=sq[:], in_=hm[:, :, ts], func=AF.Square), [Bhm], [Bsq])
            ph.op("pe", mm_group(STB, [(ones[:], sq[:, c, :]) for c in range(2)]), [Bones, Bsq], [Bstb])
            ph.op("act", lambda e: e.activation(out=r[:], in_=STB, func=AF.Sqrt, bias=EPS, scale=1.0 / 256.0), [Bstb], [Br])
            ph.op("dve", lambda e: e.reciprocal(out=r[:], in_=r[:]), [Br], [Br])
            for c in range(2):
                mo, Bmo = mst.next()
                ph.dma("sp", mo[:], MO[h * 256 + c * 128:h * 256 + (c + 1) * 128, ts], Bmo)
                ph.op("dve", lambda e, c=c, ts=ts: e.scalar_tensor_tensor(out=tmp[:], in0=hm[:, c, ts],
                                                                          scalar=gml[:, h * 2 + c:h * 2 + c + 1], in1=r[:],
                                                                          op0=ALU.mult, op1=ALU.mult), [Bhm, Bc, Br], [Btmp])
                o, Bo = ost.next()
                ph.op("dve", lambda e, o=o, mo=mo: e.tensor_tensor(out=o[:], in0=tmp[:], in1=mo[:], op=ALU.mult), [Btmp, Bmo], [Bo])
                ph.dma("sp", MIX[1024 + h * 256 + c * 128:1024 + h * 256 + (c + 1) * 128, ts], o[:], Bo)

    load_head(0)
    load_dir(0)
    load_dir(1)
    if heads > 1:
        load_head(1)
    ngroups = heads * 2
    steps = []
    for g in range(ngroups):
        h, d = g // 2, g % 2
        for jt in range(NT):
            jss = list(range(0, 4 * jt + 4)) if d == 0 else list(range(4 * jt, 32))
            for n, js in enumerate(jss):
                st = {}
                first, last = (n == 0), (n == len(jss) - 1)

                def A(g=g, h=h, d=d, jt=jt, js=js, st=st):
                    hi, gi = h % NB, g % NB
                    ts = slice(jt * TT, (jt + 1) * TT)
                    ks = slice(js * 128, (js + 1) * 128)
                    sb, Bsb = sbanks.next()
                    ph.op("pe", lambda e: e.matmul(sb, mk[hi][:, ks], mq[hi][:, ts], start=True, stop=True), [Bmk[hi], Bmq[hi]], [Bsb])
                    wt, Bwt = wst.next()
                    ph.op("act", lambda e: e.activation(out=wt[:], in_=negm[gi][:, ts], func=AF.Exp, bias=ucol[gi][:, js:js + 1],
                                                        scale=1.0), [Bnegm[gi], Bucol[gi]], [Bwt])
                    sc, Bsc = sst.next()
                    ph.op("act", lambda e: e.activation(out=sc[:], in_=sb, func=AF.Copy), [Bsb], [Bsc])
                    at, Bat = ast.next()
                    ph.op("dve", lambda e: e.tensor_tensor(out=at[:], in0=sc[:], in1=wt[:], op=ALU.mult), [Bsc, Bwt], [Bat])
                    if 4 * jt <= js <= 4 * jt + 3:
                        if d == 0:
                            pat, base, cm = [[1, TT]], jt * TT - js * 128, -1
                        else:
                            pat, base, cm = [[-1, TT]], js * 128 - jt * TT, 1
                        ph.op("pool", lambda e: e.affine_select(out=at[:], in_=at[:], pattern=pat, compare_op=ALU.is_ge, fill=0.0,
                                                                base=base, channel_multiplier=cm), [Bat], [Bat])
                    st["at"] = (at, Bat)

                def Bf(g=g, h=h, d=d, jt=jt, js=js, st=st, first=first, last=last):
                    hi, gi = h % NB, g % NB
                    ts = slice(jt * TT, (jt + 1) * TT)
                    at, Bat = st["at"]

                    def fn(pe):
                        for c in range(2):
                            pe.matmul(NUM[c], mv[hi][:, js, c * 128:(c + 1) * 128], at[:], start=first, stop=last)
                        return pe.matmul(DEN, ones[:], at[:], start=first, stop=last)
                    ph.op("pe", fn, [Bmv[hi], Bat, Bones], [Bnum[0], Bnum[1], Bden])
                    if not last:
                        return
                    ph.op("act", lambda e: e.activation(out=r[:], in_=DEN, func=AF.Abs), [Bden], [Br])
                    ph.op("dve", lambda e: e.tensor_tensor(out=r[:], in0=r[:], in1=em[gi][:, ts], op=ALU.max), [Br, Bem[gi]], [Br])
                    ph.op("dve", lambda e: e.reciprocal(out=r[:], in_=r[:]), [Br], [Br])
                    for c in range(2):
                        if d == 0:
                            ph.op("dve", lambda e, c=c: e.tensor_tensor(out=hm[:, c, ts], in0=NUM[c], in1=r[:], op=ALU.mult),
                                  [Bnum[c], Br], [Bhm])
                        else:
                            ph.op("dve", lambda e, c=c: e.tensor_tensor(out=tmp[:], in0=NUM[c], in1=r[:], op=ALU.mult),
                                  [Bnum[c], Br], [Btmp])
                            ph.op("dve", lambda e, c=c: e.tensor_tensor(out=hm[:, c, ts], in0=hm[:, c, ts], in1=tmp[:], op=ALU.add),
                                  [Bhm, Btmp], [Bhm])
                    if jt == NT - 1:
                        if g + 2 < ngroups:
                            load_dir(g + 2)
                        if d == 1:
                            head_norm(h)
                            if h + 2 < heads:
                                load_head(h + 2)
                steps.append((A, Bf))
    per_head = {}
    idx = 0
    for g in range(ngroups):
        cnt = sum((4 * jt + 4) if g % 2 == 0 else (32 - 4 * jt) for jt in range(NT))
        per_head.setdefault(g // 2, []).extend(steps[idx:idx + cnt])
        idx += cnt
    for h in range(heads):
        run_pipeline(per_head[h], 2)
    ph.flush()


def tile_cols(w, cols_list):
    w = np.asarray(w, dtype=np.float32)
    K = w.shape[0]
    kc = K // 128
    out = np.zeros((len(cols_list), 128, kc, 128), dtype=np.float32)
    for i, cols in enumerate(cols_list):
        cols = np.asarray(cols)
        out[i, :, :, :len(cols)] = w[:, cols].reshape(kc, 128, len(cols)).transpose(1, 0, 2)
    return out.reshape(len(cols_list), 128, kc * 128)


def tile_wgu(w):
    w = np.asarray(w, dtype=np.float32)
    n = w.shape[1] // 128
    kc = w.shape[0] // 128
    return np.ascontiguousarray(w.reshape(kc, 128, n, 128).transpose(2, 1, 0, 3).reshape(n, 128, kc * 128))


def tile_g(g):
    g = np.asarray(g, dtype=np.float32)
    return np.ascontiguousarray(g.reshape(-1, 128).T)


ROPE_THETA = 10000.0
LAM_INIT1 = 0.8 - 0.6 * float(np.exp(-0.3))


def prep_weights(inp):
    W = {}
    for l in (0, 1):
        for f in (1, 2):
            p = f"l{l}_ffn{f}"
            W[f"{p}_wgu"] = tile_wgu(inp[f"{p}_w_gu"])
            W[f"{p}_wdn"] = tile_wgu(inp[f"{p}_w_down"])
            W[f"{p}_g"] = tile_g(inp[f"{p}_norm"])
    r = np.arange
    cols = [r(i * 128, (i + 1) * 128) for i in range(8)]
    cols.append(np.concatenate([1024 + r(64), 1024 + 32 + r(32), 1024 + r(32)]))
    cols += [1088 + r(i * 128, (i + 1) * 128) for i in range(4)]
    cols += [1600 + r(i * 128, (i + 1) * 128) for i in range(4)]
    cols += [2112 + r(i * 128, (i + 1) * 128) for i in range(8)]
    cols += [3136 + r(i * 128, (i + 1) * 128) for i in range(8)]
    cols.append(4160 + r(16))
    W["l0_win"] = tile_cols(inp["l0_w_in"], cols)
    cq, ck = [], []
    for h in range(8):
        cq.append(h * 192 + r(128))
        cq.append(np.concatenate([h * 192 + 128 + r(64), h * 192 + 128 + 32 + r(32), h * 192 + 128 + r(32)]))
        ck.append(h * 256 + r(128))
        ck.append(h * 256 + 128 + r(128))
    W["l0_wuq"] = tile_cols(inp["l0_w_uq"], cq)
    W["l0_wukv"] = tile_cols(inp["l0_w_ukv"], ck)
    W["l0_gmix"] = tile_g(inp["l0_mix_norm"])
    W["l0_gcq"] = tile_g(inp["l0_g_cq"])
    W["l0_gckv"] = tile_g(inp["l0_g_ckv"])
    W["l0_bg"] = np.ascontiguousarray(np.asarray(inp["l0_b_gates"], np.float32).reshape(16, 1))
    W["l0_gml"] = tile_g(inp["l0_g_mlstm"])
    W["l0_wo"] = tile_wgu(inp["l0_w_o"])
    c1 = []
    for part in range(2):
        for h in range(8):
            for w in range(2):
                c1.append(part * 2048 + h * 256 + w * 128 + r(128))
    c1 += [4096 + r(i * 128, (i + 1) * 128) for i in range(16)]
    W["l1_win"] = tile_cols(inp["l1_w_in"], c1)
    W["l1_gmix"] = tile_g(inp["l1_mix_norm"])
    W["l1_wo"] = tile_wgu(inp["l1_w_o"])
    W["l1_gsub"] = tile_g(inp["l1_g_sub"])
    W["l1_lam"] = np.ascontiguousarray(np.concatenate([np.asarray(inp[k], np.float32) for k in
                                                       ("l1_lam_q1", "l1_lam_k1", "l1_lam_q2", "l1_lam_k2")]).reshape(1, 512))
    W["gfin"] = tile_g(inp["final_norm"])
    invf = (ROPE_THETA ** (-np.arange(0, 64, 2, dtype=np.float32) / 64)).astype(np.float32)
    W["invf"] = np.ascontiguousarray(np.concatenate([invf, invf]).reshape(64, 1))
    return W


def emit_copy(nc, src, dst, tag):
    ph = Phase(nc, tag)
    t = ph.sbuf("t", [128, KC, TT], F32)
    Bt = ph.buf("t")
    sv = src.rearrange("(kc p) s -> p kc s", p=128)
    dv = dst.rearrange("(kc p) s -> p kc s", p=128)
    for i in range(NT):
        ph.dma("sp", t[:], sv[:, :, i * TT:(i + 1) * TT], Bt)
        ph.dma("sp", dv[:, :, i * TT:(i + 1) * TT], t[:], Bt)
    ph.flush()


SCRATCH_EXT = True


def build_program(W, debug=False, upto=99, start=0, dheads=8, dtiles=NT):
    nc = bass.Bass("TRN2", target_bir_lowering=False)
    A = {}

    def din(name, shape, dt=F32):
        A[name] = nc.dram_tensor(name, list(shape), dt, kind="ExternalInput").ap()
        return A[name]

    def scratch(name, shape, dt, dbg=False):
        kind = "ExternalOutput" if (dbg and (debug or SCRATCH_EXT)) else "Internal"
        return nc.dram_tensor(name, list(shape), dt, kind=kind).ap()

    R = din("xT", [D_MODEL, SEQ])
    POS = din("pos", [1, SEQ], I32)
    POSC = din("posc", [128, 32], I32)
    for k, v in W.items():
        din(k, v.shape)
    OUT = nc.dram_tensor("out", [D_MODEL, SEQ], F32, kind="ExternalOutput").ap()
    COS = scratch("COS", [64, SEQ], F32)
    SIN = scratch("SIN", [64, SEQ], F32)
    QN = scratch("QN", [8, 128, SEQ], BF16)
    QR = scratch("QR", [8, 64, SEQ], BF16)
    KN = scratch("KN", [8, 128, SEQ], BF16)
    KR = scratch("KR", [64, SEQ], BF16)
    VT = scratch("VT", [SEQ, 1024], BF16)
    MQ = scratch("MQ", [4, 128, SEQ], BF16)
    MK = scratch("MK", [4, 128, SEQ], BF16)
    MVT = scratch("MVT", [SEQ, 1024], BF16)
    MO = scratch("MO", [1024, SEQ], BF16)
    G = scratch("G", [16, SEQ], F32)
    UD = scratch("UD", [8, SEQ], F32)
    NEGM = scratch("NEGM", [8, SEQ], F32)
    EMD = scratch("EMD", [8, SEQ], F32)
    MIX = scratch("MIX", [2048, SEQ], BF16, True)
    Q12 = scratch("Q12", [16, 128, SEQ], BF16, True)
    K12 = scratch("K12", [16, 128, SEQ], BF16, True)
    VT1 = scratch("VT1", [SEQ, 2048], BF16, True)
    OD = scratch("OD", [2048, SEQ], BF16, True)
    steps = [
        lambda: emit_tables(nc, POS, A["invf"], COS, SIN),
        lambda: emit_ffn(nc, R, A["l0_ffn1_wgu"], A["l0_ffn1_wdn"], A["l0_ffn1_g"], "f01"),
        lambda: emit_inproj0(nc, R, A["l0_gmix"], A["l0_win"], A["l0_gcq"], A["l0_gckv"], A["l0_wuq"], A["l0_wukv"],
                             A["l0_bg"], COS, SIN, QN, QR, KN, KR, VT, MQ, MK, MVT, MO, G),
        lambda: emit_mla(nc, QN, QR, KN, KR, VT, MIX),
        lambda: emit_mlstm_prep(nc, G, UD, NEGM, EMD),
        lambda: emit_mlstm(nc, MQ, MK, MVT, MO, UD, NEGM, EMD, A["l0_gml"], MIX),
        lambda: emit_oproj(nc, R, MIX, A["l0_wo"], "op0"),
        lambda: emit_copy(nc, R, scratch("X2", [D_MODEL, SEQ], F32, True), "cx2") if debug else None,
        lambda: emit_ffn(nc, R, A["l0_ffn2_wgu"], A["l0_ffn2_wdn"], A["l0_ffn2_g"], "f02"),
        lambda: emit_ffn(nc, R, A["l1_ffn1_wgu"], A["l1_ffn1_wdn"], A["l1_ffn1_g"], "f11"),
        lambda: emit_copy(nc, R, scratch("X4", [D_MODEL, SEQ], F32, True), "cx4") if debug else None,
        lambda: emit_inproj1(nc, R, A["l1_gmix"], A["l1_win"], Q12, K12, VT1),
        lambda: emit_diff(nc, Q12, K12, VT1, POS, POSC, A["l1_lam"], A["l1_gsub"], OD, heads=dheads, qtiles=dtiles,
                          DBG=(scratch("DLAM", [128, 4], F32, True), scratch("DDALL", [128, 2, TT], F32, True)) if debug else None),
        lambda: emit_oproj(nc, R, OD, A["l1_wo"], "op1"),
        lambda: emit_ffn(nc, R, A["l1_ffn2_wgu"], A["l1_ffn2_wdn"], A["l1_ffn2_g"], "f12"),
        lambda: emit_final(nc, R, A["gfin"], OUT),
    ]
    for i, st in enumerate(steps):
        if start <= i <= upto:
            st()
    return nc


def core_inputs(inp, W, b):
    pos = np.ascontiguousarray(np.asarray(inp["positions"][b], np.int32).reshape(1, SEQ))
    m = {"xT": np.ascontiguousarray(np.asarray(inp["x"][b], np.float32).T), "pos": pos,
         "posc": np.ascontiguousarray(pos.reshape(32, 128).T)}
    m.update(W)
    return m


def kernel(**inputs):
    W = prep_weights(inputs)
    nc = build_program(W)
    in_maps = [core_inputs(inputs, W, b) for b in range(NCORES)]
    res = run_bass_kernel_spmd(nc, in_maps, core_ids=list(range(NCORES)))
    out = np.stack([np.asarray(res.results[b]["out"], np.float32).T for b in range(NCORES)], axis=0)
    return np.ascontiguousarray(out)
```

```python
import contextlib
import numpy as np
import concourse.bass as bass
import concourse.mybir as mybir
from concourse.bass_utils import run_bass_kernel_spmd

F32 = mybir.dt.float32
BF16 = mybir.dt.bfloat16
I32 = mybir.dt.int32
AF = mybir.ActivationFunctionType
ALU = mybir.AluOpType

D_MODEL = 2048
SEQ = 4096
D_FF = 5632
EPS = 1e-6
NCORES = 8
TT = 512
NT = SEQ // TT
KC = D_MODEL // 128
FC = D_FF // 128


class Buf:
    def __init__(self, name):
        self.name = name
        self.last_w = None
        self.readers = []
        self.sem = None
        self.sem_n = 0


class Op:
    __slots__ = ("eng", "fn", "reads", "writes", "dma", "deps", "needs_inc", "ev", "idx", "tiny")

    def __init__(self, eng, fn, reads, writes, dma, tiny=False):
        self.eng, self.fn, self.reads, self.writes, self.dma = eng, fn, reads, writes, dma
        self.tiny = tiny
        self.deps = []
        self.needs_inc = False
        self.ev = None


class Phase:
    ENGS = ("pe", "act", "dve", "pool", "sp")

    def __init__(self, nc, tag):
        self.nc = nc
        self.tag = tag
        self.ops = []
        self.eng = {"pe": nc.tensor, "act": nc.scalar, "dve": nc.vector, "pool": nc.gpsimd, "sp": nc.sync}
        self.snap = nc.snapshot_sems()
        self.es = contextlib.ExitStack()
        self.bufs = []

    def sbuf(self, name, shape, dt):
        return self.es.enter_context(self.nc.sbuf_tensor(f"{self.tag}_{name}", list(shape), dt))

    def psum(self, name, shape, dt=F32):
        return self.es.enter_context(self.nc.psum_tensor(f"{self.tag}_{name}", list(shape), dt))

    def buf(self, name):
        b = Buf(name)
        self.bufs.append(b)
        return b

    def op(self, eng, fn, reads=(), writes=(), tiny=False):
        o = Op(eng, fn, tuple(reads), tuple(writes), False, tiny)
        self.ops.append(o)
        return o

    def dma(self, queue, out, in_, sb):
        sb = tuple(sb) if isinstance(sb, (tuple, list)) else (sb,)
        o = Op(queue, lambda e: e.dma_start(out=out, in_=in_), (), sb, True)
        self.ops.append(o)
        return o

    def flush(self):
        nc = self.nc
        for o in self.ops:
            deps = []
            for b in o.reads:
                if b.last_w is not None:
                    deps.append(b.last_w)
            for b in o.writes:
                if b.last_w is not None:
                    deps.append(b.last_w)
                deps.extend(b.readers)
            seen = set()
            for p in deps:
                if p is o or id(p) in seen:
                    continue
                seen.add(id(p))
                if (not p.dma) and (not o.dma) and p.eng == o.eng and not (p.tiny or o.tiny):
                    continue
                if (not p.dma) and o.dma and p.eng == o.eng:
                    continue
                o.deps.append(p)
                p.needs_inc = True
            for b in o.reads:
                if not o.dma:
                    b.readers = [r for r in b.readers if r.dma or r.eng != o.eng]
                b.readers.append(o)
            for b in o.writes:
                b.last_w = o
                b.readers = []
        last = {}
        for o in self.ops:
            if not o.dma:
                last[o.eng] = o
        for o in last.values():
            o.needs_inc = True
        esem = {}
        ecnt = {}
        for e in self.ENGS:
            esem[e] = nc.alloc_semaphore(f"{self.tag}_t_{e}")
            ecnt[e] = 0
        waited = {e: {} for e in self.ENGS}
        all_ev = {}

        def do_wait(e, ev):
            h, v, key = ev
            if waited[e].get(key, 0) >= v:
                return
            waited[e][key] = v
            self.eng[e].wait_ge(h, v)

        for o in self.ops:
            e = o.eng
            for p in o.deps:
                do_wait(e, p.ev)
            with nc.allow_non_contiguous_dma(reason="small strided side loads"):
                ins = o.fn(self.eng[e])
            if o.dma:
                b = o.writes[0]
                if b.sem is None:
                    b.sem = nc.alloc_semaphore(f"{self.tag}_d_{b.name}")
                b.sem_n += 16
                assert b.sem_n < 60000, b.name
                ins.then_inc(b.sem, 16)
                o.ev = (b.sem, b.sem_n, "d_" + b.name)
                all_ev[o.ev[2]] = o.ev
            elif o.needs_inc:
                ecnt[e] += 1
                assert ecnt[e] < 60000, (self.tag, e)
                ins.then_inc(esem[e], 1)
                o.ev = (esem[e], ecnt[e], "t_" + e)
                all_ev[o.ev[2]] = o.ev
        for ev in all_ev.values():
            do_wait("sp", ev)
        nc.all_engine_barrier()
        nc.clear_and_free_semaphores(nc.allocated_since(self.snap))
        nc.all_engine_barrier()
        self.es.close()
        self.ops = []


def mm_group(out, pairs):
    def fn(pe):
        n = len(pairs)
        for i, (l, r) in enumerate(pairs):
            ins = pe.matmul(out, l, r, start=(i == 0), stop=(i == n - 1))
        return ins
    return fn


def emit_ffn(nc, R, wgu, wdn, gn, tag, tiles=NT):
    ph = Phase(nc, tag)
    NGU, NDN = 3, 2
    xb = [ph.sbuf(f"x{i}", [128, KC, TT], F32) for i in range(2)]
    xsq = ph.sbuf("xsq", [128, KC, TT], BF16)
    xn = ph.sbuf("xn", [128, KC, TT], BF16)
    hb = ph.sbuf("h", [128, FC, TT], BF16)
    wg = [ph.sbuf(f"wg{i}", [128, 2, KC * 128], BF16) for i in range(NGU)]
    wd = [ph.sbuf(f"wd{i}", [128, FC * 128], BF16) for i in range(NDN)]
    stmp = [ph.sbuf(f"st{i}", [128, TT], F32) for i in range(2)]
    rstd = ph.sbuf("rstd", [128, TT], F32)
    ones = ph.sbuf("ones", [128, 128], BF16)
    gsb = ph.sbuf("g", [128, KC], F32)
    psum = ph.psum("ps", [128, 8, TT], F32)
    Bx = [ph.buf(f"x{i}") for i in range(2)]
    Bxsq, Bxn, Bh, Brstd, Bones, Bg = (ph.buf(n) for n in ("xsq", "xn", "h", "rstd", "ones", "g"))
    Bwg = [ph.buf(f"wg{i}") for i in range(NGU)]
    Bwd = [ph.buf(f"wd{i}") for i in range(NDN)]
    Bst = [ph.buf(f"st{i}") for i in range(2)]
    Bps = [ph.buf(f"ps{i}") for i in range(8)]
    Rv = R.rearrange("(kc p) s -> p kc s", p=128)

    ph.dma("sp", gsb[:], gn, Bg)
    ph.op("dve", lambda e: e.memset(ones[:], 1.0), writes=[Bones])
    ph.dma("sp", xb[0][:], Rv[:, :, 0:TT], Bx[0])
    gi = 0
    di = 0
    for t in range(tiles):
        x, BX = xb[t % 2], Bx[t % 2]
        if t + 1 < tiles:
            ph.dma("sp", xb[(t + 1) % 2][:], Rv[:, :, (t + 1) * TT:(t + 2) * TT], Bx[(t + 1) % 2])
        ph.op("act", lambda e, x=x: e.activation(out=xsq[:], in_=x[:], func=AF.Square), [BX], [Bxsq])
        ph.op("pe", mm_group(psum[:, 0, :], [(ones[:], xsq[:, kc, :]) for kc in range(KC)]), [Bones, Bxsq], [Bps[0]])
        ph.op("act", lambda e: e.activation(out=rstd[:], in_=psum[:, 0, :], func=AF.Sqrt, bias=EPS,
                                            scale=1.0 / D_MODEL), [Bps[0]], [Brstd])
        ph.op("dve", lambda e: e.reciprocal(out=rstd[:], in_=rstd[:]), [Brstd], [Brstd])

        def xn_fn(e, x=x):
            for kc in range(KC):
                ins = e.scalar_tensor_tensor(out=xn[:, kc, :], in0=x[:, kc, :], scalar=gsb[:, kc:kc + 1],
                                             in1=rstd[:], op0=ALU.mult, op1=ALU.mult)
            return ins
        ph.op("dve", xn_fn, [BX, Bg, Brstd], [Bxn])
        for j in range(FC):
            s = gi % NGU
            ph.dma("pool", wg[s][:, 0, :], wgu[j], Bwg[s])
            ph.dma("pool", wg[s][:, 1, :], wgu[FC + j], Bwg[s])
            bk = 2 * (gi % 3)
            for half in range(2):
                ph.op("pe", mm_group(psum[:, bk + half, :],
                                     [(wg[s][:, half, kc * 128:(kc + 1) * 128], xn[:, kc, :]) for kc in range(KC)]),
                      [Bwg[s], Bxn], [Bps[bk + half]])
            st, BST = stmp[gi % 2], Bst[gi % 2]
            ph.op("act", lambda e, st=st, bk=bk: e.activation(out=st[:], in_=psum[:, bk, :], func=AF.Silu),
                  [Bps[bk]], [BST])
            ph.op("dve", lambda e, st=st, bk=bk, j=j: e.tensor_tensor(out=hb[:, j, :], in0=st[:],
                                                                       in1=psum[:, bk + 1, :], op=ALU.mult),
                  [BST, Bps[bk + 1]], [Bh])
            gi += 1
        for m in range(KC):
            s = di % NDN
            ph.dma("pool", wd[s][:], wdn[m], Bwd[s])
            bk = 6 + di % 2
            ph.op("pe", mm_group(psum[:, bk, :], [(wd[s][:, kc * 128:(kc + 1) * 128], hb[:, kc, :]) for kc in range(FC)]),
                  [Bwd[s], Bh], [Bps[bk]])
            ph.op("dve", lambda e, x=x, bk=bk, m=m: e.scalar_tensor_tensor(
                out=x[:, m, :], in0=psum[:, bk, :], scalar=0.5, in1=x[:, m, :], op0=ALU.mult, op1=ALU.add),
                [Bps[bk]], [BX])
            di += 1
        ph.dma("sp", Rv[:, :, t * TT:(t + 1) * TT], x[:], BX)
    ph.flush()


class WS:
    def __init__(self, ph, name, nslots, width, queue="pool"):
        self.t = [ph.sbuf(f"{name}{i}", [128, width], BF16) for i in range(nslots)]
        self.b = [ph.buf(f"{name}{i}") for i in range(nslots)]
        self.i = 0
        self.ph = ph
        self.q = queue

    def load(self, src):
        k = self.i % len(self.t)
        self.i += 1
        self.ph.dma(self.q, self.t[k][:], src, self.b[k])
        return self.t[k], self.b[k]


class Rot:
    def __init__(self, items):
        self.items = items
        self.i = 0

    def next(self):
        it = self.items[self.i % len(self.items)]
        self.i += 1
        return it


def make_banks(ph, psum, ids):
    return Rot([(psum[:, k, :], ph.buf(f"bank{k}")) for k in ids])


def make_stage(ph, name, n, shape, dt):
    return Rot([(ph.sbuf(f"{name}{i}", shape, dt), ph.buf(f"{name}{i}")) for i in range(n)])


def rms_norm(ph, x, Bx, koff, nk, g, Bg, out, Bout, ooff, sq, Bsq, ones, Bones, bank, Bbank, rstd, Brstd, post=1.0):
    D = nk * 128
    ph.op("act", lambda e: e.activation(out=sq[:, 0:nk, :], in_=x[:, koff:koff + nk, :], func=AF.Square), [Bx], [Bsq])
    ph.op("pe", mm_group(bank, [(ones[:], sq[:, kc, :]) for kc in range(nk)]), [Bones, Bsq], [Bbank])
    ph.op("act", lambda e: e.activation(out=rstd[:], in_=bank, func=AF.Sqrt, bias=EPS * post ** -2,
                                        scale=1.0 / (D * post * post)), [Bbank], [Brstd])
    ph.op("dve", lambda e: e.reciprocal(out=rstd[:], in_=rstd[:]), [Brstd], [Brstd])

    def fn(e):
        for kc in range(nk):
            ins = e.scalar_tensor_tensor(out=out[:, ooff + kc, :], in0=x[:, koff + kc, :], scalar=g[:, kc:kc + 1],
                                         in1=rstd[:], op0=ALU.mult, op1=ALU.mult)
        return ins
    ph.op("dve", fn, [Bx, Bg, Brstd], [Bout])


def emit_tables(nc, POS, INVF, COS, SIN):
    ph = Phase(nc, "tb")
    posi = ph.sbuf("posi", [64, SEQ], I32)
    ang = ph.sbuf("ang", [64, SEQ], F32)
    r1 = ph.sbuf("r1", [64, SEQ], F32)
    r2 = ph.sbuf("r2", [64, SEQ], F32)
    invf = ph.sbuf("invf", [64, 1], F32)
    Bp, Ba, B1, B2, Bi = (ph.buf(n) for n in ("posi", "ang", "r1", "r2", "invf"))
    TWO_PI = 2.0 * np.pi
    SH = 1.0 - 1e-6
    C1 = 6.28125
    C2 = float(np.float32(TWO_PI - C1))
    ki = ph.sbuf("ki", [64, SEQ], I32)
    Bk = ph.buf("ki")
    ph.dma("sp", posi[:], POS.partition_broadcast(64), Bp)
    ph.dma("sp", invf[:], INVF, Bi)
    ph.op("dve", lambda e: e.tensor_copy(out=ang[:], in_=posi[:]), [Bp], [Ba])
    ph.op("dve", lambda e: e.tensor_scalar(out=ang[:], in0=ang[:], scalar1=invf[:, 0:1], scalar2=None, op0=ALU.mult),
          [Ba, Bi], [Ba])
    ph.op("dve", lambda e: e.tensor_scalar(out=r1[:], in0=ang[:], scalar1=1.0 / TWO_PI, scalar2=None, op0=ALU.mult),
          [Ba], [B1])
    ph.op("dve", lambda e: e.tensor_copy(out=ki[:], in_=r1[:]), [B1], [Bk])
    ph.op("dve", lambda e: e.tensor_copy(out=r1[:], in_=ki[:]), [Bk], [B1])
    ph.op("dve", lambda e: e.scalar_tensor_tensor(out=ang[:], in0=r1[:], scalar=-C1, in1=ang[:], op0=ALU.mult,
                                                  op1=ALU.add), [B1, Ba], [Ba])
    ph.op("dve", lambda e: e.scalar_tensor_tensor(out=ang[:], in0=r1[:], scalar=-C2, in1=ang[:], op0=ALU.mult,
                                                  op1=ALU.add), [B1, Ba], [Ba])

    def fold(t, Bt):
        ph.op("dve", lambda e: e.tensor_scalar(out=r1[:], in0=t[:], scalar1=np.pi, scalar2=TWO_PI, op0=ALU.is_gt,
                                               op1=ALU.mult), [Bt], [B1])
        ph.op("dve", lambda e: e.tensor_tensor(out=t[:], in0=t[:], in1=r1[:], op=ALU.subtract), [Bt, B1], [Bt])
        ph.op("dve", lambda e: e.tensor_scalar(out=r1[:], in0=t[:], scalar1=-np.pi, scalar2=TWO_PI, op0=ALU.is_lt,
                                               op1=ALU.mult), [Bt], [B1])
        ph.op("dve", lambda e: e.tensor_tensor(out=t[:], in0=t[:], in1=r1[:], op=ALU.add), [Bt, B1], [Bt])
    fold(ang, Ba)
    ph.op("dve", lambda e: e.tensor_scalar(out=r2[:], in0=ang[:], scalar1=np.pi / 2, scalar2=None, op0=ALU.add),
          [Ba], [B2])
    fold(r2, B2)
    sn = ph.sbuf("sn", [64, SEQ], F32)
    Bs = ph.buf("sn")
    ph.op("act", lambda e: e.activation(out=sn[0:32, :], in_=ang[0:32, :], func=AF.Sin, scale=-SH), [Ba], [Bs])
    ph.op("act", lambda e: e.activation(out=sn[32:64, :], in_=ang[32:64, :], func=AF.Sin, scale=SH), [Ba], [Bs])
    ph.dma("sp", SIN, sn[:], Bs)
    ph.op("act", lambda e: e.activation(out=r1[:], in_=r2[:], func=AF.Sin, scale=SH), [B2, B1], [B1])
    ph.dma("sp", COS, r1[:], B1)
    ph.flush()


def run_pipeline(steps, LA):
    n = len(steps)
    for i in range(n + LA):
        if i < n:
            steps[i][0]()
        if i - LA >= 0:
            steps[i - LA][1]()


def rope_combine(ph, bA, BA, bB, BB, cos_t, sin_t, Btab, f32st, bfst, scale, dst):
    s1, B1 = f32st.next()
    s2, B2 = f32st.next()
    o, Bo = bfst.next()
    ph.op("dve", lambda e: e.tensor_tensor(out=s1[0:64, :], in0=bA, in1=cos_t, op=ALU.mult), [BA, Btab], [B1])
    ph.op("dve", lambda e: e.tensor_tensor(out=s2[0:64, :], in0=bB, in1=sin_t, op=ALU.mult), [BB, Btab], [B2])
    ph.op("dve", lambda e: e.tensor_tensor(out=s1[0:64, :], in0=s1[0:64, :], in1=s2[0:64, :], op=ALU.add), [B1, B2], [B1])
    ph.op("act", lambda e: e.activation(out=o[0:64, :], in_=s1[0:64, :], func=AF.Copy, scale=scale), [B1], [Bo])
    ph.dma("sp", dst, o[0:64, :], Bo)


def emit_inproj0(nc, R, GMIX, W0, GCQ, GCKV, WUQ, WUKV, BG, COS, SIN, QN, QR, KN, KR, VT, MQ, MK, MVT, MO, G, tiles=NT):
    ph = Phase(nc, "ip0")
    x = ph.sbuf("x", [128, KC, TT], F32)
    sq = ph.sbuf("sq", [128, KC, TT], BF16)
    hn = ph.sbuf("hn", [128, KC, TT], BF16)
    cl = ph.sbuf("cl", [128, 8, TT], F32)
    cn = ph.sbuf("cn", [128, 8, TT], BF16)
    wuq = ph.sbuf("wuq", [128, 16, 512], BF16)
    wukv = ph.sbuf("wukv", [128, 16, 512], BF16)
    cos = ph.sbuf("cos", [64, SEQ], F32)
    sin = ph.sbuf("sin", [64, SEQ], F32)
    rstd = ph.sbuf("rstd", [128, TT], F32)
    ones = ph.sbuf("ones", [128, 128], BF16)
    gmix = ph.sbuf("gmix", [128, KC], F32)
    gcq = ph.sbuf("gcq", [128, 4], F32)
    gckv = ph.sbuf("gckv", [128, 4], F32)
    bg = ph.sbuf("bg", [16, 1], F32)
    psum = ph.psum("ps", [128, 8, TT], F32)
    Bx, Bsq, Bhn, Bcl, Bcn, Bwuq, Bwukv, Btab, Brstd, Bones, Bc = (ph.buf(n) for n in (
        "x", "sq", "hn", "cl", "cn", "wuq", "wukv", "tab", "rstd", "ones", "consts"))
    ws = WS(ph, "w", 6, KC * 128)
    banks = make_banks(ph, psum, list(range(8)))
    f32st = make_stage(ph, "fs", 4, [128, TT], F32)
    bfst = make_stage(ph, "bs", 6, [128, TT], BF16)
    Rv = R.rearrange("(kc p) s -> p kc s", p=128)

    ph.dma("sp", gmix[:], GMIX, Bc)
    ph.dma("sp", gcq[:], GCQ, Bc)
    ph.dma("sp", gckv[:], GCKV, Bc)
    ph.dma("sp", bg[:], BG, Bc)
    ph.dma("sp", cos[:], COS, Btab)
    ph.dma("sp", sin[:], SIN, Btab)
    for i in range(16):
        ph.dma("pool", wuq[:, i, :], WUQ[i], Bwuq)
        ph.dma("pool", wukv[:, i, :], WUKV[i], Bwukv)
    ph.op("dve", lambda e: e.memset(ones[:], 1.0), writes=[Bones])
    ph.dma("sp", x[:], Rv[:, :, 0:TT], Bx)

    def fm(w, Bw, c0, c1, a, Ba, aoff, nk, wstride=128):
        bank, Bb = banks.next()
        M = c1 - c0
        ph.op("pe", mm_group(bank[0:M, :], [(w[:, kc * wstride + c0:kc * wstride + c1], a[:, aoff + kc, :])
                                            for kc in range(nk)]), [Bw, Ba], [Bb])
        return bank, Bb

    def evac_store(bank, Bb, M, func, scale, dst, bias=None, f32=False):
        st, Bs = (f32st if f32 else bfst).next()
        if bias is None:
            ph.op("act", lambda e: e.activation(out=st[0:M, :], in_=bank[0:M, :], func=func, scale=scale), [Bb], [Bs])
        else:
            ph.op("act", lambda e: e.activation(out=st[0:M, :], in_=bank[0:M, :], func=func, scale=scale, bias=bias),
                  [Bb, Bc], [Bs])
        ph.dma("sp", dst, st[0:M, :], Bs)

    def tm_group(wlist, a, Ba, aoff, nk, tb, wstride, coff):
        bank, Bb = banks.next()

        def fn(pe):
            for c, (w, _) in enumerate(wlist):
                for kc in range(nk):
                    ins = pe.matmul(bank[:, c * 128:(c + 1) * 128], a[:, aoff + kc, tb * 128:(tb + 1) * 128],
                                    w[:, kc * wstride + coff:kc * wstride + coff + 128], start=(kc == 0), stop=(kc == nk - 1))
            return ins
        ph.op("pe", fn, [Ba] + [b_ for (_, b_) in wlist], [Bb])
        return bank, Bb

    for t in range(tiles):
        ts = slice(t * TT, (t + 1) * TT)
        bank, Bb = banks.next()
        rms_norm(ph, x, Bx, 0, KC, gmix, Bc, hn, Bhn, 0, sq, Bsq, ones, Bones, bank, Bb, rstd, Brstd)
        if t + 1 < tiles:
            ph.dma("sp", x[:], Rv[:, :, (t + 1) * TT:(t + 2) * TT], Bx)
        for i in range(8):
            w, Bw = ws.load(W0[i])
            bank, Bb = fm(w, Bw, 0, 128, hn, Bhn, 0, KC)
            ph.op("act", lambda e, i=i, bank=bank: e.activation(out=cl[:, i, :], in_=bank, func=AF.Copy), [Bb], [Bcl])
        w, Bw = ws.load(W0[8])
        bA, BA = fm(w, Bw, 0, 64, hn, Bhn, 0, KC)
        bB, BB = fm(w, Bw, 64, 128, hn, Bhn, 0, KC)
        rope_combine(ph, bA[0:64, :], BA, bB[0:64, :], BB, cos[:, ts], sin[:, ts], Btab, f32st, bfst, 1.0, KR[:, ts])
        for h in range(4):
            w, Bw = ws.load(W0[9 + h])
            bank, Bb = fm(w, Bw, 0, 128, hn, Bhn, 0, KC)
            evac_store(bank, Bb, 128, AF.Copy, 1.0, MQ[h][:, ts])
        for h in range(4):
            w, Bw = ws.load(W0[13 + h])
            bank, Bb = fm(w, Bw, 0, 128, hn, Bhn, 0, KC)
            evac_store(bank, Bb, 128, AF.Copy, 128.0 ** -0.5, MK[h][:, ts])
        for cg in range(2):
            wl = [ws.load(W0[17 + cg * 4 + c]) for c in range(4)]
            for tb in range(4):
                bank, Bb = tm_group(wl, hn, Bhn, 0, KC, tb, 128, 0)
                evac_store(bank, Bb, 128, AF.Copy, 1.0, MVT[t * TT + tb * 128:t * TT + (tb + 1) * 128, cg * 512:(cg + 1) * 512])
        for i in range(8):
            w, Bw = ws.load(W0[25 + i])
            bank, Bb = fm(w, Bw, 0, 128, hn, Bhn, 0, KC)
            evac_store(bank, Bb, 128, AF.Sigmoid, 1.0, MO[i * 128:(i + 1) * 128, ts])
        w, Bw = ws.load(W0[33])
        bank, Bb = fm(w, Bw, 0, 16, hn, Bhn, 0, KC)
        evac_store(bank, Bb, 16, AF.Identity, 1.0, G[:, ts], bias=bg[:, 0:1], f32=True)
        bank, Bb = banks.next()
        rms_norm(ph, cl, Bcl, 0, 4, gcq, Bc, cn, Bcn, 0, sq, Bsq, ones, Bones, bank, Bb, rstd, Brstd)
        bank, Bb = banks.next()
        rms_norm(ph, cl, Bcl, 4, 4, gckv, Bc, cn, Bcn, 4, sq, Bsq, ones, Bones, bank, Bb, rstd, Brstd)
        qs = 192.0 ** -0.5
        for h in range(8):
            wq = wuq[:, 2 * h, :]
            bank, Bb = fm(wq, Bwuq, 0, 128, cn, Bcn, 0, 4)
            evac_store(bank, Bb, 128, AF.Copy, qs, QN[h][:, ts])
            wr = wuq[:, 2 * h + 1, :]
            bA, BA = fm(wr, Bwuq, 0, 64, cn, Bcn, 0, 4)
            bB, BB = fm(wr, Bwuq, 64, 128, cn, Bcn, 0, 4)
            rope_combine(ph, bA[0:64, :], BA, bB[0:64, :], BB, cos[:, ts], sin[:, ts], Btab, f32st, bfst, qs, QR[h][:, ts])
        for h in range(8):
            wk = wukv[:, 2 * h, :]
            bank, Bb = fm(wk, Bwukv, 0, 128, cn, Bcn, 4, 4)
            evac_store(bank, Bb, 128, AF.Copy, 1.0, KN[h][:, ts])
        for hg in range(2):
            wl = [(wukv[:, 2 * (hg * 4 + c) + 1, :], Bwukv) for c in range(4)]
            for tb in range(4):
                bank, Bb = tm_group(wl, cn, Bcn, 4, 4, tb, 128, 0)
                evac_store(bank, Bb, 128, AF.Copy, 1.0, VT[t * TT + tb * 128:t * TT + (tb + 1) * 128, hg * 512:(hg + 1) * 512])
    ph.flush()


def emit_mla(nc, QN, QR, KN, KR, VT, MIX, heads=8, qtiles=NT):
    ph = Phase(nc, "mla")
    NB = 2
    qn = [ph.sbuf(f"qn{i}", [128, SEQ], BF16) for i in range(NB)]
    qr = [ph.sbuf(f"qr{i}", [128, SEQ], BF16) for i in range(NB)]
    kn = [ph.sbuf(f"kn{i}", [128, SEQ], BF16) for i in range(NB)]
    vv = [ph.sbuf(f"v{i}", [128, 32, 128], BF16) for i in range(NB)]
    kr = ph.sbuf("kr", [128, SEQ], BF16)
    ones = ph.sbuf("ones", [128, 128], BF16)
    rden = ph.sbuf("rden", [128, TT], F32)
    psum = ph.psum("ps", [128, 8, TT], F32)
    Bqn, Bqr, Bkn, Bv = ([ph.buf(f"{n}{i}") for i in range(NB)] for n in ("qn", "qr", "kn", "v"))
    Bkr, Bones, Brden = ph.buf("kr"), ph.buf("ones"), ph.buf("rden")
    sbanks = make_banks(ph, psum, [0, 1, 2, 3])
    obanks = Rot([((psum[:, 4 + 2 * i, :], ph.buf(f"ob{i}")), (psum[:, 5 + 2 * i, :], ph.buf(f"db{i}"))) for i in range(2)])
    pst = make_stage(ph, "pt", 4, [128, TT], BF16)
    ost = make_stage(ph, "os", 2, [128, TT], BF16)
    VTv = VT.rearrange("(st p) c -> p st c", p=128)
    ph.op("pool", lambda e: e.memset(kr[64:128, :], 0.0), writes=[Bkr])
    for i in range(NB):
        ph.op("pool", lambda e, i=i: e.memset(qr[i][64:128, :], 0.0), writes=[Bqr[i]])
    ph.dma("sp", kr[0:64, :], KR, Bkr)
    ph.op("dve", lambda e: e.memset(ones[:], 1.0), writes=[Bones])

    def load(h):
        i = h % NB
        ph.dma("sp", qn[i][:], QN[h], Bqn[i])
        ph.dma("sp", qr[i][0:64, :], QR[h], Bqr[i])
        ph.dma("sp", kn[i][:], KN[h], Bkn[i])
        for a in range(4):
            ph.dma("sp", vv[i][:, a * 8:(a + 1) * 8, :], VTv[:, a * 8:(a + 1) * 8, h * 128:(h + 1) * 128], Bv[i])
    load(0)
    if heads > 1:
        load(1)
    steps = []
    for h in range(heads):
        for jq in range(qtiles):
            grp = {}
            for jk in range(32):
                st = {}

                def A(h=h, jq=jq, jk=jk, st=st):
                    i = h % NB
                    qs = slice(jq * TT, (jq + 1) * TT)
                    ks = slice(jk * 128, (jk + 1) * 128)
                    sb, Bsb = sbanks.next()
                    ph.op("pe", mm_group(sb, [(kn[i][:, ks], qn[i][:, qs]), (kr[:, ks], qr[i][:, qs])]),
                          [Bkn[i], Bqn[i], Bkr, Bqr[i]], [Bsb])
                    pt, Bpt = pst.next()
                    ph.op("act", lambda e, pt=pt, sb=sb: e.activation(out=pt[:], in_=sb, func=AF.Exp), [Bsb], [Bpt])
                    st["pt"] = (pt, Bpt)

                def Bf(h=h, jq=jq, jk=jk, st=st, grp=grp):
                    i = h % NB
                    qs = slice(jq * TT, (jq + 1) * TT)
                    if jk == 0:
                        grp["acc"] = obanks.next()
                    (ob, Bob), (db, Bdb) = grp["acc"]
                    pt, Bpt = st["pt"]

                    def fn(pe):
                        pe.matmul(ob, vv[i][:, jk, :], pt[:], start=(jk == 0), stop=(jk == 31))
                        return pe.matmul(db, ones[:], pt[:], start=(jk == 0), stop=(jk == 31))
                    ph.op("pe", fn, [Bv[i], Bpt, Bones], [Bob, Bdb])
                    if jk == 31:
                        ph.op("dve", lambda e: e.reciprocal(out=rden[:], in_=db), [Bdb], [Brden])
                        o, Bo = ost.next()
                        ph.op("dve", lambda e: e.tensor_tensor(out=o[:], in0=ob, in1=rden[:], op=ALU.mult), [Bob, Brden], [Bo])
                        ph.dma("sp", MIX[h * 128:(h + 1) * 128, qs], o[:], Bo)
                        if jq == qtiles - 1 and h + 2 < heads:
                            load(h + 2)
                steps.append((A, Bf))
    run_pipeline(steps, 2)
    ph.flush()


def emit_oproj(nc, R, A, WO, tag, tiles=NT):
    ph = Phase(nc, tag)
    xb = [ph.sbuf(f"x{i}", [128, KC, TT], F32) for i in range(2)]
    ab = [ph.sbuf(f"a{i}", [128, KC, TT], BF16) for i in range(2)]
    Bx = [ph.buf(f"x{i}") for i in range(2)]
    Ba = [ph.buf(f"a{i}") for i in range(2)]
    psum = ph.psum("ps", [128, 8, TT], F32)
    banks = make_banks(ph, psum, list(range(8)))
    ws = WS(ph, "w", 4, KC * 128)
    Rv = R.rearrange("(kc p) s -> p kc s", p=128)
    Av = A.rearrange("(kc p) s -> p kc s", p=128)

    def load(t):
        ph.dma("sp", xb[t % 2][:], Rv[:, :, t * TT:(t + 1) * TT], Bx[t % 2])
        ph.dma("sp", ab[t % 2][:], Av[:, :, t * TT:(t + 1) * TT], Ba[t % 2])
    load(0)
    for t in range(tiles):
        x, a, BX, BA = xb[t % 2], ab[t % 2], Bx[t % 2], Ba[t % 2]
        if t + 1 < tiles:
            load(t + 1)
        for m in range(KC):
            w, Bw = ws.load(WO[m])
            bank, Bb = banks.next()
            ph.op("pe", mm_group(bank, [(w[:, kc * 128:(kc + 1) * 128], a[:, kc, :]) for kc in range(KC)]), [Bw, BA], [Bb])
            ph.op("dve", lambda e, x=x, m=m, bank=bank: e.tensor_tensor(out=x[:, m, :], in0=bank, in1=x[:, m, :], op=ALU.add),
                  [Bb], [BX])
        ph.dma("sp", Rv[:, :, t * TT:(t + 1) * TT], x[:], BX)
    ph.flush()


def emit_final(nc, R, GF, OUT, tiles=NT):
    ph = Phase(nc, "fin")
    xb = [ph.sbuf(f"x{i}", [128, KC, TT], F32) for i in range(2)]
    ob = [ph.sbuf(f"o{i}", [128, KC, TT], F32) for i in range(2)]
    sq = ph.sbuf("sq", [128, KC, TT], BF16)
    rstd = ph.sbuf("rstd", [128, TT], F32)
    ones = ph.sbuf("ones", [128, 128], BF16)
    g = ph.sbuf("g", [128, KC], F32)
    psum = ph.psum("ps", [128, 8, TT], F32)
    Bx = [ph.buf(f"x{i}") for i in range(2)]
    Bo = [ph.buf(f"o{i}") for i in range(2)]
    Bsq, Brstd, Bones, Bg = (ph.buf(n) for n in ("sq", "rstd", "ones", "g"))
    banks = make_banks(ph, psum, [0, 1])
    Rv = R.rearrange("(kc p) s -> p kc s", p=128)
    Ov = OUT.rearrange("(kc p) s -> p kc s", p=128)
    ph.dma("sp", g[:], GF, Bg)
    ph.op("dve", lambda e: e.memset(ones[:], 1.0), writes=[Bones])
    ph.dma("sp", xb[0][:], Rv[:, :, 0:TT], Bx[0])
    for t in range(tiles):
        if t + 1 < tiles:
            ph.dma("sp", xb[(t + 1) % 2][:], Rv[:, :, (t + 1) * TT:(t + 2) * TT], Bx[(t + 1) % 2])
        bank, Bb = banks.next()
        rms_norm(ph, xb[t % 2], Bx[t % 2], 0, KC, g, Bg, ob[t % 2], Bo[t % 2], 0, sq, Bsq, ones, Bones, bank, Bb, rstd, Brstd)
        ph.dma("sp", Ov[:, :, t * TT:(t + 1) * TT], ob[t % 2][:], Bo[t % 2])
    ph.flush()


def emit_inproj1(nc, R, GMIX, W1, Q12, K12, VT, tiles=NT):
    ph = Phase(nc, "ip1")
    x = ph.sbuf("x", [128, KC, TT], F32)
    sq = ph.sbuf("sq", [128, KC, TT], BF16)
    hn = ph.sbuf("hn", [128, KC, TT], BF16)
    rstd = ph.sbuf("rstd", [128, TT], F32)
    ones = ph.sbuf("ones", [128, 128], BF16)
    gmix = ph.sbuf("gmix", [128, KC], F32)
    psum = ph.psum("ps", [128, 8, TT], F32)
    Bx, Bsq, Bhn, Brstd, Bones, Bc = (ph.buf(n) for n in ("x", "sq", "hn", "rstd", "ones", "consts"))
    ws = WS(ph, "w", 6, KC * 128)
    banks = make_banks(ph, psum, list(range(8)))
    bfst = make_stage(ph, "bs", 6, [128, TT], BF16)
    Rv = R.rearrange("(kc p) s -> p kc s", p=128)
    ph.dma("sp", gmix[:], GMIX, Bc)
    ph.op("dve", lambda e: e.memset(ones[:], 1.0), writes=[Bones])
    ph.dma("sp", x[:], Rv[:, :, 0:TT], Bx)
    for t in range(tiles):
        ts = slice(t * TT, (t + 1) * TT)
        bank, Bb = banks.next()
        rms_norm(ph, x, Bx, 0, KC, gmix, Bc, hn, Bhn, 0, sq, Bsq, ones, Bones, bank, Bb, rstd, Brstd)
        if t + 1 < tiles:
            ph.dma("sp", x[:], Rv[:, :, (t + 1) * TT:(t + 2) * TT], Bx)
        for i in range(32):
            w, Bw = ws.load(W1[i])
            bank, Bb = banks.next()
            ph.op("pe", mm_group(bank, [(w[:, kc * 128:(kc + 1) * 128], hn[:, kc, :]) for kc in range(KC)]), [Bw, Bhn], [Bb])
            st, Bs = bfst.next()
            sc = 128.0 ** -0.5 if i < 16 else 1.0
            ph.op("act", lambda e, st=st, bank=bank, sc=sc: e.activation(out=st[:], in_=bank, func=AF.Copy, scale=sc), [Bb], [Bs])
            ph.dma("sp", (Q12 if i < 16 else K12)[i % 16][:, ts], st[:], Bs)
        for cg in range(4):
            wl = [ws.load(W1[32 + cg * 4 + c]) for c in range(4)]
            for tb in range(4):
                bank, Bb = banks.next()

                def fn(pe, wl=wl, tb=tb, bank=bank):
                    for c, (w, _) in enumerate(wl):
                        for kc in range(KC):
                            ins = pe.matmul(bank[:, c * 128:(c + 1) * 128], hn[:, kc, tb * 128:(tb + 1) * 128],
                                            w[:, kc * 128:(kc + 1) * 128], start=(kc == 0), stop=(kc == KC - 1))
                    return ins
                ph.op("pe", fn, [Bhn] + [b_ for (_, b_) in wl], [Bb])
                st, Bs = bfst.next()
                ph.op("act", lambda e, st=st, bank=bank: e.activation(out=st[:], in_=bank, func=AF.Copy), [Bb], [Bs])
                ph.dma("sp", VT[t * TT + tb * 128:t * TT + (tb + 1) * 128, cg * 512:(cg + 1) * 512], st[:], Bs)
    ph.flush()


def emit_diff(nc, Q12, K12, VT, POS, POSC, LAM, GSUB, OD, heads=8, qtiles=NT, DBG=None):
    ph = Phase(nc, "da")
    NB = 2
    pq = ph.sbuf("pq", [128, SEQ], F32)
    pk = ph.sbuf("pk", [128, 32], F32)
    pki = ph.sbuf("pki", [128, 32], I32)
    dall = ph.sbuf("dall", [128, 32, TT], F32)
    qq = [ph.sbuf(f"q{i}", [128, 2, TT], BF16) for i in range(NB)]
    kk = [ph.sbuf(f"k{i}", [128, 2, SEQ], BF16) for i in range(NB)]
    vv = [ph.sbuf(f"v{i}", [128, 32, 256], BF16) for i in range(NB)]
    ones = ph.sbuf("ones", [128, 128], BF16)
    lamv = ph.sbuf("lamv", [128, 512], F32)
    lamt = ph.sbuf("lamt", [128, 256], F32)
    lam = ph.sbuf("lam", [128, 4], F32)
    gsub = ph.sbuf("gsub", [128, 2], F32)
    r1 = ph.sbuf("r1", [128, 2, TT], F32)
    ot = ph.sbuf("ot", [128, 2, TT], F32)
    tmp = ph.sbuf("tmp", [128, TT], F32)
    sq = ph.sbuf("sq", [128, 2, TT], BF16)
    acc = [ph.sbuf(f"acc{i}", [128, 2, TT], F32) for i in range(2)]
    ahi = ph.sbuf("ahi", [128, 2, TT], BF16)
    alo = ph.sbuf("alo", [128, 2, TT], BF16)
    psum = ph.psum("ps", [128, 8, TT], F32)
    Bpq, Bpk, Bdall, Bones, Blam, Bc, Br1, Bot, Btmp, Bsq, Bahi, Balo = (ph.buf(n) for n in (
        "pq", "pk", "dall", "ones", "lam", "consts", "r1", "ot", "tmp", "sq", "ahi", "alo"))
    Bacc = [ph.buf("acc0"), ph.buf("acc1")]
    Bq = [ph.buf(f"q{i}") for i in range(NB)]
    Bk = [ph.buf(f"k{i}") for i in range(NB)]
    Bv = [ph.buf(f"v{i}") for i in range(NB)]
    spairs = Rot([(psum[:, 0:2, :], ph.buf("sp0")), (psum[:, 2:4, :], ph.buf("sp1"))])
    N_ = [[psum[:, 4, :], psum[:, 5, :]], [psum[:, 6, :], psum[:, 7, :]]]
    Bn = [[ph.buf(f"n{m}{c}") for c in range(2)] for m in range(2)]
    tst = make_stage(ph, "ts", 3, [128, 2, TT], BF16)
    est = make_stage(ph, "es", 3, [128, TT], BF16)
    pst = make_stage(ph, "pt", 4, [128, 2, TT], BF16)
    ost = make_stage(ph, "os", 2, [128, TT], BF16)
    VTv = VT.rearrange("(st p) c -> p st c", p=128)

    pqi = dall[:, 0:8, :].rearrange("p a b -> p (a b)").bitcast(I32)
    ph.dma("sp", pqi, POS.partition_broadcast(128), Bdall)
    ph.dma("sp", pki[:], POSC, Bpk)
    ph.dma("sp", lamv[:], LAM.partition_broadcast(128), Blam)
    ph.dma("sp", gsub[:], GSUB, Bc)
    ph.op("dve", lambda e: e.tensor_copy(out=pq[:], in_=pqi), [Bdall], [Bpq])
    ph.op("dve", lambda e: e.tensor_copy(out=pk[:], in_=pki[:]), [Bpk], [Bpk], tiny=True)
    ph.op("dve", lambda e: e.tensor_scalar(out=pk[:], in0=pk[:], scalar1=-1.0, scalar2=None, op0=ALU.mult), [Bpk], [Bpk], tiny=True)
    ph.op("dve", lambda e: e.memset(ones[:], 1.0), writes=[Bones])
    T = dict(tiny=True)
    ph.op("dve", lambda e: e.memset(lam[:], 0.0), writes=[Blam], **T)
    ph.op("dve", lambda e: e.tensor_tensor(out=lamt[:, 0:128], in0=lamv[:, 0:128], in1=lamv[:, 128:256], op=ALU.mult), [Blam], [Blam], **T)
    ph.op("dve", lambda e: e.tensor_tensor(out=lamt[:, 128:256], in0=lamv[:, 256:384], in1=lamv[:, 384:512], op=ALU.mult), [Blam], [Blam], **T)
    ph.op("dve", lambda e: e.reduce_sum(out=lam[:, 0:1], in_=lamt[:, 0:128], axis=mybir.AxisListType.X), [Blam], [Blam], **T)
    ph.op("dve", lambda e: e.reduce_sum(out=lam[:, 1:2], in_=lamt[:, 128:256], axis=mybir.AxisListType.X), [Blam], [Blam], **T)
    ph.op("act", lambda e: e.activation(out=lam[:, 0:2], in_=lam[:, 0:2], func=AF.Exp), [Blam], [Blam], **T)
    ph.op("act", lambda e: e.activation(out=lam[:, 3:4], in_=lam[:, 0:1], func=AF.Identity, scale=-1.0, bias=-LAM_INIT1), [Blam], [Blam], **T)
    ph.op("act", lambda e: e.activation(out=lam[:, 2:3], in_=lam[:, 1:2], func=AF.Identity, scale=1.0, bias=lam[:, 3:4]), [Blam], [Blam], **T)
    if DBG is not None:
        ph.dma("sp", DBG[0], lam[:], Blam)

    def load(jq, h, n):
        i = n % NB
        qs = slice(jq * TT, (jq + 1) * TT)
        for w in range(2):
            ph.dma("sp", qq[i][:, w, :], Q12[h * 2 + w][:, qs], Bq[i])
            ph.dma("sp", kk[i][:, w, :], K12[h * 2 + w], Bk[i])
        for a_ in range(4):
            ph.dma("sp", vv[i][:, a_ * 8:(a_ + 1) * 8, :], VTv[:, a_ * 8:(a_ + 1) * 8, h * 256:(h + 1) * 256], Bv[i])

    order = [(jq, h) for jq in range(qtiles) for h in range(heads)]
    load(order[0][0], order[0][1], 0)
    if len(order) > 1:
        load(order[1][0], order[1][1], 1)
    post = 1.0 - LAM_INIT1
    steps = []
    for n, (jq, h) in enumerate(order):
        for jk in range(32):
            st = {}

            def A(n=n, jq=jq, h=h, jk=jk, st=st):
                i = n % NB
                qs = slice(jq * TT, (jq + 1) * TT)
                ks = slice(jk * 128, (jk + 1) * 128)
                if h == 0 and jk == 0:
                    def dfn(e):
                        for j2 in range(32):
                            e.tensor_scalar(out=dall[:, j2, :], in0=pq[:, qs], scalar1=pk[:, j2:j2 + 1], scalar2=None, op0=ALU.add)
                            ins = e.scalar_tensor_tensor(out=dall[:, j2, :], in0=dall[:, j2, :], scalar=-1.0, in1=dall[:, j2, :],
                                                         op0=ALU.mult, op1=ALU.max)
                        return ins
                    ph.op("dve", dfn, [Bpq, Bpk], [Bdall])
                    if DBG is not None and jq == 0:
                        ph.dma("sp", DBG[1], dall[:, 0:2, :], Bdall)
                nslope = -(2.0 ** (-8.0 * (h + 1) / 8))
                ee, Be = est.next()
                ph.op("act", lambda e: e.activation(out=ee[:], in_=dall[:, jk, :], func=AF.Exp, scale=nslope), [Bdall], [Be])
                sp, Bsp = spairs.next()

                def sfn(pe):
                    pe.matmul(sp[:, 0, :], kk[i][:, 0, ks], qq[i][:, 0, :], start=True, stop=True)
                    return pe.matmul(sp[:, 1, :], kk[i][:, 1, ks], qq[i][:, 1, :], start=True, stop=True)
                ph.op("pe", sfn, [Bk[i], Bq[i]], [Bsp])
                tt, Bt = tst.next()
                ph.op("act", lambda e: e.activation(out=tt[:], in_=sp, func=AF.Exp), [Bsp], [Bt])
                pt, Bp = pst.next()

                def pfn(e):
                    e.tensor_tensor(out=pt[:, 0, :], in0=tt[:, 0, :], in1=ee[:], op=ALU.mult)
                    return e.tensor_tensor(out=pt[:, 1, :], in0=tt[:, 1, :], in1=ee[:], op=ALU.mult)
                ph.op("dve", pfn, [Bt, Be], [Bp])
                ac, Bac = acc[n % 2], Bacc[n % 2]
                if jk == 0:
                    ph.op("dve", lambda e: e.tensor_copy(out=ac[:], in_=pt[:]), [Bp], [Bac])
                else:
                    ph.op("dve", lambda e: e.tensor_tensor(out=ac[:], in0=ac[:], in1=pt[:], op=ALU.add), [Bp, Bac], [Bac])
                st["pt"] = (pt, Bp)

            def Bf(n=n, jq=jq, h=h, jk=jk, st=st):
                i = n % NB
                qs = slice(jq * TT, (jq + 1) * TT)
                pt, Bp = st["pt"]

                def fn(pe):
                    for m in range(2):
                        for c in range(2):
                            ins = pe.matmul(N_[m][c], vv[i][:, jk, c * 128:(c + 1) * 128], pt[:, m, :], start=(jk == 0), stop=(jk == 31))
                    return ins
                ph.op("pe", fn, [Bv[i], Bp], [Bn[0][0], Bn[0][1], Bn[1][0], Bn[1][1]])
                if jk != 31:
                    return
                ac, Bac = acc[n % 2], Bacc[n % 2]
                ph.op("dve", lambda e: e.tensor_copy(out=ahi[:], in_=ac[:]), [Bac], [Bahi])
                ph.op("dve", lambda e: e.tensor_tensor(out=alo[:], in0=ac[:], in1=ahi[:], op=ALU.subtract), [Bac, Bahi], [Balo])
                dp, Bdp = spairs.next()

                def dfn2(pe):
                    for m in range(2):
                        pe.matmul(dp[:, m, :], ones[:], ahi[:, m, :], start=True, stop=False)
                        ins = pe.matmul(dp[:, m, :], ones[:], alo[:, m, :], start=False, stop=True)
                    return ins
                ph.op("pe", dfn2, [Bones, Bahi, Balo], [Bdp])
                ph.op("dve", lambda e: e.reciprocal(out=r1[:], in_=dp), [Bdp], [Br1])
                ph.op("dve", lambda e: e.tensor_scalar(out=r1[:, 1, :], in0=r1[:, 1, :], scalar1=lam[:, 2:3], scalar2=None, op0=ALU.mult),
                      [Br1, Blam], [Br1])
                for c in range(2):
                    ph.op("dve", lambda e, c=c: e.tensor_tensor(out=ot[:, c, :], in0=N_[0][c], in1=r1[:, 0, :], op=ALU.mult),
                          [Bn[0][c], Br1], [Bot])
                    ph.op("dve", lambda e, c=c: e.tensor_tensor(out=tmp[:], in0=N_[1][c], in1=r1[:, 1, :], op=ALU.mult),
                          [Bn[1][c], Br1], [Btmp])
                    ph.op("dve", lambda e, c=c: e.tensor_tensor(out=ot[:, c, :], in0=ot[:, c, :], in1=tmp[:], op=ALU.add),
                          [Bot, Btmp], [Bot])
                ph.op("act", lambda e: e.activation(out=sq[:], in_=ot[:], func=AF.Square), [Bot], [Bsq])
                np_, Bnp = spairs.next()
                ph.op("pe", mm_group(np_[:, 0, :], [(ones[:], sq[:, c, :]) for c in range(2)]), [Bones, Bsq], [Bnp])
                ph.op("act", lambda e: e.activation(out=r1[:, 0, :], in_=np_[:, 0, :], func=AF.Sqrt, bias=EPS * post ** -2,
                                                    scale=1.0 / (256.0 * post * post)), [Bnp, Br1], [Br1])
                ph.op("dve", lambda e: e.reciprocal(out=r1[:, 0, :], in_=r1[:, 0, :]), [Br1], [Br1])
                for c in range(2):
                    o, Bo = ost.next()
                    ph.op("dve", lambda e, o=o, c=c: e.scalar_tensor_tensor(out=o[:], in0=ot[:, c, :], scalar=gsub[:, c:c + 1],
                                                                           in1=r1[:, 0, :], op0=ALU.mult, op1=ALU.mult), [Bot, Bc, Br1], [Bo])
                    ph.dma("sp", OD[h * 256 + c * 128:h * 256 + (c + 1) * 128, qs], o[:], Bo)
                if n + 2 < len(order):
                    load(order[n + 2][0], order[n + 2][1], n + 2)
            steps.append((A, Bf))
    run_pipeline(steps, 1)
    ph.flush()


def emit_mlstm_prep(nc, G, UD, NEGM, EMD):
    ph = Phase(nc, "mlp")
    S = SEQ
    li = ph.sbuf("li", [4, S], F32)
    lf = ph.sbuf("lf", [4, S], F32)
    fc = ph.sbuf("fc", [4, S], F32)
    u = ph.sbuf("u", [4, S], F32)
    m0 = ph.sbuf("m0", [4, S], F32)
    m1 = ph.sbuf("m1", [4, S], F32)
    one = ph.sbuf("one", [4, S], F32)
    tot = ph.sbuf("tot", [4, 1], F32)
    Btot = ph.buf("tot")
    Bli, Blf, Bfc, Bu, Bm0, Bm1, Bone = (ph.buf(n) for n in ("li", "lf", "fc", "u", "m0", "m1", "one"))
    ph.op("dve", lambda e: e.memset(one[:], 1.0), writes=[Bone])
    for d in range(2):
        ph.dma("sp", li[:], G[8 * d:8 * d + 4, :], Bli)
        ph.dma("sp", lf[:], G[8 * d + 4:8 * d + 8, :], Blf)
        ph.op("act", lambda e, lf=lf: e.activation(out=lf[:], in_=lf[:], func=AF.Exp, scale=-1.0), [Blf], [Blf])
        ph.op("act", lambda e, lf=lf: e.activation(out=lf[:], in_=lf[:], func=AF.Ln, bias=1.0, scale=1.0), [Blf], [Blf])
        ph.op("dve", lambda e, lf=lf: e.tensor_scalar(out=lf[:], in0=lf[:], scalar1=-1.0, scalar2=None, op0=ALU.mult), [Blf], [Blf])
        ph.op("dve", lambda e, fc=fc, one=one, lf=lf: e.tensor_tensor_scan(out=fc[:], data0=one[:], data1=lf[:], initial=0.0,
                                                                          op0=ALU.mult, op1=ALU.add), [Bone, Blf], [Bfc])
        if d == 1:
            ph.op("dve", lambda e: e.tensor_copy(out=tot[:], in_=fc[:, S - 1:S]), [Bfc], [Btot], tiny=True)
            ph.op("dve", lambda e: e.scalar_tensor_tensor(out=fc[:], in0=fc[:], scalar=-1.0, in1=lf[:], op0=ALU.mult,
                                                          op1=ALU.add), [Bfc, Blf], [Bfc])
            ph.op("dve", lambda e: e.tensor_scalar(out=fc[:], in0=fc[:], scalar1=tot[:, 0:1], scalar2=None, op0=ALU.add),
                  [Bfc, Btot], [Bfc])
        ph.op("dve", lambda e, u=u, li=li, fc=fc: e.tensor_tensor(out=u[:], in0=li[:], in1=fc[:], op=ALU.subtract), [Bli, Bfc], [Bu])
        if d == 0:
            ph.op("dve", lambda e, m0=m0, u=u: e.tensor_tensor_scan(out=m0[:], data0=u[:], data1=u[:], initial=-1e30,
                                                                    op0=ALU.max, op1=ALU.max), [Bu], [Bm0])
            mf, Bmf = m0, Bm0
        else:
            ph.op("dve", lambda e, m0=m0, u=u: e.tensor_copy(out=m0[:], in_=u[:]), [Bu], [Bm0])
            cur, Bcur, nxt, Bnxt = m0, Bm0, m1, Bm1
            k = 1
            while k < S:
                ph.op("dve", lambda e, cur=cur, nxt=nxt, k=k: e.tensor_tensor(out=nxt[:, 0:S - k], in0=cur[:, 0:S - k],
                                                                             in1=cur[:, k:S], op=ALU.max), [Bcur], [Bnxt])
                ph.op("dve", lambda e, cur=cur, nxt=nxt, k=k: e.tensor_copy(out=nxt[:, S - k:S], in_=cur[:, S - k:S]), [Bcur], [Bnxt])
                cur, Bcur, nxt, Bnxt = nxt, Bnxt, cur, Bcur
                k *= 2
            mf, Bmf = cur, Bcur
        ph.dma("sp", UD[4 * d:4 * d + 4, :], u[:], Bu)
        ph.op("dve", lambda e, fc=fc, mf=mf: e.tensor_tensor(out=fc[:], in0=fc[:], in1=mf[:], op=ALU.add), [Bfc, Bmf], [Bfc])
        ph.op("act", lambda e, fc=fc: e.activation(out=fc[:], in_=fc[:], func=AF.Exp, scale=-1.0), [Bfc], [Bfc])
        ph.dma("sp", EMD[4 * d:4 * d + 4, :], fc[:], Bfc)
        ph.op("dve", lambda e, lf=lf, mf=mf: e.tensor_scalar(out=lf[:], in0=mf[:], scalar1=-1.0, scalar2=None, op0=ALU.mult),
              [Bmf, Blf], [Blf])
        ph.dma("sp", NEGM[4 * d:4 * d + 4, :], lf[:], Blf)
    ph.flush()


def emit_mlstm(nc, MQ, MK, MVT, MO, UD, NEGM, EMD, GML, MIX, heads=4):
    ph = Phase(nc, "ml")
    S = SEQ
    NB = 2
    mq = [ph.sbuf(f"mq{i}", [128, S], BF16) for i in range(NB)]
    mk = [ph.sbuf(f"mk{i}", [128, S], BF16) for i in range(NB)]
    mv = [ph.sbuf(f"mv{i}", [128, 32, 256], BF16) for i in range(NB)]
    negm = [ph.sbuf(f"negm{i}", [128, S], F32) for i in range(NB)]
    em = [ph.sbuf(f"em{i}", [128, S], F32) for i in range(NB)]
    ucol = [ph.sbuf(f"ucol{i}", [128, 32], F32) for i in range(NB)]
    hm = ph.sbuf("hm", [128, 2, S], F32)
    ones = ph.sbuf("ones", [128, 128], BF16)
    gml = ph.sbuf("gml", [128, 8], F32)
    r = ph.sbuf("r", [128, TT], F32)
    tmp = ph.sbuf("tmp", [128, TT], F32)
    sq = ph.sbuf("sq", [128, 2, TT], BF16)
    psum = ph.psum("ps", [128, 8, TT], F32)
    Bmq, Bmk, Bmv, Bnegm, Bem, Bucol = ([ph.buf(f"{n}{i}") for i in range(NB)] for n in ("mq", "mk", "mv", "negm", "em", "ucol"))
    Bhm, Bones, Bc, Br, Btmp, Bsq = (ph.buf(n) for n in ("hm", "ones", "consts", "r", "tmp", "sq"))
    sbanks = make_banks(ph, psum, [0, 1, 2, 3])
    NUM = [psum[:, 4, :], psum[:, 5, :]]
    DEN = psum[:, 6, :]
    STB = psum[:, 7, :]
    Bnum = [ph.buf("num0"), ph.buf("num1")]
    Bden, Bstb = ph.buf("den"), ph.buf("stb")
    wst = make_stage(ph, "wt", 3, [128, TT], BF16)
    sst = make_stage(ph, "sc", 3, [128, TT], BF16)
    ast = make_stage(ph, "at", 4, [128, TT], BF16)
    mst = make_stage(ph, "mo", 2, [128, TT], BF16)
    ost = make_stage(ph, "os", 2, [128, TT], BF16)
    MVv = MVT.rearrange("(st p) c -> p st c", p=128)
    ph.dma("sp", gml[:], GML, Bc)
    ph.op("dve", lambda e: e.memset(ones[:], 1.0), writes=[Bones])

    def load_head(h):
        i = h % NB
        ph.dma("sp", mq[i][:], MQ[h], Bmq[i])
        ph.dma("sp", mk[i][:], MK[h], Bmk[i])
        for a_ in range(4):
            ph.dma("sp", mv[i][:, a_ * 8:(a_ + 1) * 8, :], MVv[:, a_ * 8:(a_ + 1) * 8, h * 256:(h + 1) * 256], Bmv[i])

    def load_dir(g):
        h, d = g // 2, g % 2
        i = g % NB
        row = 4 * d + h
        ph.dma("sp", negm[i][:], NEGM[row:row + 1, :].partition_broadcast(128), Bnegm[i])
        ph.dma("sp", em[i][:], EMD[row:row + 1, :].partition_broadcast(128), Bem[i])
        for a_ in range(4):
            ph.dma("sp", ucol[i][:, a_ * 8:(a_ + 1) * 8], UD[row, a_ * 1024:(a_ + 1) * 1024].rearrange("(st p) -> p st", p=128),
                   Bucol[i])

    def head_norm(h):
        for jt in range(NT):
            ts = slice(jt * TT, (jt + 1) * TT)
            ph.op("act", lambda e, ts=ts: e.activation(out=sq[:], in_=hm[:, :, ts], func=AF.Square), [Bhm], [Bsq])
            ph.op("pe", mm_group(STB, [(ones[:], sq[:, c, :]) for c in range(2)]), [Bones, Bsq], [Bstb])
            ph.op("act", lambda e: e.activation(out=r[:], in_=STB, func=AF.Sqrt, bias=EPS, scale=1.0 / 256.0), [Bstb], [Br])
            ph.op("dve", lambda e: e.reciprocal(out=r[:], in_=r[:]), [Br], [Br])
            for c in range(2):
                mo, Bmo = mst.next()
                ph.dma("sp", mo[:], MO[h * 256 + c * 128:h * 256 + (c + 1) * 128, ts], Bmo)
                ph.op("dve", lambda e, c=c, ts=ts: e.scalar_tensor_tensor(out=tmp[:], in0=hm[:, c, ts],
                                                                          scalar=gml[:, h * 2 + c:h * 2 + c + 1], in1=r[:],
                                                                          op0=ALU.mult, op1=ALU.mult), [Bhm, Bc, Br], [Btmp])
                o, Bo = ost.next()
                ph.op("dve", lambda e, o=o, mo=mo: e.tensor_tensor(out=o[:], in0=tmp[:], in1=mo[:], op=ALU.mult), [Btmp, Bmo], [Bo])
                ph.dma("sp", MIX[1024 + h * 256 + c * 128:1024 + h * 256 + (c + 1) * 128, ts], o[:], Bo)

    load_head(0)
    load_dir(0)
    load_dir(1)
    if heads > 1:
        load_head(1)
    ngroups = heads * 2
    steps = []
    for g in range(ngroups):
        h, d = g // 2, g % 2
        for jt in range(NT):
            jss = list(range(0, 4 * jt + 4)) if d == 0 else list(range(4 * jt, 32))
            for n, js in enumerate(jss):
                st = {}
                first, last = (n == 0), (n == len(jss) - 1)

                def A(g=g, h=h, d=d, jt=jt, js=js, st=st):
                    hi, gi = h % NB, g % NB
                    ts = slice(jt * TT, (jt + 1) * TT)
                    ks = slice(js * 128, (js + 1) * 128)
                    sb, Bsb = sbanks.next()
                    ph.op("pe", lambda e: e.matmul(sb, mk[hi][:, ks], mq[hi][:, ts], start=True, stop=True), [Bmk[hi], Bmq[hi]], [Bsb])
                    wt, Bwt = wst.next()
                    ph.op("act", lambda e: e.activation(out=wt[:], in_=negm[gi][:, ts], func=AF.Exp, bias=ucol[gi][:, js:js + 1],
                                                        scale=1.0), [Bnegm[gi], Bucol[gi]], [Bwt])
                    sc, Bsc = sst.next()
                    ph.op("act", lambda e: e.activation(out=sc[:], in_=sb, func=AF.Copy), [Bsb], [Bsc])
                    at, Bat = ast.next()
                    ph.op("dve", lambda e: e.tensor_tensor(out=at[:], in0=sc[:], in1=wt[:], op=ALU.mult), [Bsc, Bwt], [Bat])
                    if 4 * jt <= js <= 4 * jt + 3:
                        if d == 0:
                            pat, base, cm = [[1, TT]], jt * TT - js * 128, -1
                        else:
                            pat, base, cm = [[-1, TT]], js * 128 - jt * TT, 1
                        ph.op("pool", lambda e: e.affine_select(out=at[:], in_=at[:], pattern=pat, compare_op=ALU.is_ge, fill=0.0,
                                                                base=base, channel_multiplier=cm), [Bat], [Bat])
                    st["at"] = (at, Bat)

                def Bf(g=g, h=h, d=d, jt=jt, js=js, st=st, first=first, last=last):
                    hi, gi = h % NB, g % NB
                    ts = slice(jt * TT, (jt + 1) * TT)
                    at, Bat = st["at"]

                    def fn(pe):
                        for c in range(2):
                            pe.matmul(NUM[c], mv[hi][:, js, c * 128:(c + 1) * 128], at[:], start=first, stop=last)
                        return pe.matmul(DEN, ones[:], at[:], start=first, stop=last)
                    ph.op("pe", fn, [Bmv[hi], Bat, Bones], [Bnum[0], Bnum[1], Bden])
                    if not last:
                        return
                    ph.op("act", lambda e: e.activation(out=r[:], in_=DEN, func=AF.Abs), [Bden], [Br])
                    ph.op("dve", lambda e: e.tensor_tensor(out=r[:], in0=r[:], in1=em[gi][:, ts], op=ALU.max), [Br, Bem[gi]], [Br])
                    ph.op("dve", lambda e: e.reciprocal(out=r[:], in_=r[:]), [Br], [Br])
                    for c in range(2):
                        if d == 0:
                            ph.op("dve", lambda e, c=c: e.tensor_tensor(out=hm[:, c, ts], in0=NUM[c], in1=r[:], op=ALU.mult),
                                  [Bnum[c], Br], [Bhm])
                        else:
                            ph.op("dve", lambda e, c=c: e.tensor_tensor(out=tmp[:], in0=NUM[c], in1=r[:], op=ALU.mult),
                                  [Bnum[c], Br], [Btmp])
                            ph.op("dve", lambda e, c=c: e.tensor_tensor(out=hm[:, c, ts], in0=hm[:, c, ts], in1=tmp[:], op=ALU.add),
                                  [Bhm, Btmp], [Bhm])
                    if jt == NT - 1:
                        if g + 2 < ngroups:
                            load_dir(g + 2)
                        if d == 1:
                            head_norm(h)
                            if h + 2 < heads:
                                load_head(h + 2)
                steps.append((A, Bf))
    per_head = {}
    idx = 0
    for g in range(ngroups):
        cnt = sum((4 * jt + 4) if g % 2 == 0 else (32 - 4 * jt) for jt in range(NT))
        per_head.setdefault(g // 2, []).extend(steps[idx:idx + cnt])
        idx += cnt
    for h in range(heads):
        run_pipeline(per_head[h], 2)
    ph.flush()


def tile_cols(w, cols_list):
    w = np.asarray(w, dtype=np.float32)
    K = w.shape[0]
    kc = K // 128
    out = np.zeros((len(cols_list), 128, kc, 128), dtype=np.float32)
    for i, cols in enumerate(cols_list):
        cols = np.asarray(cols)
        out[i, :, :, :len(cols)] = w[:, cols].reshape(kc, 128, len(cols)).transpose(1, 0, 2)
    return out.reshape(len(cols_list), 128, kc * 128)


def tile_wgu(w):
    w = np.asarray(w, dtype=np.float32)
    n = w.shape[1] // 128
    kc = w.shape[0] // 128
    return np.ascontiguousarray(w.reshape(kc, 128, n, 128).transpose(2, 1, 0, 3).reshape(n, 128, kc * 128))


def tile_g(g):
    g = np.asarray(g, dtype=np.float32)
    return np.ascontiguousarray(g.reshape(-1, 128).T)


ROPE_THETA = 10000.0
LAM_INIT1 = 0.8 - 0.6 * float(np.exp(-0.3))


def prep_weights(inp):
    W = {}
    for l in (0, 1):
        for f in (1, 2):
            p = f"l{l}_ffn{f}"
            W[f"{p}_wgu"] = tile_wgu(inp[f"{p}_w_gu"])
            W[f"{p}_wdn"] = tile_wgu(inp[f"{p}_w_down"])
            W[f"{p}_g"] = tile_g(inp[f"{p}_norm"])
    r = np.arange
    cols = [r(i * 128, (i + 1) * 128) for i in range(8)]
    cols.append(np.concatenate([1024 + r(64), 1024 + 32 + r(32), 1024 + r(32)]))
    cols += [1088 + r(i * 128, (i + 1) * 128) for i in range(4)]
    cols += [1600 + r(i * 128, (i + 1) * 128) for i in range(4)]
    cols += [2112 + r(i * 128, (i + 1) * 128) for i in range(8)]
    cols += [3136 + r(i * 128, (i + 1) * 128) for i in range(8)]
    cols.append(4160 + r(16))
    W["l0_win"] = tile_cols(inp["l0_w_in"], cols)
    cq, ck = [], []
    for h in range(8):
        cq.append(h * 192 + r(128))
        cq.append(np.concatenate([h * 192 + 128 + r(64), h * 192 + 128 + 32 + r(32), h * 192 + 128 + r(32)]))
        ck.append(h * 256 + r(128))
        ck.append(h * 256 + 128 + r(128))
    W["l0_wuq"] = tile_cols(inp["l0_w_uq"], cq)
    W["l0_wukv"] = tile_cols(inp["l0_w_ukv"], ck)
    W["l0_gmix"] = tile_g(inp["l0_mix_norm"])
    W["l0_gcq"] = tile_g(inp["l0_g_cq"])
    W["l0_gckv"] = tile_g(inp["l0_g_ckv"])
    W["l0_bg"] = np.ascontiguousarray(np.asarray(inp["l0_b_gates"], np.float32).reshape(16, 1))
    W["l0_gml"] = tile_g(inp["l0_g_mlstm"])
    W["l0_wo"] = tile_wgu(inp["l0_w_o"])
    c1 = []
    for part in range(2):
        for h in range(8):
            for w in range(2):
                c1.append(part * 2048 + h * 256 + w * 128 + r(128))
    c1 += [4096 + r(i * 128, (i + 1) * 128) for i in range(16)]
    W["l1_win"] = tile_cols(inp["l1_w_in"], c1)
    W["l1_gmix"] = tile_g(inp["l1_mix_norm"])
    W["l1_wo"] = tile_wgu(inp["l1_w_o"])
    W["l1_gsub"] = tile_g(inp["l1_g_sub"])
    W["l1_lam"] = np.ascontiguousarray(np.concatenate([np.asarray(inp[k], np.float32) for k in
                                                       ("l1_lam_q1", "l1_lam_k1", "l1_lam_q2", "l1_lam_k2")]).reshape(1, 512))
    W["gfin"] = tile_g(inp["final_norm"])
    invf = (ROPE_THETA ** (-np.arange(0, 64, 2, dtype=np.float32) / 64)).astype(np.float32)
    W["invf"] = np.ascontiguousarray(np.concatenate([invf, invf]).reshape(64, 1))
    return W


def emit_copy(nc, src, dst, tag):
    ph = Phase(nc, tag)
    t = ph.sbuf("t", [128, KC, TT], F32)
    Bt = ph.buf("t")
    sv = src.rearrange("(kc p) s -> p kc s", p=128)
    dv = dst.rearrange("(kc p) s -> p kc s", p=128)
    for i in range(NT):
        ph.dma("sp", t[:], sv[:, :, i * TT:(i + 1) * TT], Bt)
        ph.dma("sp", dv[:, :, i * TT:(i + 1) * TT], t[:], Bt)
    ph.flush()


SCRATCH_EXT = True


def build_program(W, debug=False, upto=99, start=0, dheads=8, dtiles=NT):
    nc = bass.Bass("TRN2", target_bir_lowering=False)
    A = {}

    def din(name, shape, dt=F32):
        A[name] = nc.dram_tensor(name, list(shape), dt, kind="ExternalInput").ap()
        return A[name]

    def scratch(name, shape, dt, dbg=False):
        kind = "ExternalOutput" if (dbg and (debug or SCRATCH_EXT)) else "Internal"
        return nc.dram_tensor(name, list(shape), dt, kind=kind).ap()

    R = din("xT", [D_MODEL, SEQ])
    POS = din("pos", [1, SEQ], I32)
    POSC = din("posc", [128, 32], I32)
    for k, v in W.items():
        din(k, v.shape)
    OUT = nc.dram_tensor("out", [D_MODEL, SEQ], F32, kind="ExternalOutput").ap()
    COS = scratch("COS", [64, SEQ], F32)
    SIN = scratch("SIN", [64, SEQ], F32)
    QN = scratch("QN", [8, 128, SEQ], BF16)
    QR = scratch("QR", [8, 64, SEQ], BF16)
    KN = scratch("KN", [8, 128, SEQ], BF16)
    KR = scratch("KR", [64, SEQ], BF16)
    VT = scratch("VT", [SEQ, 1024], BF16)
    MQ = scratch("MQ", [4, 128, SEQ], BF16)
    MK = scratch("MK", [4, 128, SEQ], BF16)
    MVT = scratch("MVT", [SEQ, 1024], BF16)
    MO = scratch("MO", [1024, SEQ], BF16)
    G = scratch("G", [16, SEQ], F32)
    UD = scratch("UD", [8, SEQ], F32)
    NEGM = scratch("NEGM", [8, SEQ], F32)
    EMD = scratch("EMD", [8, SEQ], F32)
    MIX = scratch("MIX", [2048, SEQ], BF16, True)
    Q12 = scratch("Q12", [16, 128, SEQ], BF16, True)
    K12 = scratch("K12", [16, 128, SEQ], BF16, True)
    VT1 = scratch("VT1", [SEQ, 2048], BF16, True)
    OD = scratch("OD", [2048, SEQ], BF16, True)
    steps = [
        lambda: emit_tables(nc, POS, A["invf"], COS, SIN),
        lambda: emit_ffn(nc, R, A["l0_ffn1_wgu"], A["l0_ffn1_wdn"], A["l0_ffn1_g"], "f01"),
        lambda: emit_inproj0(nc, R, A["l0_gmix"], A["l0_win"], A["l0_gcq"], A["l0_gckv"], A["l0_wuq"], A["l0_wukv"],
                             A["l0_bg"], COS, SIN, QN, QR, KN, KR, VT, MQ, MK, MVT, MO, G),
        lambda: emit_mla(nc, QN, QR, KN, KR, VT, MIX),
        lambda: emit_mlstm_prep(nc, G, UD, NEGM, EMD),
        lambda: emit_mlstm(nc, MQ, MK, MVT, MO, UD, NEGM, EMD, A["l0_gml"], MIX),
        lambda: emit_oproj(nc, R, MIX, A["l0_wo"], "op0"),
        lambda: emit_copy(nc, R, scratch("X2", [D_MODEL, SEQ], F32, True), "cx2") if debug else None,
        lambda: emit_ffn(nc, R, A["l0_ffn2_wgu"], A["l0_ffn2_wdn"], A["l0_ffn2_g"], "f02"),
        lambda: emit_ffn(nc, R, A["l1_ffn1_wgu"], A["l1_ffn1_wdn"], A["l1_ffn1_g"], "f11"),
        lambda: emit_copy(nc, R, scratch("X4", [D_MODEL, SEQ], F32, True), "cx4") if debug else None,
        lambda: emit_inproj1(nc, R, A["l1_gmix"], A["l1_win"], Q12, K12, VT1),
        lambda: emit_diff(nc, Q12, K12, VT1, POS, POSC, A["l1_lam"], A["l1_gsub"], OD, heads=dheads, qtiles=dtiles,
                          DBG=(scratch("DLAM", [128, 4], F32, True), scratch("DDALL", [128, 2, TT], F32, True)) if debug else None),
        lambda: emit_oproj(nc, R, OD, A["l1_wo"], "op1"),
        lambda: emit_ffn(nc, R, A["l1_ffn2_wgu"], A["l1_ffn2_wdn"], A["l1_ffn2_g"], "f12"),
        lambda: emit_final(nc, R, A["gfin"], OUT),
    ]
    for i, st in enumerate(steps):
        if start <= i <= upto:
            st()
    return nc


def core_inputs(inp, W, b):
    pos = np.ascontiguousarray(np.asarray(inp["positions"][b], np.int32).reshape(1, SEQ))
    m = {"xT": np.ascontiguousarray(np.asarray(inp["x"][b], np.float32).T), "pos": pos,
         "posc": np.ascontiguousarray(pos.reshape(32, 128).T)}
    m.update(W)
    return m


def kernel(**inputs):
    W = prep_weights(inputs)
    nc = build_program(W)
    in_maps = [core_inputs(inputs, W, b) for b in range(NCORES)]
    res = run_bass_kernel_spmd(nc, in_maps, core_ids=list(range(NCORES)))
    out = np.stack([np.asarray(res.results[b]["out"], np.float32).T for b in range(NCORES)], axis=0)
    return np.ascontiguousarray(out)
```

```python
import contextlib
import numpy as np
import concourse.bass as bass
import concourse.mybir as mybir
from concourse.bass_utils import run_bass_kernel_spmd

F32 = mybir.dt.float32
BF16 = mybir.dt.bfloat16
I32 = mybir.dt.int32
AF = mybir.ActivationFunctionType
ALU = mybir.AluOpType

D_MODEL = 2048
SEQ = 4096
D_FF = 5632
EPS = 1e-6
NCORES = 8
TT = 512
NT = SEQ // TT
KC = D_MODEL // 128
FC = D_FF // 128


class Buf:
    def __init__(self, name):
        self.name = name
        self.last_w = None
        self.readers = []
        self.sem = None
        self.sem_n = 0


class Op:
    __slots__ = ("eng", "fn", "reads", "writes", "dma", "deps", "needs_inc", "ev", "idx", "tiny")

    def __init__(self, eng, fn, reads, writes, dma, tiny=False):
        self.eng, self.fn, self.reads, self.writes, self.dma = eng, fn, reads, writes, dma
        self.tiny = tiny
        self.deps = []
        self.needs_inc = False
        self.ev = None


class Phase:
    ENGS = ("pe", "act", "dve", "pool", "sp")

    def __init__(self, nc, tag):
        self.nc = nc
        self.tag = tag
        self.ops = []
        self.eng = {"pe": nc.tensor, "act": nc.scalar, "dve": nc.vector, "pool": nc.gpsimd, "sp": nc.sync}
        self.snap = nc.snapshot_sems()
        self.es = contextlib.ExitStack()
        self.bufs = []

    def sbuf(self, name, shape, dt):
        return self.es.enter_context(self.nc.sbuf_tensor(f"{self.tag}_{name}", list(shape), dt))

    def psum(self, name, shape, dt=F32):
        return self.es.enter_context(self.nc.psum_tensor(f"{self.tag}_{name}", list(shape), dt))

    def buf(self, name):
        b = Buf(name)
        self.bufs.append(b)
        return b

    def op(self, eng, fn, reads=(), writes=(), tiny=False):
        o = Op(eng, fn, tuple(reads), tuple(writes), False, tiny)
        self.ops.append(o)
        return o

    def dma(self, queue, out, in_, sb):
        sb = tuple(sb) if isinstance(sb, (tuple, list)) else (sb,)
        o = Op(queue, lambda e: e.dma_start(out=out, in_=in_), (), sb, True)
        self.ops.append(o)
        return o

    def flush(self):
        nc = self.nc
        for o in self.ops:
            deps = []
            for b in o.reads:
                if b.last_w is not None:
                    deps.append(b.last_w)
            for b in o.writes:
                if b.last_w is not None:
                    deps.append(b.last_w)
                deps.extend(b.readers)
            seen = set()
            for p in deps:
                if p is o or id(p) in seen:
                    continue
                seen.add(id(p))
                if (not p.dma) and (not o.dma) and p.eng == o.eng and not (p.tiny or o.tiny):
                    continue
                if (not p.dma) and o.dma and p.eng == o.eng:
                    continue
                o.deps.append(p)
                p.needs_inc = True
            for b in o.reads:
                if not o.dma:
                    b.readers = [r for r in b.readers if r.dma or r.eng != o.eng]
                b.readers.append(o)
            for b in o.writes:
                b.last_w = o
                b.readers = []
        last = {}
        for o in self.ops:
            if not o.dma:
                last[o.eng] = o
        for o in last.values():
            o.needs_inc = True
        esem = {}
        ecnt = {}
        for e in self.ENGS:
            esem[e] = nc.alloc_semaphore(f"{self.tag}_t_{e}")
            ecnt[e] = 0
        waited = {e: {} for e in self.ENGS}
        all_ev = {}

        def do_wait(e, ev):
            h, v, key = ev
            if waited[e].get(key, 0) >= v:
                return
            waited[e][key] = v
            self.eng[e].wait_ge(h, v)

        for o in self.ops:
            e = o.eng
            for p in o.deps:
                do_wait(e, p.ev)
            with nc.allow_non_contiguous_dma(reason="small strided side loads"):
                ins = o.fn(self.eng[e])
            if o.dma:
                b = o.writes[0]
                if b.sem is None:
                    b.sem = nc.alloc_semaphore(f"{self.tag}_d_{b.name}")
                b.sem_n += 16
                assert b.sem_n < 60000, b.name
                ins.then_inc(b.sem, 16)
                o.ev = (b.sem, b.sem_n, "d_" + b.name)
                all_ev[o.ev[2]] = o.ev
            elif o.needs_inc:
                ecnt[e] += 1
                assert ecnt[e] < 60000, (self.tag, e)
                ins.then_inc(esem[e], 1)
                o.ev = (esem[e], ecnt[e], "t_" + e)
                all_ev[o.ev[2]] = o.ev
        for ev in all_ev.values():
            do_wait("sp", ev)
        nc.all_engine_barrier()
        nc.clear_and_free_semaphores(nc.allocated_since(self.snap))
        nc.all_engine_barrier()
        self.es.close()
        self.ops = []


def mm_group(out, pairs):
    def fn(pe):
        n = len(pairs)
        for i, (l, r) in enumerate(pairs):
            ins = pe.matmul(out, l, r, start=(i == 0), stop=(i == n - 1))
        return ins
    return fn


def emit_ffn(nc, R, wgu, wdn, gn, tag, tiles=NT):
    ph = Phase(nc, tag)
    NGU, NDN = 3, 2
    xb = [ph.sbuf(f"x{i}", [128, KC, TT], F32) for i in range(2)]
    xsq = ph.sbuf("xsq", [128, KC, TT], BF16)
    xn = ph.sbuf("xn", [128, KC, TT], BF16)
    hb = ph.sbuf("h", [128, FC, TT], BF16)
    wg = [ph.sbuf(f"wg{i}", [128, 2, KC * 128], BF16) for i in range(NGU)]
    wd = [ph.sbuf(f"wd{i}", [128, FC * 128], BF16) for i in range(NDN)]
    stmp = [ph.sbuf(f"st{i}", [128, TT], F32) for i in range(2)]
    rstd = ph.sbuf("rstd", [128, TT], F32)
    ones = ph.sbuf("ones", [128, 128], BF16)
    gsb = ph.sbuf("g", [128, KC], F32)
    psum = ph.psum("ps", [128, 8, TT], F32)
    Bx = [ph.buf(f"x{i}") for i in range(2)]
    Bxsq, Bxn, Bh, Brstd, Bones, Bg = (ph.buf(n) for n in ("xsq", "xn", "h", "rstd", "ones", "g"))
    Bwg = [ph.buf(f"wg{i}") for i in range(NGU)]
    Bwd = [ph.buf(f"wd{i}") for i in range(NDN)]
    Bst = [ph.buf(f"st{i}") for i in range(2)]
    Bps = [ph.buf(f"ps{i}") for i in range(8)]
    Rv = R.rearrange("(kc p) s -> p kc s", p=128)

    ph.dma("sp", gsb[:], gn, Bg)
    ph.op("dve", lambda e: e.memset(ones[:], 1.0), writes=[Bones])
    ph.dma("sp", xb[0][:], Rv[:, :, 0:TT], Bx[0])
    gi = 0
    di = 0
    for t in range(tiles):
        x, BX = xb[t % 2], Bx[t % 2]
        if t + 1 < tiles:
            ph.dma("sp", xb[(t + 1) % 2][:], Rv[:, :, (t + 1) * TT:(t + 2) * TT], Bx[(t + 1) % 2])
        ph.op("act", lambda e, x=x: e.activation(out=xsq[:], in_=x[:], func=AF.Square), [BX], [Bxsq])
        ph.op("pe", mm_group(psum[:, 0, :], [(ones[:], xsq[:, kc, :]) for kc in range(KC)]), [Bones, Bxsq], [Bps[0]])
        ph.op("act", lambda e: e.activation(out=rstd[:], in_=psum[:, 0, :], func=AF.Sqrt, bias=EPS,
                                            scale=1.0 / D_MODEL), [Bps[0]], [Brstd])
        ph.op("dve", lambda e: e.reciprocal(out=rstd[:], in_=rstd[:]), [Brstd], [Brstd])

        def xn_fn(e, x=x):
            for kc in range(KC):
                ins = e.scalar_tensor_tensor(out=xn[:, kc, :], in0=x[:, kc, :], scalar=gsb[:, kc:kc + 1],
                                             in1=rstd[:], op0=ALU.mult, op1=ALU.mult)
            return ins
        ph.op("dve", xn_fn, [BX, Bg, Brstd], [Bxn])
        for j in range(FC):
            s = gi % NGU
            ph.dma("pool", wg[s][:, 0, :], wgu[j], Bwg[s])
            ph.dma("pool", wg[s][:, 1, :], wgu[FC + j], Bwg[s])
            bk = 2 * (gi % 3)
            for half in range(2):
                ph.op("pe", mm_group(psum[:, bk + half, :],
                                     [(wg[s][:, half, kc * 128:(kc + 1) * 128], xn[:, kc, :]) for kc in range(KC)]),
                      [Bwg[s], Bxn], [Bps[bk + half]])
            st, BST = stmp[gi % 2], Bst[gi % 2]
            ph.op("act", lambda e, st=st, bk=bk: e.activation(out=st[:], in_=psum[:, bk, :], func=AF.Silu),
                  [Bps[bk]], [BST])
            ph.op("dve", lambda e, st=st, bk=bk, j=j: e.tensor_tensor(out=hb[:, j, :], in0=st[:],
                                                                       in1=psum[:, bk + 1, :], op=ALU.mult),
                  [BST, Bps[bk + 1]], [Bh])
            gi += 1
        for m in range(KC):
            s = di % NDN
            ph.dma("pool", wd[s][:], wdn[m], Bwd[s])
            bk = 6 + di % 2
            ph.op("pe", mm_group(psum[:, bk, :], [(wd[s][:, kc * 128:(kc + 1) * 128], hb[:, kc, :]) for kc in range(FC)]),
                  [Bwd[s], Bh], [Bps[bk]])
            ph.op("dve", lambda e, x=x, bk=bk, m=m: e.scalar_tensor_tensor(
                out=x[:, m, :], in0=psum[:, bk, :], scalar=0.5, in1=x[:, m, :], op0=ALU.mult, op1=ALU.add),
                [Bps[bk]], [BX])
            di += 1
        ph.dma("sp", Rv[:, :, t * TT:(t + 1) * TT], x[:], BX)
    ph.flush()


class WS:
    def __init__(self, ph, name, nslots, width, queue="pool"):
        self.t = [ph.sbuf(f"{name}{i}", [128, width], BF16) for i in range(nslots)]
        self.b = [ph.buf(f"{name}{i}") for i in range(nslots)]
        self.i = 0
        self.ph = ph
        self.q = queue

    def load(self, src):
        k = self.i % len(self.t)
        self.i += 1
        self.ph.dma(self.q, self.t[k][:], src, self.b[k])
        return self.t[k], self.b[k]


class Rot:
    def __init__(self, items):
        self.items = items
        self.i = 0

    def next(self):
        it = self.items[self.i % len(self.items)]
        self.i += 1
        return it


def make_banks(ph, psum, ids):
    return Rot([(psum[:, k, :], ph.buf(f"bank{k}")) for k in ids])


def make_stage(ph, name, n, shape, dt):
    return Rot([(ph.sbuf(f"{name}{i}", shape, dt), ph.buf(f"{name}{i}")) for i in range(n)])


def rms_norm(ph, x, Bx, koff, nk, g, Bg, out, Bout, ooff, sq, Bsq, ones, Bones, bank, Bbank, rstd, Brstd, post=1.0):
    D = nk * 128
    ph.op("act", lambda e: e.activation(out=sq[:, 0:nk, :], in_=x[:, koff:koff + nk, :], func=AF.Square), [Bx], [Bsq])
    ph.op("pe", mm_group(bank, [(ones[:], sq[:, kc, :]) for kc in range(nk)]), [Bones, Bsq], [Bbank])
    ph.op("act", lambda e: e.activation(out=rstd[:], in_=bank, func=AF.Sqrt, bias=EPS * post ** -2,
                                        scale=1.0 / (D * post * post)), [Bbank], [Brstd])
    ph.op("dve", lambda e: e.reciprocal(out=rstd[:], in_=rstd[:]), [Brstd], [Brstd])

    def fn(e):
        for kc in range(nk):
            ins = e.scalar_tensor_tensor(out=out[:, ooff + kc, :], in0=x[:, koff + kc, :], scalar=g[:, kc:kc + 1],
                                         in1=rstd[:], op0=ALU.mult, op1=ALU.mult)
        return ins
    ph.op("dve", fn, [Bx, Bg, Brstd], [Bout])


def emit_tables(nc, POS, INVF, COS, SIN):
    ph = Phase(nc, "tb")
    posi = ph.sbuf("posi", [64, SEQ], I32)
    ang = ph.sbuf("ang", [64, SEQ], F32)
    r1 = ph.sbuf("r1", [64, SEQ], F32)
    r2 = ph.sbuf("r2", [64, SEQ], F32)
    invf = ph.sbuf("invf", [64, 1], F32)
    Bp, Ba, B1, B2, Bi = (ph.buf(n) for n in ("posi", "ang", "r1", "r2", "invf"))
    TWO_PI = 2.0 * np.pi
    SH = 1.0 - 1e-6
    C1 = 6.28125
    C2 = float(np.float32(TWO_PI - C1))
    ki = ph.sbuf("ki", [64, SEQ], I32)
    Bk = ph.buf("ki")
    ph.dma("sp", posi[:], POS.partition_broadcast(64), Bp)
    ph.dma("sp", invf[:], INVF, Bi)
    ph.op("dve", lambda e: e.tensor_copy(out=ang[:], in_=posi[:]), [Bp], [Ba])
    ph.op("dve", lambda e: e.tensor_scalar(out=ang[:], in0=ang[:], scalar1=invf[:, 0:1], scalar2=None, op0=ALU.mult),
          [Ba, Bi], [Ba])
    ph.op("dve", lambda e: e.tensor_scalar(out=r1[:], in0=ang[:], scalar1=1.0 / TWO_PI, scalar2=None, op0=ALU.mult),
          [Ba], [B1])
    ph.op("dve", lambda e: e.tensor_copy(out=ki[:], in_=r1[:]), [B1], [Bk])
    ph.op("dve", lambda e: e.tensor_copy(out=r1[:], in_=ki[:]), [Bk], [B1])
    ph.op("dve", lambda e: e.scalar_tensor_tensor(out=ang[:], in0=r1[:], scalar=-C1, in1=ang[:], op0=ALU.mult,
                                                  op1=ALU.add), [B1, Ba], [Ba])
    ph.op("dve", lambda e: e.scalar_tensor_tensor(out=ang[:], in0=r1[:], scalar=-C2, in1=ang[:], op0=ALU.mult,
                                                  op1=ALU.add), [B1, Ba], [Ba])

    def fold(t, Bt):
        ph.op("dve", lambda e: e.tensor_scalar(out=r1[:], in0=t[:], scalar1=np.pi, scalar2=TWO_PI, op0=ALU.is_gt,
                                               op1=ALU.mult), [Bt], [B1])
        ph.op("dve", lambda e: e.tensor_tensor(out=t[:], in0=t[:], in1=r1[:], op=ALU.subtract), [Bt, B1], [Bt])
        ph.op("dve", lambda e: e.tensor_scalar(out=r1[:], in0=t[:], scalar1=-np.pi, scalar2=TWO_PI, op0=ALU.is_lt,
                                               op1=ALU.mult), [Bt], [B1])
        ph.op("dve", lambda e: e.tensor_tensor(out=t[:], in0=t[:], in1=r1[:], op=ALU.add), [Bt, B1], [Bt])
    fold(ang, Ba)
    ph.op("dve", lambda e: e.tensor_scalar(out=r2[:], in0=ang[:], scalar1=np.pi / 2, scalar2=None, op0=ALU.add),
          [Ba], [B2])
    fold(r2, B2)
    sn = ph.sbuf("sn", [64, SEQ], F32)
    Bs = ph.buf("sn")
    ph.op("act", lambda e: e.activation(out=sn[0:32, :], in_=ang[0:32, :], func=AF.Sin, scale=-SH), [Ba], [Bs])
    ph.op("act", lambda e: e.activation(out=sn[32:64, :], in_=ang[32:64, :], func=AF.Sin, scale=SH), [Ba], [Bs])
    ph.dma("sp", SIN, sn[:], Bs)
    ph.op("act", lambda e: e.activation(out=r1[:], in_=r2[:], func=AF.Sin, scale=SH), [B2, B1], [B1])
    ph.dma("sp", COS, r1[:], B1)
    ph.flush()


def run_pipeline(steps, LA):
    n = len(steps)
    for i in range(n + LA):
        if i < n:
            steps[i][0]()
        if i - LA >= 0:
            steps[i - LA][1]()


def rope_combine(ph, bA, BA, bB, BB, cos_t, sin_t, Btab, f32st, bfst, scale, dst):
    s1, B1 = f32st.next()
    s2, B2 = f32st.next()
    o, Bo = bfst.next()
    ph.op("dve", lambda e: e.tensor_tensor(out=s1[0:64, :], in0=bA, in1=cos_t, op=ALU.mult), [BA, Btab], [B1])
    ph.op("dve", lambda e: e.tensor_tensor(out=s2[0:64, :], in0=bB, in1=sin_t, op=ALU.mult), [BB, Btab], [B2])
    ph.op("dve", lambda e: e.tensor_tensor(out=s1[0:64, :], in0=s1[0:64, :], in1=s2[0:64, :], op=ALU.add), [B1, B2], [B1])
    ph.op("act", lambda e: e.activation(out=o[0:64, :], in_=s1[0:64, :], func=AF.Copy, scale=scale), [B1], [Bo])
    ph.dma("sp", dst, o[0:64, :], Bo)


def emit_inproj0(nc, R, GMIX, W0, GCQ, GCKV, WUQ, WUKV, BG, COS, SIN, QN, QR, KN, KR, VT, MQ, MK, MVT, MO, G, tiles=NT):
    ph = Phase(nc, "ip0")
    x = ph.sbuf("x", [128, KC, TT], F32)
    sq = ph.sbuf("sq", [128, KC, TT], BF16)
    hn = ph.sbuf("hn", [128, KC, TT], BF16)
    cl = ph.sbuf("cl", [128, 8, TT], F32)
    cn = ph.sbuf("cn", [128, 8, TT], BF16)
    wuq = ph.sbuf("wuq", [128, 16, 512], BF16)
    wukv = ph.sbuf("wukv", [128, 16, 512], BF16)
    cos = ph.sbuf("cos", [64, SEQ], F32)
    sin = ph.sbuf("sin", [64, SEQ], F32)
    rstd = ph.sbuf("rstd", [128, TT], F32)
    ones = ph.sbuf("ones", [128, 128], BF16)
    gmix = ph.sbuf("gmix", [128, KC], F32)
    gcq = ph.sbuf("gcq", [128, 4], F32)
    gckv = ph.sbuf("gckv", [128, 4], F32)
    bg = ph.sbuf("bg", [16, 1], F32)
    psum = ph.psum("ps", [128, 8, TT], F32)
    Bx, Bsq, Bhn, Bcl, Bcn, Bwuq, Bwukv, Btab, Brstd, Bones, Bc = (ph.buf(n) for n in (
        "x", "sq", "hn", "cl", "cn", "wuq", "wukv", "tab", "rstd", "ones", "consts"))
    ws = WS(ph, "w", 6, KC * 128)
    banks = make_banks(ph, psum, list(range(8)))
    f32st = make_stage(ph, "fs", 4, [128, TT], F32)
    bfst = make_stage(ph, "bs", 6, [128, TT], BF16)
    Rv = R.rearrange("(kc p) s -> p kc s", p=128)

    ph.dma("sp", gmix[:], GMIX, Bc)
    ph.dma("sp", gcq[:], GCQ, Bc)
    ph.dma("sp", gckv[:], GCKV, Bc)
    ph.dma("sp", bg[:], BG, Bc)
    ph.dma("sp", cos[:], COS, Btab)
    ph.dma("sp", sin[:], SIN, Btab)
    for i in range(16):
        ph.dma("pool", wuq[:, i, :], WUQ[i], Bwuq)
        ph.dma("pool", wukv[:, i, :], WUKV[i], Bwukv)
    ph.op("dve", lambda e: e.memset(ones[:], 1.0), writes=[Bones])
    ph.dma("sp", x[:], Rv[:, :, 0:TT], Bx)

    def fm(w, Bw, c0, c1, a, Ba, aoff, nk, wstride=128):
        bank, Bb = banks.next()
        M = c1 - c0
        ph.op("pe", mm_group(bank[0:M, :], [(w[:, kc * wstride + c0:kc * wstride + c1], a[:, aoff + kc, :])
                                            for kc in range(nk)]), [Bw, Ba], [Bb])
        return bank, Bb

    def evac_store(bank, Bb, M, func, scale, dst, bias=None, f32=False):
        st, Bs = (f32st if f32 else bfst).next()
        if bias is None:
            ph.op("act", lambda e: e.activation(out=st[0:M, :], in_=bank[0:M, :], func=func, scale=scale), [Bb], [Bs])
        else:
            ph.op("act", lambda e: e.activation(out=st[0:M, :], in_=bank[0:M, :], func=func, scale=scale, bias=bias),
                  [Bb, Bc], [Bs])
        ph.dma("sp", dst, st[0:M, :], Bs)

    def tm_group(wlist, a, Ba, aoff, nk, tb, wstride, coff):
        bank, Bb = banks.next()

        def fn(pe):
            for c, (w, _) in enumerate(wlist):
                for kc in range(nk):
                    ins = pe.matmul(bank[:, c * 128:(c + 1) * 128], a[:, aoff + kc, tb * 128:(tb + 1) * 128],
                                    w[:, kc * wstride + coff:kc * wstride + coff + 128], start=(kc == 0), stop=(kc == nk - 1))
            return ins
        ph.op("pe", fn, [Ba] + [b_ for (_, b_) in wlist], [Bb])
        return bank, Bb

    for t in range(tiles):
        ts = slice(t * TT, (t + 1) * TT)
        bank, Bb = banks.next()
        rms_norm(ph, x, Bx, 0, KC, gmix, Bc, hn, Bhn, 0, sq, Bsq, ones, Bones, bank, Bb, rstd, Brstd)
        if t + 1 < tiles:
            ph.dma("sp", x[:], Rv[:, :, (t + 1) * TT:(t + 2) * TT], Bx)
        for i in range(8):
            w, Bw = ws.load(W0[i])
            bank, Bb = fm(w, Bw, 0, 128, hn, Bhn, 0, KC)
            ph.op("act", lambda e, i=i, bank=bank: e.activation(out=cl[:, i, :], in_=bank, func=AF.Copy), [Bb], [Bcl])
        w, Bw = ws.load(W0[8])
        bA, BA = fm(w, Bw, 0, 64, hn, Bhn, 0, KC)
        bB, BB = fm(w, Bw, 64, 128, hn, Bhn, 0, KC)
        rope_combine(ph, bA[0:64, :], BA, bB[0:64, :], BB, cos[:, ts], sin[:, ts], Btab, f32st, bfst, 1.0, KR[:, ts])
        for h in range(4):
            w, Bw = ws.load(W0[9 + h])
            bank, Bb = fm(w, Bw, 0, 128, hn, Bhn, 0, KC)
            evac_store(bank, Bb, 128, AF.Copy, 1.0, MQ[h][:, ts])
        for h in range(4):
            w, Bw = ws.load(W0[13 + h])
            bank, Bb = fm(w, Bw, 0, 128, hn, Bhn, 0, KC)
            evac_store(bank, Bb, 128, AF.Copy, 128.0 ** -0.5, MK[h][:, ts])
        for cg in range(2):
            wl = [ws.load(W0[17 + cg * 4 + c]) for c in range(4)]
            for tb in range(4):
                bank, Bb = tm_group(wl, hn, Bhn, 0, KC, tb, 128, 0)
                evac_store(bank, Bb, 128, AF.Copy, 1.0, MVT[t * TT + tb * 128:t * TT + (tb + 1) * 128, cg * 512:(cg + 1) * 512])
        for i in range(8):
            w, Bw = ws.load(W0[25 + i])
            bank, Bb = fm(w, Bw, 0, 128, hn, Bhn, 0, KC)
            evac_store(bank, Bb, 128, AF.Sigmoid, 1.0, MO[i * 128:(i + 1) * 128, ts])
        w, Bw = ws.load(W0[33])
        bank, Bb = fm(w, Bw, 0, 16, hn, Bhn, 0, KC)
        evac_store(bank, Bb, 16, AF.Identity, 1.0, G[:, ts], bias=bg[:, 0:1], f32=True)
        bank, Bb = banks.next()
        rms_norm(ph, cl, Bcl, 0, 4, gcq, Bc, cn, Bcn, 0, sq, Bsq, ones, Bones, bank, Bb, rstd, Brstd)
        bank, Bb = banks.next()
        rms_norm(ph, cl, Bcl, 4, 4, gckv, Bc, cn, Bcn, 4, sq, Bsq, ones, Bones, bank, Bb, rstd, Brstd)
        qs = 192.0 ** -0.5
        for h in range(8):
            wq = wuq[:, 2 * h, :]
            bank, Bb = fm(wq, Bwuq, 0, 128, cn, Bcn, 0, 4)
            evac_store(bank, Bb, 128, AF.Copy, qs, QN[h][:, ts])
            wr = wuq[:, 2 * h + 1, :]
            bA, BA = fm(wr, Bwuq, 0, 64, cn, Bcn, 0, 4)
            bB, BB = fm(wr, Bwuq, 64, 128, cn, Bcn, 0, 4)
            rope_combine(ph, bA[0:64, :], BA, bB[0:64, :], BB, cos[:, ts], sin[:, ts], Btab, f32st, bfst, qs, QR[h][:, ts])
        for h in range(8):
            wk = wukv[:, 2 * h, :]
            bank, Bb = fm(wk, Bwukv, 0, 128, cn, Bcn, 4, 4)
            evac_store(bank, Bb, 128, AF.Copy, 1.0, KN[h][:, ts])
        for hg in range(2):
            wl = [(wukv[:, 2 * (hg * 4 + c) + 1, :], Bwukv) for c in range(4)]
            for tb in range(4):
                bank, Bb = tm_group(wl, cn, Bcn, 4, 4, tb, 128, 0)
                evac_store(bank, Bb, 128, AF.Copy, 1.0, VT[t * TT + tb * 128:t * TT + (tb + 1) * 128, hg * 512:(hg + 1) * 512])
    ph.flush()


def emit_mla(nc, QN, QR, KN, KR, VT, MIX, heads=8, qtiles=NT):
    ph = Phase(nc, "mla")
    NB = 2
    qn = [ph.sbuf(f"qn{i}", [128, SEQ], BF16) for i in range(NB)]
    qr = [ph.sbuf(f"qr{i}", [128, SEQ], BF16) for i in range(NB)]
    kn = [ph.sbuf(f"kn{i}", [128, SEQ], BF16) for i in range(NB)]
    vv = [ph.sbuf(f"v{i}", [128, 32, 128], BF16) for i in range(NB)]
    kr = ph.sbuf("kr", [128, SEQ], BF16)
    ones = ph.sbuf("ones", [128, 128], BF16)
    rden = ph.sbuf("rden", [128, TT], F32)
    psum = ph.psum("ps", [128, 8, TT], F32)
    Bqn, Bqr, Bkn, Bv = ([ph.buf(f"{n}{i}") for i in range(NB)] for n in ("qn", "qr", "kn", "v"))
    Bkr, Bones, Brden = ph.buf("kr"), ph.buf("ones"), ph.buf("rden")
    sbanks = make_banks(ph, psum, [0, 1, 2, 3])
    obanks = Rot([((psum[:, 4 + 2 * i, :], ph.buf(f"ob{i}")), (psum[:, 5 + 2 * i, :], ph.buf(f"db{i}"))) for i in range(2)])
    pst = make_stage(ph, "pt", 4, [128, TT], BF16)
    ost = make_stage(ph, "os", 2, [128, TT], BF16)
    acc = [ph.sbuf(f"acc{i}", [128, TT], F32) for i in range(2)]
    Bacc = [ph.buf("acc0"), ph.buf("acc1")]
    ahi = ph.sbuf("ahi", [128, TT], BF16)
    alo = ph.sbuf("alo", [128, TT], BF16)
    Bahi, Balo = ph.buf("ahi"), ph.buf("alo")
    VTv = VT.rearrange("(st p) c -> p st c", p=128)
    ph.op("pool", lambda e: e.memset(kr[64:128, :], 0.0), writes=[Bkr])
    for i in range(NB):
        ph.op("pool", lambda e, i=i: e.memset(qr[i][64:128, :], 0.0), writes=[Bqr[i]])
    ph.dma("sp", kr[0:64, :], KR, Bkr)
    ph.op("dve", lambda e: e.memset(ones[:], 1.0), writes=[Bones])

    def load(h):
        i = h % NB
        ph.dma("sp", qn[i][:], QN[h], Bqn[i])
        ph.dma("sp", qr[i][0:64, :], QR[h], Bqr[i])
        ph.dma("sp", kn[i][:], KN[h], Bkn[i])
        for a in range(4):
            ph.dma("sp", vv[i][:, a * 8:(a + 1) * 8, :], VTv[:, a * 8:(a + 1) * 8, h * 128:(h + 1) * 128], Bv[i])
    load(0)
    if heads > 1:
        load(1)
    steps = []
    for h in range(heads):
        for jq in range(qtiles):
            grp = {}
            for jk in range(32):
                st = {}

                def A(h=h, jq=jq, jk=jk, st=st):
                    i = h % NB
                    qs = slice(jq * TT, (jq + 1) * TT)
                    ks = slice(jk * 128, (jk + 1) * 128)
                    sb, Bsb = sbanks.next()
                    ph.op("pe", mm_group(sb, [(kn[i][:, ks], qn[i][:, qs]), (kr[:, ks], qr[i][:, qs])]),
                          [Bkn[i], Bqn[i], Bkr, Bqr[i]], [Bsb])
                    pt, Bpt = pst.next()
                    ph.op("act", lambda e, pt=pt, sb=sb: e.activation(out=pt[:], in_=sb, func=AF.Exp), [Bsb], [Bpt])
                    gi = (h * qtiles + jq) % 2
                    ac, Bac = acc[gi], Bacc[gi]
                    if jk == 0:
                        ph.op("dve", lambda e: e.tensor_copy(out=ac[:], in_=pt[:]), [Bpt], [Bac])
                    else:
                        ph.op("dve", lambda e: e.tensor_tensor(out=ac[:], in0=ac[:], in1=pt[:], op=ALU.add), [Bpt, Bac], [Bac])
                    st["pt"] = (pt, Bpt)

                def Bf(h=h, jq=jq, jk=jk, st=st, grp=grp):
                    i = h % NB
                    qs = slice(jq * TT, (jq + 1) * TT)
                    if jk == 0:
                        grp["acc"] = obanks.next()
                    (ob, Bob), (db, Bdb) = grp["acc"]
                    pt, Bpt = st["pt"]

                    ph.op("pe", lambda pe: pe.matmul(ob, vv[i][:, jk, :], pt[:], start=(jk == 0), stop=(jk == 31)), [Bv[i], Bpt], [Bob])
                    if jk == 31:
                        gi = (h * qtiles + jq) % 2
                        ac, Bac = acc[gi], Bacc[gi]
                        ph.op("dve", lambda e: e.tensor_copy(out=ahi[:], in_=ac[:]), [Bac], [Bahi])
                        ph.op("dve", lambda e: e.tensor_tensor(out=alo[:], in0=ac[:], in1=ahi[:], op=ALU.subtract), [Bac, Bahi], [Balo])
                        ph.op("pe", mm_group(db, [(ones[:], ahi[:]), (ones[:], alo[:])]), [Bones, Bahi, Balo], [Bdb])
                        ph.op("dve", lambda e: e.reciprocal(out=rden[:], in_=db), [Bdb], [Brden])
                        o, Bo = ost.next()
                        ph.op("dve", lambda e: e.tensor_tensor(out=o[:], in0=ob, in1=rden[:], op=ALU.mult), [Bob, Brden], [Bo])
                        ph.dma("sp", MIX[h * 128:(h + 1) * 128, qs], o[:], Bo)
                        if jq == qtiles - 1 and h + 2 < heads:
                            load(h + 2)
                steps.append((A, Bf))
    run_pipeline(steps, 2)
    ph.flush()


def emit_oproj(nc, R, A, WO, tag, tiles=NT):
    ph = Phase(nc, tag)
    xb = [ph.sbuf(f"x{i}", [128, KC, TT], F32) for i in range(2)]
    ab = [ph.sbuf(f"a{i}", [128, KC, TT], BF16) for i in range(2)]
    Bx = [ph.buf(f"x{i}") for i in range(2)]
    Ba = [ph.buf(f"a{i}") for i in range(2)]
    psum = ph.psum("ps", [128, 8, TT], F32)
    banks = make_banks(ph, psum, list(range(8)))
    ws = WS(ph, "w", 4, KC * 128)
    Rv = R.rearrange("(kc p) s -> p kc s", p=128)
    Av = A.rearrange("(kc p) s -> p kc s", p=128)

    def load(t):
        ph.dma("sp", xb[t % 2][:], Rv[:, :, t * TT:(t + 1) * TT], Bx[t % 2])
        ph.dma("sp", ab[t % 2][:], Av[:, :, t * TT:(t + 1) * TT], Ba[t % 2])
    load(0)
    for t in range(tiles):
        x, a, BX, BA = xb[t % 2], ab[t % 2], Bx[t % 2], Ba[t % 2]
        if t + 1 < tiles:
            load(t + 1)
        for m in range(KC):
            w, Bw = ws.load(WO[m])
            bank, Bb = banks.next()
            ph.op("pe", mm_group(bank, [(w[:, kc * 128:(kc + 1) * 128], a[:, kc, :]) for kc in range(KC)]), [Bw, BA], [Bb])
            ph.op("dve", lambda e, x=x, m=m, bank=bank: e.tensor_tensor(out=x[:, m, :], in0=bank, in1=x[:, m, :], op=ALU.add),
                  [Bb], [BX])
        ph.dma("sp", Rv[:, :, t * TT:(t + 1) * TT], x[:], BX)
    ph.flush()


def emit_final(nc, R, GF, OUT, tiles=NT):
    ph = Phase(nc, "fin")
    xb = [ph.sbuf(f"x{i}", [128, KC, TT], F32) for i in range(2)]
    ob = [ph.sbuf(f"o{i}", [128, KC, TT], F32) for i in range(2)]
    sq = ph.sbuf("sq", [128, KC, TT], BF16)
    rstd = ph.sbuf("rstd", [128, TT], F32)
    ones = ph.sbuf("ones", [128, 128], BF16)
    g = ph.sbuf("g", [128, KC], F32)
    psum = ph.psum("ps", [128, 8, TT], F32)
    Bx = [ph.buf(f"x{i}") for i in range(2)]
    Bo = [ph.buf(f"o{i}") for i in range(2)]
    Bsq, Brstd, Bones, Bg = (ph.buf(n) for n in ("sq", "rstd", "ones", "g"))
    banks = make_banks(ph, psum, [0, 1])
    Rv = R.rearrange("(kc p) s -> p kc s", p=128)
    Ov = OUT.rearrange("(kc p) s -> p kc s", p=128)
    ph.dma("sp", g[:], GF, Bg)
    ph.op("dve", lambda e: e.memset(ones[:], 1.0), writes=[Bones])
    ph.dma("sp", xb[0][:], Rv[:, :, 0:TT], Bx[0])
    for t in range(tiles):
        if t + 1 < tiles:
            ph.dma("sp", xb[(t + 1) % 2][:], Rv[:, :, (t + 1) * TT:(t + 2) * TT], Bx[(t + 1) % 2])
        bank, Bb = banks.next()
        rms_norm(ph, xb[t % 2], Bx[t % 2], 0, KC, g, Bg, ob[t % 2], Bo[t % 2], 0, sq, Bsq, ones, Bones, bank, Bb, rstd, Brstd)
        ph.dma("sp", Ov[:, :, t * TT:(t + 1) * TT], ob[t % 2][:], Bo[t % 2])
    ph.flush()


def emit_inproj1(nc, R, GMIX, W1, Q12, K12, VT, tiles=NT):
    ph = Phase(nc, "ip1")
    x = ph.sbuf("x", [128, KC, TT], F32)
    sq = ph.sbuf("sq", [128, KC, TT], BF16)
    hn = ph.sbuf("hn", [128, KC, TT], BF16)
    rstd = ph.sbuf("rstd", [128, TT], F32)
    ones = ph.sbuf("ones", [128, 128], BF16)
    gmix = ph.sbuf("gmix", [128, KC], F32)
    psum = ph.psum("ps", [128, 8, TT], F32)
    Bx, Bsq, Bhn, Brstd, Bones, Bc = (ph.buf(n) for n in ("x", "sq", "hn", "rstd", "ones", "consts"))
    ws = WS(ph, "w", 6, KC * 128)
    banks = make_banks(ph, psum, list(range(8)))
    bfst = make_stage(ph, "bs", 6, [128, TT], BF16)
    Rv = R.rearrange("(kc p) s -> p kc s", p=128)
    ph.dma("sp", gmix[:], GMIX, Bc)
    ph.op("dve", lambda e: e.memset(ones[:], 1.0), writes=[Bones])
    ph.dma("sp", x[:], Rv[:, :, 0:TT], Bx)
    for t in range(tiles):
        ts = slice(t * TT, (t + 1) * TT)
        bank, Bb = banks.next()
        rms_norm(ph, x, Bx, 0, KC, gmix, Bc, hn, Bhn, 0, sq, Bsq, ones, Bones, bank, Bb, rstd, Brstd)
        if t + 1 < tiles:
            ph.dma("sp", x[:], Rv[:, :, (t + 1) * TT:(t + 2) * TT], Bx)
        for i in range(32):
            w, Bw = ws.load(W1[i])
            bank, Bb = banks.next()
            ph.op("pe", mm_group(bank, [(w[:, kc * 128:(kc + 1) * 128], hn[:, kc, :]) for kc in range(KC)]), [Bw, Bhn], [Bb])
            st, Bs = bfst.next()
            sc = 128.0 ** -0.5 if i < 16 else 1.0
            ph.op("act", lambda e, st=st, bank=bank, sc=sc: e.activation(out=st[:], in_=bank, func=AF.Copy, scale=sc), [Bb], [Bs])
            ph.dma("sp", (Q12 if i < 16 else K12)[i % 16][:, ts], st[:], Bs)
        for cg in range(4):
            wl = [ws.load(W1[32 + cg * 4 + c]) for c in range(4)]
            for tb in range(4):
                bank, Bb = banks.next()

                def fn(pe, wl=wl, tb=tb, bank=bank):
                    for c, (w, _) in enumerate(wl):
                        for kc in range(KC):
                            ins = pe.matmul(bank[:, c * 128:(c + 1) * 128], hn[:, kc, tb * 128:(tb + 1) * 128],
                                            w[:, kc * 128:(kc + 1) * 128], start=(kc == 0), stop=(kc == KC - 1))
                    return ins
                ph.op("pe", fn, [Bhn] + [b_ for (_, b_) in wl], [Bb])
                st, Bs = bfst.next()
                ph.op("act", lambda e, st=st, bank=bank: e.activation(out=st[:], in_=bank, func=AF.Copy), [Bb], [Bs])
                ph.dma("sp", VT[t * TT + tb * 128:t * TT + (tb + 1) * 128, cg * 512:(cg + 1) * 512], st[:], Bs)
    ph.flush()


def emit_diff(nc, Q12, K12, VT, POS, POSC, LAM, GSUB, OD, heads=8, qtiles=NT, DBG=None):
    ph = Phase(nc, "da")
    NB = 2
    pq = ph.sbuf("pq", [128, SEQ], F32)
    pk = ph.sbuf("pk", [128, 32], F32)
    pki = ph.sbuf("pki", [128, 32], I32)
    dall = ph.sbuf("dall", [128, 32, TT], F32)
    qq = [ph.sbuf(f"q{i}", [128, 2, TT], BF16) for i in range(NB)]
    kk = [ph.sbuf(f"k{i}", [128, 2, SEQ], BF16) for i in range(NB)]
    vv = [ph.sbuf(f"v{i}", [128, 32, 256], BF16) for i in range(NB)]
    ones = ph.sbuf("ones", [128, 128], BF16)
    lamv = ph.sbuf("lamv", [128, 512], F32)
    lamt = ph.sbuf("lamt", [128, 256], F32)
    lam = ph.sbuf("lam", [128, 4], F32)
    gsub = ph.sbuf("gsub", [128, 2], F32)
    r1 = ph.sbuf("r1", [128, 2, TT], F32)
    ot = ph.sbuf("ot", [128, 2, TT], F32)
    tmp = ph.sbuf("tmp", [128, TT], F32)
    sq = ph.sbuf("sq", [128, 2, TT], BF16)
    acc = [ph.sbuf(f"acc{i}", [128, 2, TT], F32) for i in range(2)]
    ahi = ph.sbuf("ahi", [128, 2, TT], BF16)
    alo = ph.sbuf("alo", [128, 2, TT], BF16)
    psum = ph.psum("ps", [128, 8, TT], F32)
    Bpq, Bpk, Bdall, Bones, Blam, Bc, Br1, Bot, Btmp, Bsq, Bahi, Balo = (ph.buf(n) for n in (
        "pq", "pk", "dall", "ones", "lam", "consts", "r1", "ot", "tmp", "sq", "ahi", "alo"))
    Bacc = [ph.buf("acc0"), ph.buf("acc1")]
    Bq = [ph.buf(f"q{i}") for i in range(NB)]
    Bk = [ph.buf(f"k{i}") for i in range(NB)]
    Bv = [ph.buf(f"v{i}") for i in range(NB)]
    spairs = Rot([(psum[:, 0:2, :], ph.buf("sp0")), (psum[:, 2:4, :], ph.buf("sp1"))])
    N_ = [[psum[:, 4, :], psum[:, 5, :]], [psum[:, 6, :], psum[:, 7, :]]]
    Bn = [[ph.buf(f"n{m}{c}") for c in range(2)] for m in range(2)]
    tst = make_stage(ph, "ts", 3, [128, 2, TT], BF16)
    est = make_stage(ph, "es", 3, [128, TT], BF16)
    pst = make_stage(ph, "pt", 4, [128, 2, TT], BF16)
    ost = make_stage(ph, "os", 2, [128, TT], BF16)
    VTv = VT.rearrange("(st p) c -> p st c", p=128)

    pqi = dall[:, 0:8, :].rearrange("p a b -> p (a b)").bitcast(I32)
    ph.dma("sp", pqi, POS.partition_broadcast(128), Bdall)
    ph.dma("sp", pki[:], POSC, Bpk)
    ph.dma("sp", lamv[:], LAM.partition_broadcast(128), Blam)
    ph.dma("sp", gsub[:], GSUB, Bc)
    ph.op("dve", lambda e: e.tensor_copy(out=pq[:], in_=pqi), [Bdall], [Bpq])
    ph.op("dve", lambda e: e.tensor_copy(out=pk[:], in_=pki[:]), [Bpk], [Bpk], tiny=True)
    ph.op("dve", lambda e: e.tensor_scalar(out=pk[:], in0=pk[:], scalar1=-1.0, scalar2=None, op0=ALU.mult), [Bpk], [Bpk], tiny=True)
    ph.op("dve", lambda e: e.memset(ones[:], 1.0), writes=[Bones])
    T = dict(tiny=True)
    ph.op("dve", lambda e: e.memset(lam[:], 0.0), writes=[Blam], **T)
    ph.op("dve", lambda e: e.tensor_tensor(out=lamt[:, 0:128], in0=lamv[:, 0:128], in1=lamv[:, 128:256], op=ALU.mult), [Blam], [Blam], **T)
    ph.op("dve", lambda e: e.tensor_tensor(out=lamt[:, 128:256], in0=lamv[:, 256:384], in1=lamv[:, 384:512], op=ALU.mult), [Blam], [Blam], **T)
    ph.op("dve", lambda e: e.reduce_sum(out=lam[:, 0:1], in_=lamt[:, 0:128], axis=mybir.AxisListType.X), [Blam], [Blam], **T)
    ph.op("dve", lambda e: e.reduce_sum(out=lam[:, 1:2], in_=lamt[:, 128:256], axis=mybir.AxisListType.X), [Blam], [Blam], **T)
    ph.op("act", lambda e: e.activation(out=lam[:, 0:2], in_=lam[:, 0:2], func=AF.Exp), [Blam], [Blam], **T)
    ph.op("act", lambda e: e.activation(out=lam[:, 3:4], in_=lam[:, 0:1], func=AF.Identity, scale=-1.0, bias=-LAM_INIT1), [Blam], [Blam], **T)
    ph.op("act", lambda e: e.activation(out=lam[:, 2:3], in_=lam[:, 1:2], func=AF.Identity, scale=1.0, bias=lam[:, 3:4]), [Blam], [Blam], **T)
    if DBG is not None:
        ph.dma("sp", DBG[0], lam[:], Blam)

    def load(jq, h, n):
        i = n % NB
        qs = slice(jq * TT, (jq + 1) * TT)
        for w in range(2):
            ph.dma("sp", qq[i][:, w, :], Q12[h * 2 + w][:, qs], Bq[i])
            ph.dma("sp", kk[i][:, w, :], K12[h * 2 + w], Bk[i])
        for a_ in range(4):
            ph.dma("sp", vv[i][:, a_ * 8:(a_ + 1) * 8, :], VTv[:, a_ * 8:(a_ + 1) * 8, h * 256:(h + 1) * 256], Bv[i])

    order = [(jq, h) for jq in range(qtiles) for h in range(heads)]
    load(order[0][0], order[0][1], 0)
    if len(order) > 1:
        load(order[1][0], order[1][1], 1)
    post = 1.0 - LAM_INIT1
    steps = []
    for n, (jq, h) in enumerate(order):
        for jk in range(32):
            st = {}

            def A(n=n, jq=jq, h=h, jk=jk, st=st):
                i = n % NB
                qs = slice(jq * TT, (jq + 1) * TT)
                ks = slice(jk * 128, (jk + 1) * 128)
                if h == 0 and jk == 0:
                    def dfn(e):
                        for j2 in range(32):
                            e.tensor_scalar(out=dall[:, j2, :], in0=pq[:, qs], scalar1=pk[:, j2:j2 + 1], scalar2=None, op0=ALU.add)
                            ins = e.scalar_tensor_tensor(out=dall[:, j2, :], in0=dall[:, j2, :], scalar=-1.0, in1=dall[:, j2, :],
                                                         op0=ALU.mult, op1=ALU.max)
                        return ins
                    ph.op("dve", dfn, [Bpq, Bpk], [Bdall])
                    if DBG is not None and jq == 0:
                        ph.dma("sp", DBG[1], dall[:, 0:2, :], Bdall)
                nslope = -(2.0 ** (-8.0 * (h + 1) / 8))
                ee, Be = est.next()
                ph.op("act", lambda e: e.activation(out=ee[:], in_=dall[:, jk, :], func=AF.Exp, scale=nslope), [Bdall], [Be])
                sp, Bsp = spairs.next()

                def sfn(pe):
                    pe.matmul(sp[:, 0, :], kk[i][:, 0, ks], qq[i][:, 0, :], start=True, stop=True)
                    return pe.matmul(sp[:, 1, :], kk[i][:, 1, ks], qq[i][:, 1, :], start=True, stop=True)
                ph.op("pe", sfn, [Bk[i], Bq[i]], [Bsp])
                tt, Bt = tst.next()
                ph.op("act", lambda e: e.activation(out=tt[:], in_=sp, func=AF.Exp), [Bsp], [Bt])
                pt, Bp = pst.next()

                def pfn(e):
                    e.tensor_tensor(out=pt[:, 0, :], in0=tt[:, 0, :], in1=ee[:], op=ALU.mult)
                    return e.tensor_tensor(out=pt[:, 1, :], in0=tt[:, 1, :], in1=ee[:], op=ALU.mult)
                ph.op("dve", pfn, [Bt, Be], [Bp])
                ac, Bac = acc[n % 2], Bacc[n % 2]
                if jk == 0:
                    ph.op("dve", lambda e: e.tensor_copy(out=ac[:], in_=pt[:]), [Bp], [Bac])
                else:
                    ph.op("dve", lambda e: e.tensor_tensor(out=ac[:], in0=ac[:], in1=pt[:], op=ALU.add), [Bp, Bac], [Bac])
                st["pt"] = (pt, Bp)

            def Bf(n=n, jq=jq, h=h, jk=jk, st=st):
                i = n % NB
                qs = slice(jq * TT, (jq + 1) * TT)
                pt, Bp = st["pt"]

                def fn(pe):
                    for m in range(2):
                        for c in range(2):
                            ins = pe.matmul(N_[m][c], vv[i][:, jk, c * 128:(c + 1) * 128], pt[:, m, :], start=(jk == 0), stop=(jk == 31))
                    return ins
                ph.op("pe", fn, [Bv[i], Bp], [Bn[0][0], Bn[0][1], Bn[1][0], Bn[1][1]])
                if jk != 31:
                    return
                ac, Bac = acc[n % 2], Bacc[n % 2]
                ph.op("dve", lambda e: e.tensor_copy(out=ahi[:], in_=ac[:]), [Bac], [Bahi])
                ph.op("dve", lambda e: e.tensor_tensor(out=alo[:], in0=ac[:], in1=ahi[:], op=ALU.subtract), [Bac, Bahi], [Balo])
                dp, Bdp = spairs.next()

                def dfn2(pe):
                    for m in range(2):
                        pe.matmul(dp[:, m, :], ones[:], ahi[:, m, :], start=True, stop=False)
                        ins = pe.matmul(dp[:, m, :], ones[:], alo[:, m, :], start=False, stop=True)
                    return ins
                ph.op("pe", dfn2, [Bones, Bahi, Balo], [Bdp])
                ph.op("dve", lambda e: e.reciprocal(out=r1[:], in_=dp), [Bdp], [Br1])
                ph.op("dve", lambda e: e.tensor_scalar(out=r1[:, 1, :], in0=r1[:, 1, :], scalar1=lam[:, 2:3], scalar2=None, op0=ALU.mult),
                      [Br1, Blam], [Br1])
                for c in range(2):
                    ph.op("dve", lambda e, c=c: e.tensor_tensor(out=ot[:, c, :], in0=N_[0][c], in1=r1[:, 0, :], op=ALU.mult),
                          [Bn[0][c], Br1], [Bot])
                    ph.op("dve", lambda e, c=c: e.tensor_tensor(out=tmp[:], in0=N_[1][c], in1=r1[:, 1, :], op=ALU.mult),
                          [Bn[1][c], Br1], [Btmp])
                    ph.op("dve", lambda e, c=c: e.tensor_tensor(out=ot[:, c, :], in0=ot[:, c, :], in1=tmp[:], op=ALU.add),
                          [Bot, Btmp], [Bot])
                ph.op("act", lambda e: e.activation(out=sq[:], in_=ot[:], func=AF.Square), [Bot], [Bsq])
                np_, Bnp = spairs.next()
                ph.op("pe", mm_group(np_[:, 0, :], [(ones[:], sq[:, c, :]) for c in range(2)]), [Bones, Bsq], [Bnp])
                ph.op("act", lambda e: e.activation(out=r1[:, 0, :], in_=np_[:, 0, :], func=AF.Sqrt, bias=EPS * post ** -2,
                                                    scale=1.0 / (256.0 * post * post)), [Bnp, Br1], [Br1])
                ph.op("dve", lambda e: e.reciprocal(out=r1[:, 0, :], in_=r1[:, 0, :]), [Br1], [Br1])
                for c in range(2):
                    o, Bo = ost.next()
                    ph.op("dve", lambda e, o=o, c=c: e.scalar_tensor_tensor(out=o[:], in0=ot[:, c, :], scalar=gsub[:, c:c + 1],
                                                                           in1=r1[:, 0, :], op0=ALU.mult, op1=ALU.mult), [Bot, Bc, Br1], [Bo])
                    ph.dma("sp", OD[h * 256 + c * 128:h * 256 + (c + 1) * 128, qs], o[:], Bo)
                if n + 2 < len(order):
                    load(order[n + 2][0], order[n + 2][1], n + 2)
            steps.append((A, Bf))
    run_pipeline(steps, 1)
    ph.flush()


def emit_mlstm_prep(nc, G, UD, NEGM, EMD):
    ph = Phase(nc, "mlp")
    S = SEQ
    li = ph.sbuf("li", [4, S], F32)
    lf = ph.sbuf("lf", [4, S], F32)
    fc = ph.sbuf("fc", [4, S], F32)
    u = ph.sbuf("u", [4, S], F32)
    m0 = ph.sbuf("m0", [4, S], F32)
    m1 = ph.sbuf("m1", [4, S], F32)
    one = ph.sbuf("one", [4, S], F32)
    tot = ph.sbuf("tot", [4, 1], F32)
    Btot = ph.buf("tot")
    Bli, Blf, Bfc, Bu, Bm0, Bm1, Bone = (ph.buf(n) for n in ("li", "lf", "fc", "u", "m0", "m1", "one"))
    ph.op("dve", lambda e: e.memset(one[:], 1.0), writes=[Bone])
    for d in range(2):
        ph.dma("sp", li[:], G[8 * d:8 * d + 4, :], Bli)
        ph.dma("sp", lf[:], G[8 * d + 4:8 * d + 8, :], Blf)
        ph.op("act", lambda e, lf=lf: e.activation(out=lf[:], in_=lf[:], func=AF.Exp, scale=-1.0), [Blf], [Blf])
        ph.op("act", lambda e, lf=lf: e.activation(out=lf[:], in_=lf[:], func=AF.Ln, bias=1.0, scale=1.0), [Blf], [Blf])
        ph.op("dve", lambda e, lf=lf: e.tensor_scalar(out=lf[:], in0=lf[:], scalar1=-1.0, scalar2=None, op0=ALU.mult), [Blf], [Blf])
        ph.op("dve", lambda e, fc=fc, one=one, lf=lf: e.tensor_tensor_scan(out=fc[:], data0=one[:], data1=lf[:], initial=0.0,
                                                                          op0=ALU.mult, op1=ALU.add), [Bone, Blf], [Bfc])
        if d == 1:
            ph.op("dve", lambda e: e.tensor_copy(out=tot[:], in_=fc[:, S - 1:S]), [Bfc], [Btot], tiny=True)
            ph.op("dve", lambda e: e.scalar_tensor_tensor(out=fc[:], in0=fc[:], scalar=-1.0, in1=lf[:], op0=ALU.mult,
                                                          op1=ALU.add), [Bfc, Blf], [Bfc])
            ph.op("dve", lambda e: e.tensor_scalar(out=fc[:], in0=fc[:], scalar1=tot[:, 0:1], scalar2=None, op0=ALU.add),
                  [Bfc, Btot], [Bfc])
        ph.op("dve", lambda e, u=u, li=li, fc=fc: e.tensor_tensor(out=u[:], in0=li[:], in1=fc[:], op=ALU.subtract), [Bli, Bfc], [Bu])
        if d == 0:
            ph.op("dve", lambda e, m0=m0, u=u: e.tensor_tensor_scan(out=m0[:], data0=u[:], data1=u[:], initial=-1e30,
                                                                    op0=ALU.max, op1=ALU.max), [Bu], [Bm0])
            mf, Bmf = m0, Bm0
        else:
            ph.op("dve", lambda e, m0=m0, u=u: e.tensor_copy(out=m0[:], in_=u[:]), [Bu], [Bm0])
            cur, Bcur, nxt, Bnxt = m0, Bm0, m1, Bm1
            k = 1
            while k < S:
                ph.op("dve", lambda e, cur=cur, nxt=nxt, k=k: e.tensor_tensor(out=nxt[:, 0:S - k], in0=cur[:, 0:S - k],
                                                                             in1=cur[:, k:S], op=ALU.max), [Bcur], [Bnxt])
                ph.op("dve", lambda e, cur=cur, nxt=nxt, k=k: e.tensor_copy(out=nxt[:, S - k:S], in_=cur[:, S - k:S]), [Bcur], [Bnxt])
                cur, Bcur, nxt, Bnxt = nxt, Bnxt, cur, Bcur
                k *= 2
            mf, Bmf = cur, Bcur
        ph.dma("sp", UD[4 * d:4 * d + 4, :], u[:], Bu)
        ph.op("dve", lambda e, fc=fc, mf=mf: e.tensor_tensor(out=fc[:], in0=fc[:], in1=mf[:], op=ALU.add), [Bfc, Bmf], [Bfc])
        ph.op("act", lambda e, fc=fc: e.activation(out=fc[:], in_=fc[:], func=AF.Exp, scale=-1.0), [Bfc], [Bfc])
        ph.dma("sp", EMD[4 * d:4 * d + 4, :], fc[:], Bfc)
        ph.op("dve", lambda e, lf=lf, mf=mf: e.tensor_scalar(out=lf[:], in0=mf[:], scalar1=-1.0, scalar2=None, op0=ALU.mult),
              [Bmf, Blf], [Blf])
        ph.dma("sp", NEGM[4 * d:4 * d + 4, :], lf[:], Blf)
    ph.flush()


def emit_mlstm(nc, MQ, MK, MVT, MO, UD, NEGM, EMD, GML, MIX, heads=4):
    ph = Phase(nc, "ml")
    S = SEQ
    NB = 2
    mq = [ph.sbuf(f"mq{i}", [128, S], BF16) for i in range(NB)]
    mk = [ph.sbuf(f"mk{i}", [128, S], BF16) for i in range(NB)]
    mv = [ph.sbuf(f"mv{i}", [128, 32, 256], BF16) for i in range(NB)]
    negm = [ph.sbuf(f"negm{i}", [128, S], F32) for i in range(NB)]
    em = [ph.sbuf(f"em{i}", [128, S], F32) for i in range(NB)]
    ucol = [ph.sbuf(f"ucol{i}", [128, 32], F32) for i in range(NB)]
    hm = ph.sbuf("hm", [128, 2, S], F32)
    ones = ph.sbuf("ones", [128, 128], BF16)
    gml = ph.sbuf("gml", [128, 8], F32)
    r = ph.sbuf("r", [128, TT], F32)
    tmp = ph.sbuf("tmp", [128, TT], F32)
    sq = ph.sbuf("sq", [128, 2, TT], BF16)
    psum = ph.psum("ps", [128, 8, TT], F32)
    Bmq, Bmk, Bmv, Bnegm, Bem, Bucol = ([ph.buf(f"{n}{i}") for i in range(NB)] for n in ("mq", "mk", "mv", "negm", "em", "ucol"))
    Bhm, Bones, Bc, Br, Btmp, Bsq = (ph.buf(n) for n in ("hm", "ones", "consts", "r", "tmp", "sq"))
    sbanks = make_banks(ph, psum, [0, 1, 2, 3])
    NUM = [psum[:, 4, :], psum[:, 5, :]]
    DEN = psum[:, 6, :]
    STB = psum[:, 7, :]
    Bnum = [ph.buf("num0"), ph.buf("num1")]
    Bden, Bstb = ph.buf("den"), ph.buf("stb")
    wst = make_stage(ph, "wt", 3, [128, TT], BF16)
    sst = make_stage(ph, "sc", 3, [128, TT], BF16)
    ast = make_stage(ph, "at", 4, [128, TT], BF16)
    mst = make_stage(ph, "mo", 2, [128, TT], BF16)
    ost = make_stage(ph, "os", 2, [128, TT], BF16)
    MVv = MVT.rearrange("(st p) c -> p st c", p=128)
    ph.dma("sp", gml[:], GML, Bc)
    ph.op("dve", lambda e: e.memset(ones[:], 1.0), writes=[Bones])

    def load_head(h):
        i = h % NB
        ph.dma("sp", mq[i][:], MQ[h], Bmq[i])
        ph.dma("sp", mk[i][:], MK[h], Bmk[i])
        for a_ in range(4):
            ph.dma("sp", mv[i][:, a_ * 8:(a_ + 1) * 8, :], MVv[:, a_ * 8:(a_ + 1) * 8, h * 256:(h + 1) * 256], Bmv[i])

    def load_dir(g):
        h, d = g // 2, g % 2
        i = g % NB
        row = 4 * d + h
        ph.dma("sp", negm[i][:], NEGM[row:row + 1, :].partition_broadcast(128), Bnegm[i])
        ph.dma("sp", em[i][:], EMD[row:row + 1, :].partition_broadcast(128), Bem[i])
        for a_ in range(4):
            ph.dma("sp", ucol[i][:, a_ * 8:(a_ + 1) * 8], UD[row, a_ * 1024:(a_ + 1) * 1024].rearrange("(st p) -> p st", p=128),
                   Bucol[i])

    def head_norm(h):
        for jt in range(NT):
            ts = slice(jt * TT, (jt + 1) * TT)
            ph.op("act", lambda e, ts=ts: e.activation(out=sq[:], in_=hm[:, :, ts], func=AF.Square), [Bhm], [Bsq])
            ph.op("pe", mm_group(STB, [(ones[:], sq[:, c, :]) for c in range(2)]), [Bones, Bsq], [Bstb])
            ph.op("act", lambda e: e.activation(out=r[:], in_=STB, func=AF.Sqrt, bias=EPS, scale=1.0 / 256.0), [Bstb], [Br])
            ph.op("dve", lambda e: e.reciprocal(out=r[:], in_=r[:]), [Br], [Br])
            for c in range(2):
                mo, Bmo = mst.next()
                ph.dma("sp", mo[:], MO[h * 256 + c * 128:h * 256 + (c + 1) * 128, ts], Bmo)
                ph.op("dve", lambda e, c=c, ts=ts: e.scalar_tensor_tensor(out=tmp[:], in0=hm[:, c, ts],
                                                                          scalar=gml[:, h * 2 + c:h * 2 + c + 1], in1=r[:],
                                                                          op0=ALU.mult, op1=ALU.mult), [Bhm, Bc, Br], [Btmp])
                o, Bo = ost.next()
                ph.op("dve", lambda e, o=o, mo=mo: e.tensor_tensor(out=o[:], in0=tmp[:], in1=mo[:], op=ALU.mult), [Btmp, Bmo], [Bo])
                ph.dma("sp", MIX[1024 + h * 256 + c * 128:1024 + h * 256 + (c + 1) * 128, ts], o[:], Bo)

    load_head(0)
    load_dir(0)
    load_dir(1)
    if heads > 1:
        load_head(1)
    ngroups = heads * 2
    steps = []
    for g in range(ngroups):
        h, d = g // 2, g % 2
        for jt in range(NT):
            jss = list(range(0, 4 * jt + 4)) if d == 0 else list(range(4 * jt, 32))
            for n, js in enumerate(jss):
                st = {}
                first, last = (n == 0), (n == len(jss) - 1)

                def A(g=g, h=h, d=d, jt=jt, js=js, st=st):
                    hi, gi = h % NB, g % NB
                    ts = slice(jt * TT, (jt + 1) * TT)
                    ks = slice(js * 128, (js + 1) * 128)
                    sb, Bsb = sbanks.next()
                    ph.op("pe", lambda e: e.matmul(sb, mk[hi][:, ks], mq[hi][:, ts], start=True, stop=True), [Bmk[hi], Bmq[hi]], [Bsb])
                    wt, Bwt = wst.next()
                    ph.op("act", lambda e: e.activation(out=wt[:], in_=negm[gi][:, ts], func=AF.Exp, bias=ucol[gi][:, js:js + 1],
                                                        scale=1.0), [Bnegm[gi], Bucol[gi]], [Bwt])
                    at, Bat = ast.next()
                    if js % 2 == 0:
                        sc, Bsc = sst.next()
                        ph.op("act", lambda e: e.activation(out=sc[:], in_=sb, func=AF.Copy), [Bsb], [Bsc])
                        ph.op("dve", lambda e: e.tensor_tensor(out=at[:], in0=sc[:], in1=wt[:], op=ALU.mult), [Bsc, Bwt], [Bat])
                    else:
                        ph.op("dve", lambda e: e.tensor_tensor(out=at[:], in0=sb, in1=wt[:], op=ALU.mult), [Bsb, Bwt], [Bat])
                    if 4 * jt <= js <= 4 * jt + 3:
                        if d == 0:
                            pat, base, cm = [[1, TT]], jt * TT - js * 128, -1
                        else:
                            pat, base, cm = [[-1, TT]], js * 128 - jt * TT, 1
                        ph.op("pool", lambda e: e.affine_select(out=at[:], in_=at[:], pattern=pat, compare_op=ALU.is_ge, fill=0.0,
                                                                base=base, channel_multiplier=cm), [Bat], [Bat])
                    st["at"] = (at, Bat)

                def Bf(g=g, h=h, d=d, jt=jt, js=js, st=st, first=first, last=last):
                    hi, gi = h % NB, g % NB
                    ts = slice(jt * TT, (jt + 1) * TT)
                    at, Bat = st["at"]

                    def fn(pe):
                        for c in range(2):
                            pe.matmul(NUM[c], mv[hi][:, js, c * 128:(c + 1) * 128], at[:], start=first, stop=last)
                        return pe.matmul(DEN, ones[:], at[:], start=first, stop=last)
                    ph.op("pe", fn, [Bmv[hi], Bat, Bones], [Bnum[0], Bnum[1], Bden])
                    if not last:
                        return
                    ph.op("act", lambda e: e.activation(out=r[:], in_=DEN, func=AF.Abs), [Bden], [Br])
                    ph.op("dve", lambda e: e.tensor_tensor(out=r[:], in0=r[:], in1=em[gi][:, ts], op=ALU.max), [Br, Bem[gi]], [Br])
                    ph.op("dve", lambda e: e.reciprocal(out=r[:], in_=r[:]), [Br], [Br])
                    for c in range(2):
                        if d == 0:
                            ph.op("dve", lambda e, c=c: e.tensor_tensor(out=hm[:, c, ts], in0=NUM[c], in1=r[:], op=ALU.mult),
                                  [Bnum[c], Br], [Bhm])
                        else:
                            ph.op("dve", lambda e, c=c: e.tensor_tensor(out=tmp[:], in0=NUM[c], in1=r[:], op=ALU.mult),
                                  [Bnum[c], Br], [Btmp])
                            ph.op("dve", lambda e, c=c: e.tensor_tensor(out=hm[:, c, ts], in0=hm[:, c, ts], in1=tmp[:], op=ALU.add),
                                  [Bhm, Btmp], [Bhm])
                    if jt == NT - 1:
                        if g + 2 < ngroups:
                            load_dir(g + 2)
                        if d == 1:
                            head_norm(h)
                            if h + 2 < heads:
                                load_head(h + 2)
                steps.append((A, Bf))
    per_head = {}
    idx = 0
    for g in range(ngroups):
        cnt = sum((4 * jt + 4) if g % 2 == 0 else (32 - 4 * jt) for jt in range(NT))
        per_head.setdefault(g // 2, []).extend(steps[idx:idx + cnt])
        idx += cnt
    for h in range(heads):
        run_pipeline(per_head[h], 2)
    ph.flush()


def tile_cols(w, cols_list):
    w = np.asarray(w, dtype=np.float32)
    K = w.shape[0]
    kc = K // 128
    out = np.zeros((len(cols_list), 128, kc, 128), dtype=np.float32)
    for i, cols in enumerate(cols_list):
        cols = np.asarray(cols)
        out[i, :, :, :len(cols)] = w[:, cols].reshape(kc, 128, len(cols)).transpose(1, 0, 2)
    return out.reshape(len(cols_list), 128, kc * 128)


def tile_wgu(w):
    w = np.asarray(w, dtype=np.float32)
    n = w.shape[1] // 128
    kc = w.shape[0] // 128
    return np.ascontiguousarray(w.reshape(kc, 128, n, 128).transpose(2, 1, 0, 3).reshape(n, 128, kc * 128))


def tile_g(g):
    g = np.asarray(g, dtype=np.float32)
    return np.ascontiguousarray(g.reshape(-1, 128).T)


ROPE_THETA = 10000.0
LAM_INIT1 = 0.8 - 0.6 * float(np.exp(-0.3))


def prep_weights(inp):
    W = {}
    for l in (0, 1):
        for f in (1, 2):
            p = f"l{l}_ffn{f}"
            W[f"{p}_wgu"] = tile_wgu(inp[f"{p}_w_gu"])
            W[f"{p}_wdn"] = tile_wgu(inp[f"{p}_w_down"])
            W[f"{p}_g"] = tile_g(inp[f"{p}_norm"])
    r = np.arange
    cols = [r(i * 128, (i + 1) * 128) for i in range(8)]
    cols.append(np.concatenate([1024 + r(64), 1024 + 32 + r(32), 1024 + r(32)]))
    cols += [1088 + r(i * 128, (i + 1) * 128) for i in range(4)]
    cols += [1600 + r(i * 128, (i + 1) * 128) for i in range(4)]
    cols += [2112 + r(i * 128, (i + 1) * 128) for i in range(8)]
    cols += [3136 + r(i * 128, (i + 1) * 128) for i in range(8)]
    cols.append(4160 + r(16))
    W["l0_win"] = tile_cols(inp["l0_w_in"], cols)
    cq, ck = [], []
    for h in range(8):
        cq.append(h * 192 + r(128))
        cq.append(np.concatenate([h * 192 + 128 + r(64), h * 192 + 128 + 32 + r(32), h * 192 + 128 + r(32)]))
        ck.append(h * 256 + r(128))
        ck.append(h * 256 + 128 + r(128))
    W["l0_wuq"] = tile_cols(inp["l0_w_uq"], cq)
    W["l0_wukv"] = tile_cols(inp["l0_w_ukv"], ck)
    W["l0_gmix"] = tile_g(inp["l0_mix_norm"])
    W["l0_gcq"] = tile_g(inp["l0_g_cq"])
    W["l0_gckv"] = tile_g(inp["l0_g_ckv"])
    W["l0_bg"] = np.ascontiguousarray(np.asarray(inp["l0_b_gates"], np.float32).reshape(16, 1))
    W["l0_gml"] = tile_g(inp["l0_g_mlstm"])
    W["l0_wo"] = tile_wgu(inp["l0_w_o"])
    c1 = []
    for part in range(2):
        for h in range(8):
            for w in range(2):
                c1.append(part * 2048 + h * 256 + w * 128 + r(128))
    c1 += [4096 + r(i * 128, (i + 1) * 128) for i in range(16)]
    W["l1_win"] = tile_cols(inp["l1_w_in"], c1)
    W["l1_gmix"] = tile_g(inp["l1_mix_norm"])
    W["l1_wo"] = tile_wgu(inp["l1_w_o"])
    W["l1_gsub"] = tile_g(inp["l1_g_sub"])
    W["l1_lam"] = np.ascontiguousarray(np.concatenate([np.asarray(inp[k], np.float32) for k in
                                                       ("l1_lam_q1", "l1_lam_k1", "l1_lam_q2", "l1_lam_k2")]).reshape(1, 512))
    W["gfin"] = tile_g(inp["final_norm"])
    invf = (ROPE_THETA ** (-np.arange(0, 64, 2, dtype=np.float32) / 64)).astype(np.float32)
    W["invf"] = np.ascontiguousarray(np.concatenate([invf, invf]).reshape(64, 1))
    return W


def emit_copy(nc, src, dst, tag):
    ph = Phase(nc, tag)
    t = ph.sbuf("t", [128, KC, TT], F32)
    Bt = ph.buf("t")
    sv = src.rearrange("(kc p) s -> p kc s", p=128)
    dv = dst.rearrange("(kc p) s -> p kc s", p=128)
    for i in range(NT):
        ph.dma("sp", t[:], sv[:, :, i * TT:(i + 1) * TT], Bt)
        ph.dma("sp", dv[:, :, i * TT:(i + 1) * TT], t[:], Bt)
    ph.flush()


SCRATCH_EXT = True


def build_program(W, debug=False, upto=99, start=0, dheads=8, dtiles=NT):
    nc = bass.Bass("TRN2", target_bir_lowering=False)
    A = {}

    def din(name, shape, dt=F32):
        A[name] = nc.dram_tensor(name, list(shape), dt, kind="ExternalInput").ap()
        return A[name]

    def scratch(name, shape, dt, dbg=False):
        kind = "ExternalOutput" if (dbg and (debug or SCRATCH_EXT)) else "Internal"
        return nc.dram_tensor(name, list(shape), dt, kind=kind).ap()

    R = din("xT", [D_MODEL, SEQ])
    POS = din("pos", [1, SEQ], I32)
    POSC = din("posc", [128, 32], I32)
    for k, v in W.items():
        din(k, v.shape)
    OUT = nc.dram_tensor("out", [D_MODEL, SEQ], F32, kind="ExternalOutput").ap()
    COS = scratch("COS", [64, SEQ], F32)
    SIN = scratch("SIN", [64, SEQ], F32)
    QN = scratch("QN", [8, 128, SEQ], BF16)
    QR = scratch("QR", [8, 64, SEQ], BF16)
    KN = scratch("KN", [8, 128, SEQ], BF16)
    KR = scratch("KR", [64, SEQ], BF16)
    VT = scratch("VT", [SEQ, 1024], BF16)
    MQ = scratch("MQ", [4, 128, SEQ], BF16)
    MK = scratch("MK", [4, 128, SEQ], BF16)
    MVT = scratch("MVT", [SEQ, 1024], BF16)
    MO = scratch("MO", [1024, SEQ], BF16)
    G = scratch("G", [16, SEQ], F32)
    UD = scratch("UD", [8, SEQ], F32)
    NEGM = scratch("NEGM", [8, SEQ], F32)
    EMD = scratch("EMD", [8, SEQ], F32)
    MIX = scratch("MIX", [2048, SEQ], BF16, True)
    Q12 = scratch("Q12", [16, 128, SEQ], BF16, True)
    K12 = scratch("K12", [16, 128, SEQ], BF16, True)
    VT1 = scratch("VT1", [SEQ, 2048], BF16, True)
    OD = scratch("OD", [2048, SEQ], BF16, True)
    steps = [
        lambda: emit_tables(nc, POS, A["invf"], COS, SIN),
        lambda: emit_ffn(nc, R, A["l0_ffn1_wgu"], A["l0_ffn1_wdn"], A["l0_ffn1_g"], "f01"),
        lambda: emit_inproj0(nc, R, A["l0_gmix"], A["l0_win"], A["l0_gcq"], A["l0_gckv"], A["l0_wuq"], A["l0_wukv"],
                             A["l0_bg"], COS, SIN, QN, QR, KN, KR, VT, MQ, MK, MVT, MO, G),
        lambda: emit_mla(nc, QN, QR, KN, KR, VT, MIX),
        lambda: emit_mlstm_prep(nc, G, UD, NEGM, EMD),
        lambda: emit_mlstm(nc, MQ, MK, MVT, MO, UD, NEGM, EMD, A["l0_gml"], MIX),
        lambda: emit_oproj(nc, R, MIX, A["l0_wo"], "op0"),
        lambda: emit_copy(nc, R, scratch("X2", [D_MODEL, SEQ], F32, True), "cx2") if debug else None,
        lambda: emit_ffn(nc, R, A["l0_ffn2_wgu"], A["l0_ffn2_wdn"], A["l0_ffn2_g"], "f02"),
        lambda: emit_ffn(nc, R, A["l1_ffn1_wgu"], A["l1_ffn1_wdn"], A["l1_ffn1_g"], "f11"),
        lambda: emit_copy(nc, R, scratch("X4", [D_MODEL, SEQ], F32, True), "cx4") if debug else None,
        lambda: emit_inproj1(nc, R, A["l1_gmix"], A["l1_win"], Q12, K12, VT1),
        lambda: emit_diff(nc, Q12, K12, VT1, POS, POSC, A["l1_lam"], A["l1_gsub"], OD, heads=dheads, qtiles=dtiles,
                          DBG=(scratch("DLAM", [128, 4], F32, True), scratch("DDALL", [128, 2, TT], F32, True)) if debug else None),
        lambda: emit_oproj(nc, R, OD, A["l1_wo"], "op1"),
        lambda: emit_ffn(nc, R, A["l1_ffn2_wgu"], A["l1_ffn2_wdn"], A["l1_ffn2_g"], "f12"),
        lambda: emit_final(nc, R, A["gfin"], OUT),
    ]
    for i, st in enumerate(steps):
        if start <= i <= upto:
            st()
    return nc


def core_inputs(inp, W, b):
    pos = np.ascontiguousarray(np.asarray(inp["positions"][b], np.int32).reshape(1, SEQ))
    m = {"xT": np.ascontiguousarray(np.asarray(inp["x"][b], np.float32).T), "pos": pos,
         "posc": np.ascontiguousarray(pos.reshape(32, 128).T)}
    m.update(W)
    return m


def kernel(**inputs):
    W = prep_weights(inputs)
    nc = build_program(W)
    in_maps = [core_inputs(inputs, W, b) for b in range(NCORES)]
    res = run_bass_kernel_spmd(nc, in_maps, core_ids=list(range(NCORES)))
    out = np.stack([np.asarray(res.results[b]["out"], np.float32).T for b in range(NCORES)], axis=0)
    return np.ascontiguousarray(out)
```
